# Optimizing a Trainium2 kernel written in Bass

```python
import math
import jax, jax.numpy as jnp
from jax import lax
import numpy as np

D_MODEL = 4096
BATCH = 4
SEQ = 4096
DEPTH = 2

DN_ALPHA = (2 * DEPTH) ** 0.25
DN_BETA = (8 * DEPTH) ** -0.25
LN_EPS = 1e-5
RMS_EPS = 1e-6
NEG_INF = -1e30
FORCE_SCORE = 1e9

MIX_WIDTH = D_MODEL

A_HEADS = 4
A_DQK = 256
A_DV = 512
A_QK = A_HEADS * A_DQK
A_V = A_HEADS * A_DV
A_CHUNK = 64
A_GATE_CAP = 15.0

B_HEADS = 16
B_Q_LORA = 1024
B_KV_LORA = 512
B_NOPE = 128
B_ROPE = 64
B_DV = 128
ROPE_THETA = 10000.0
Q_BLOCK = 128

C_HEADS = 32
C_KV_GROUPS = 4
C_HD = 128
C_KV = C_KV_GROUPS * C_HD
C_CMP_LEN = 32
C_CMP_STRIDE = 16
C_SEL_LEN = 64
C_N_SEL = 16
C_WINDOW = 512
C_CMP_HIDDEN = 256
C_Q_BLOCK = 64

FFN_HIDDEN = -(-(8 * D_MODEL) // (3 * 256)) * 256

AB_SPLITS = [int(v) for v in np.cumsum([A_QK, A_QK, A_V, A_HEADS, A_HEADS, A_V, B_Q_LORA, B_KV_LORA])]
AB_IN = AB_SPLITS[-1] + B_ROPE
C_SPLITS = [int(v) for v in np.cumsum([C_HEADS * C_HD] + [C_KV] * 6)]
C_IN = C_SPLITS[-1] + 3 * C_HEADS

kernel_name = "hybrid_mlstm_mla_nsa_deepnorm"


def layer_norm(x, g, b):
    xf = x.astype(jnp.float32)
    mu = jnp.mean(xf, -1, keepdims=True)
    var = jnp.mean(jnp.square(xf - mu), -1, keepdims=True)
    return ((xf - mu) * lax.rsqrt(var + LN_EPS) * g + b).astype(x.dtype)


def rms_norm(x, g):
    xf = x.astype(jnp.float32)
    return (xf * lax.rsqrt(jnp.mean(xf * xf, -1, keepdims=True) + RMS_EPS) * g).astype(x.dtype)


def rope(x, pos):
    half = x.shape[-1] // 2
    inv = ROPE_THETA ** (-jnp.arange(half, dtype=jnp.float32) / half)
    ang = pos.astype(jnp.float32)[:, None] * inv[None, :]
    cos, sin = jnp.cos(ang)[:, None, :], jnp.sin(ang)[:, None, :]
    xf = x.astype(jnp.float32)
    x1, x2 = xf[..., :half], xf[..., half:]
    return jnp.concatenate([x1 * cos - x2 * sin, x1 * sin + x2 * cos], -1).astype(x.dtype)


def masked_softmax(s, mask, axis=-1):
    s = jnp.where(mask, s.astype(jnp.float32), NEG_INF)
    return jax.nn.softmax(s, axis=axis) * mask


def soft_cap(z):
    return A_GATE_CAP * jnp.tanh(z / A_GATE_CAP)


def mlstm_chunkwise(q, k, v, logi, logf):
    B, S, H, DQK = q.shape
    DV = v.shape[-1]
    nc = S // A_CHUNK

    def to_chunks(t):
        t = t.reshape((B, nc, A_CHUNK, H) + t.shape[3:])
        return jnp.moveaxis(t, (1, 3), (0, 2))

    causal = jnp.tril(jnp.ones((A_CHUNK, A_CHUNK), bool))

    def step(carry, inp):
        C, n, m = carry
        qc, kc, vc, li, lf = inp
        b = jnp.cumsum(lf, axis=-1)
        dmat = jnp.where(causal, b[..., :, None] - b[..., None, :] + li[..., None, :], -jnp.inf)
        m_inter = b + m[..., None]
        m_t = jnp.maximum(m_inter, jnp.max(dmat, -1))
        w_inter = jnp.exp(m_inter - m_t)
        s = jnp.einsum('bhtd,bhsd->bhts', qc, kc) * jnp.exp(dmat - m_t[..., None])
        num = w_inter[..., None] * jnp.einsum('bhtd,bhde->bhte', qc, C) + jnp.einsum('bhts,bhse->bhte', s, vc)
        den = w_inter * jnp.einsum('bhtd,bhd->bht', qc, n) + jnp.sum(s, -1)
        h = num / jnp.maximum(jnp.abs(den), jnp.exp(-m_t))[..., None]
        g = b[..., -1]
        w_s = g[..., None] - b + li
        m_new = jnp.maximum(g + m, jnp.max(w_s, -1))
        decay = jnp.exp(g + m - m_new)
        kw = kc * jnp.exp(w_s - m_new[..., None])[..., None]
        C_new = decay[..., None, None] * C + jnp.einsum('bhsd,bhse->bhde', kw, vc)
        n_new = decay[..., None] * n + jnp.sum(kw, axis=2)
        return (C_new, n_new, m_new), h

    f32 = jnp.float32
    init = (jnp.zeros((B, H, DQK, DV), f32), jnp.zeros((B, H, DQK), f32), jnp.zeros((B, H), f32))
    _, hs = lax.scan(step, init, (to_chunks(q), to_chunks(k), to_chunks(v), to_chunks(logi), to_chunks(logf)))
    return jnp.moveaxis(hs, (0, 2), (1, 3)).reshape(B, S, H, DV)


def mla_attention(q_nope, q_rope, k_nope, k_rope, v):
    B, S, H, _ = q_nope.shape
    scale = (B_NOPE + B_ROPE) ** -0.5
    kpos = jnp.arange(S)

    def block(i):
        t0 = i * Q_BLOCK
        qn = lax.dynamic_slice_in_dim(q_nope, t0, Q_BLOCK, axis=1)
        qr = lax.dynamic_slice_in_dim(q_rope, t0, Q_BLOCK, axis=1)
        s = (jnp.einsum('bqhd,bkhd->bhqk', qn, k_nope) + jnp.einsum('bqhd,bkd->bhqk', qr, k_rope)) * scale
        mask = (t0 + jnp.arange(Q_BLOCK))[:, None] >= kpos[None, :]
        p = masked_softmax(s, mask)
        return jnp.einsum('bhqk,bkhd->bqhd', p.astype(v.dtype), v)

    out = lax.map(block, jnp.arange(S // Q_BLOCK))
    return jnp.moveaxis(out, 0, 1).reshape(B, S, H, v.shape[-1])


def mixer_ab(x, w_in, b_igate, b_fgate, mlstm_norm, q_norm, kv_norm, w_uq, w_ukv, w_o):
    B, S, _ = x.shape
    f32 = jnp.float32
    proj = x @ w_in
    q_a, k_a, v_a, ig, fg, og, cq, ckv, kr = jnp.split(proj, AB_SPLITS, axis=-1)
    qa = q_a.reshape(B, S, A_HEADS, A_DQK).astype(f32)
    ka = k_a.reshape(B, S, A_HEADS, A_DQK).astype(f32) * (A_DQK ** -0.5)
    va = v_a.reshape(B, S, A_HEADS, A_DV).astype(f32)
    logi = soft_cap(ig.astype(f32) + b_igate)
    logf = jax.nn.log_sigmoid(soft_cap(fg.astype(f32) + b_fgate))
    h_a = mlstm_chunkwise(qa, ka, va, logi, logf)
    h_a = rms_norm(h_a, mlstm_norm.reshape(A_HEADS, A_DV)).reshape(B, S, A_V)
    h_a = h_a * jax.nn.sigmoid(og.astype(f32))
    pos = jnp.arange(S)
    qb = (rms_norm(cq, q_norm) @ w_uq).reshape(B, S, B_HEADS, B_NOPE + B_ROPE)
    q_nope, q_rope = qb[..., :B_NOPE], rope(qb[..., B_NOPE:], pos)
    kvb = (rms_norm(ckv, kv_norm) @ w_ukv).reshape(B, S, B_HEADS, B_NOPE + B_DV)
    k_nope, v_b = kvb[..., :B_NOPE], kvb[..., B_NOPE:]
    k_rope = rope(kr.reshape(B, S, 1, B_ROPE), pos)[:, :, 0]
    h_b = mla_attention(q_nope, q_rope, k_nope, k_rope, v_b).reshape(B, S, B_HEADS * B_DV)
    h = jnp.concatenate([h_a.astype(x.dtype), h_b.astype(x.dtype)], -1)
    return h @ w_o


def mixer_c(x, w_in, b_gate, pe_k, pe_v, cmp_w1_k, cmp_w2_k, cmp_w1_v, cmp_w2_v, w_o):
    B, S, _ = x.shape
    G, HG, HD = C_KV_GROUPS, C_HEADS // C_KV_GROUPS, C_HD
    f32 = jnp.float32
    scale = HD ** -0.5
    proj = x @ w_in
    q, kc, vc, ks, vs, kw, vw, gates = jnp.split(proj, C_SPLITS, axis=-1)
    q = q.reshape(B, S, G, HG, HD)
    kc, vc, ks, vs, kw, vw = [t.reshape(B, S, G, HD) for t in (kc, vc, ks, vs, kw, vw)]
    gates = jax.nn.sigmoid(gates.astype(f32) + b_gate).reshape(B, S, 3, G, HG)

    n_cmp = (S - C_CMP_LEN) // C_CMP_STRIDE + 1
    cmp_start = jnp.arange(n_cmp) * C_CMP_STRIDE
    blk_idx = cmp_start[:, None] + jnp.arange(C_CMP_LEN)[None, :]
    cmp_end = cmp_start + C_CMP_LEN - 1

    def compress(t, pe, w1, w2):
        blocks = t[:, blk_idx] + pe[:, None, :]
        blocks = jnp.moveaxis(blocks, 3, 2).reshape(B, n_cmp, G, C_CMP_LEN * HD)
        return jax.nn.gelu(blocks @ w1) @ w2

    k_cmp = compress(kc, pe_k, cmp_w1_k, cmp_w2_k)
    v_cmp = compress(vc, pe_v, cmp_w1_v, cmp_w2_v)

    n_sel = S // C_SEL_LEN
    n_top = min(C_N_SEL, n_sel)
    sel_start = jnp.arange(n_sel) * C_SEL_LEN
    overlap = ((cmp_start[:, None] < sel_start[None, :] + C_SEL_LEN)
               & (cmp_start[:, None] + C_CMP_LEN > sel_start[None, :])).astype(f32)
    ks_blocks = jnp.moveaxis(ks.reshape(B, n_sel, C_SEL_LEN, G, HD), 3, 1)
    vs_blocks = jnp.moveaxis(vs.reshape(B, n_sel, C_SEL_LEN, G, HD), 3, 1)
    bi = jnp.arange(B)[:, None, None, None]
    gi = jnp.arange(G)[None, :, None, None]
    sblk = jnp.arange(n_sel)

    pad = ((0, 0), (C_WINDOW, 0), (0, 0), (0, 0))
    kw_pad, vw_pad = jnp.pad(kw, pad), jnp.pad(vw, pad)
    band = C_WINDOW + C_Q_BLOCK

    def block(i):
        t0 = i * C_Q_BLOCK
        tq = t0 + jnp.arange(C_Q_BLOCK)
        qb = lax.dynamic_slice_in_dim(q, t0, C_Q_BLOCK, 1).astype(f32)
        gb = lax.dynamic_slice_in_dim(gates, t0, C_Q_BLOCK, 1)
        s_c = jnp.einsum('bqghd,bngd->bghqn', qb, k_cmp) * scale
        p_c = masked_softmax(s_c, cmp_end[None, :] <= tq[:, None])
        o_c = jnp.einsum('bghqn,bngd->bqghd', p_c, v_cmp)
        imp = jnp.einsum('bghqn,ns->bgqs', p_c, overlap)
        cur = tq // C_SEL_LEN
        forced = (sblk[None, :] == 0) | (sblk[None, :] == cur[:, None]) | (sblk[None, :] == cur[:, None] - 1)
        imp = jnp.where(forced, FORCE_SCORE, imp)
        imp = jnp.where(sblk[None, :] <= cur[:, None], imp, NEG_INF)
        _, sel = lax.top_k(imp, n_top)
        k_sel = ks_blocks[bi, gi, sel]
        v_sel = vs_blocks[bi, gi, sel]
        s_s = jnp.einsum('bqghd,bgqnld->bghqnl', qb, k_sel) * scale
        kpos_s = sel[..., None] * C_SEL_LEN + jnp.arange(C_SEL_LEN)
        m_s = (kpos_s <= tq[None, None, :, None, None])[:, :, None]
        p_s = masked_softmax(s_s, m_s, axis=(-2, -1))
        o_s = jnp.einsum('bghqnl,bgqnld->bqghd', p_s, v_sel)
        kwb = lax.dynamic_slice_in_dim(kw_pad, t0, band, 1)
        vwb = lax.dynamic_slice_in_dim(vw_pad, t0, band, 1)
        kpos_w = t0 - C_WINDOW + jnp.arange(band)
        m_w = ((kpos_w[None, :] <= tq[:, None]) & (kpos_w[None, :] > tq[:, None] - C_WINDOW)
               & (kpos_w[None, :] >= 0))
        s_w = jnp.einsum('bqghd,bkgd->bghqk', qb, kwb) * scale
        p_w = masked_softmax(s_w, m_w)
        o_w = jnp.einsum('bghqk,bkgd->bqghd', p_w, vwb)
        return (gb[:, :, 0, :, :, None] * o_c + gb[:, :, 1, :, :, None] * o_s
                + gb[:, :, 2, :, :, None] * o_w)

    out = lax.map(block, jnp.arange(S // C_Q_BLOCK))
    out = jnp.moveaxis(out, 0, 1).reshape(B, S, C_HEADS * HD).astype(x.dtype)
    return out @ w_o


def swiglu(x, w_gate, w_up, w_down):
    return (jax.nn.silu(x @ w_gate) * (x @ w_up)) @ w_down


def setup_inputs(seed: int = 0) -> dict:
    key = jax.random.key(seed)
    ks = iter(jax.random.split(key, 40))
    NE, NO = (DEPTH + 1) // 2, DEPTH // 2

    def nrm(shape, scale):
        return jax.random.normal(next(ks), shape, jnp.float32) * scale

    def gain(shape):
        return 1.0 + nrm(shape, 0.02)

    return {
        "x": nrm((BATCH, SEQ, D_MODEL), 1.0),
        "ab_w_in": nrm((NE, D_MODEL, AB_IN), D_MODEL ** -0.5),
        "ab_b_igate": nrm((NE, A_HEADS), 0.1),
        "ab_b_fgate": 3.0 + 3.0 * jax.random.uniform(next(ks), (NE, A_HEADS), jnp.float32),
        "ab_mlstm_norm": gain((NE, A_V)),
        "ab_q_norm": gain((NE, B_Q_LORA)),
        "ab_kv_norm": gain((NE, B_KV_LORA)),
        "ab_w_uq": nrm((NE, B_Q_LORA, B_HEADS * (B_NOPE + B_ROPE)), B_Q_LORA ** -0.5),
        "ab_w_ukv": nrm((NE, B_KV_LORA, B_HEADS * (B_NOPE + B_DV)), B_KV_LORA ** -0.5),
        "ab_w_o": nrm((NE, MIX_WIDTH, D_MODEL), DN_BETA * MIX_WIDTH ** -0.5),
        "c_w_in": nrm((NO, D_MODEL, C_IN), D_MODEL ** -0.5),
        "c_b_gate": nrm((NO, 3 * C_HEADS), 0.1),
        "c_pe_k": nrm((NO, C_CMP_LEN, C_HD), 0.1),
        "c_pe_v": nrm((NO, C_CMP_LEN, C_HD), 0.1),
        "c_cmp_w1_k": nrm((NO, C_CMP_LEN * C_HD, C_CMP_HIDDEN), (C_CMP_LEN * C_HD) ** -0.5),
        "c_cmp_w2_k": nrm((NO, C_CMP_HIDDEN, C_HD), C_CMP_HIDDEN ** -0.5),
        "c_cmp_w1_v": nrm((NO, C_CMP_LEN * C_HD, C_CMP_HIDDEN), (C_CMP_LEN * C_HD) ** -0.5),
        "c_cmp_w2_v": nrm((NO, C_CMP_HIDDEN, C_HD), C_CMP_HIDDEN ** -0.5),
        "c_w_o": nrm((NO, MIX_WIDTH, D_MODEL), DN_BETA * MIX_WIDTH ** -0.5),
        "ffn_w_gate": nrm((DEPTH, D_MODEL, FFN_HIDDEN), D_MODEL ** -0.5),
        "ffn_w_up": nrm((DEPTH, D_MODEL, FFN_HIDDEN), D_MODEL ** -0.5),
        "ffn_w_down": nrm((DEPTH, FFN_HIDDEN, D_MODEL), DN_BETA * FFN_HIDDEN ** -0.5),
        "ln_mix_g": gain((DEPTH, D_MODEL)),
        "ln_mix_b": nrm((DEPTH, D_MODEL), 0.02),
        "ln_ffn_g": gain((DEPTH, D_MODEL)),
        "ln_ffn_b": nrm((DEPTH, D_MODEL), 0.02),
    }


def reference(x, ab_w_in, ab_b_igate, ab_b_fgate, ab_mlstm_norm, ab_q_norm, ab_kv_norm, ab_w_uq, ab_w_ukv,
              ab_w_o, c_w_in, c_b_gate, c_pe_k, c_pe_v, c_cmp_w1_k, c_cmp_w2_k, c_cmp_w1_v, c_cmp_w2_v, c_w_o,
              ffn_w_gate, ffn_w_up, ffn_w_down, ln_mix_g, ln_mix_b, ln_ffn_g, ln_ffn_b):
    for layer in range(DEPTH):
        j = layer // 2
        if layer % 2 == 0:
            y = mixer_ab(x, ab_w_in[j], ab_b_igate[j], ab_b_fgate[j], ab_mlstm_norm[j], ab_q_norm[j],
                         ab_kv_norm[j], ab_w_uq[j], ab_w_ukv[j], ab_w_o[j])
        else:
            y = mixer_c(x, c_w_in[j], c_b_gate[j], c_pe_k[j], c_pe_v[j], c_cmp_w1_k[j], c_cmp_w2_k[j],
                        c_cmp_w1_v[j], c_cmp_w2_v[j], c_w_o[j])
        x = layer_norm(DN_ALPHA * x + y, ln_mix_g[layer], ln_mix_b[layer])
        x = layer_norm(DN_ALPHA * x + swiglu(x, ffn_w_gate[layer], ffn_w_up[layer], ffn_w_down[layer]),
                       ln_ffn_g[layer], ln_ffn_b[layer])
    return x
```

```python
import math
from contextlib import ExitStack

import numpy as np
import ml_dtypes
import concourse.bass as bass
import concourse.mybir as mybir
from concourse.bass_utils import run_bass_kernel_spmd

F32 = mybir.dt.float32
BF16 = mybir.dt.bfloat16
AF = mybir.ActivationFunctionType
ALU = mybir.AluOpType
AX = mybir.AxisListType

D = 4096
SEQ = 4096
FF = 11008
NFC = FF // 128
ALPHA = 4.0 ** 0.25
LN_EPS = 1e-5
RMS_EPS = 1e-6
NDS = 28


class Tk:
    __slots__ = ("h", "lw", "rd", "name")

    def __init__(self, h, name):
        self.h = h
        self.lw = None
        self.rd = {}
        self.name = name

    def __getitem__(self, idx):
        return self.h[idx]


class KB:
    def __init__(self, nc):
        self.nc = nc
        self.es = ExitStack()
        self.eng = {"pe": nc.tensor, "act": nc.scalar, "dve": nc.vector, "pool": nc.gpsimd, "sp": nc.sync}
        self.sem = {}
        self.cnt = {}
        self.known = {}
        for k in self.eng:
            self.sem[k] = self.es.enter_context(nc.semaphore("s_" + k))
            self.cnt[k] = 0
            self.known[k] = {}
        self.dsem = [self.es.enter_context(nc.semaphore("d%d" % i)) for i in range(NDS)]
        self.dcnt = [0] * NDS
        self.dnext = 0
        self.nid = 0

    def sb(self, shape, dt, name=None):
        self.nid += 1
        name = "%s_s%d" % (name or "sb", self.nid)
        return Tk(self.es.enter_context(self.nc.sbuf_tensor(name, list(shape), dt)), name)

    def ps(self, shape=(128, 512), dt=F32, name=None):
        self.nid += 1
        name = "%s_p%d" % (name or "ps", self.nid)
        return Tk(self.es.enter_context(self.nc.psum_tensor(name, list(shape), dt)), name)

    def dram(self, name, shape, dt, kind="Internal"):
        return Tk(self.nc.dram_tensor(name, list(shape), dt, kind=kind).ap(), name)

    def barrier(self):
        for e, eng in self.eng.items():
            kn = self.known[e]
            for k in self.eng:
                if k != e and kn.get(k, 0) < self.cnt[k]:
                    eng.wait_ge(self.sem[k], self.cnt[k])
                    kn[k] = self.cnt[k]
            for i in range(NDS):
                key = ("d", i)
                if kn.get(key, 0) < self.dcnt[i]:
                    eng.wait_ge(self.dsem[i], self.dcnt[i])
                    kn[key] = self.dcnt[i]

    def push(self):
        self._outer = getattr(self, "_outer", [])
        self._outer.append(self.es)
        self.es = ExitStack()

    def pop(self):
        self.barrier()
        self.es.close()
        self.es = self._outer.pop()

    def semof(self, key):
        if isinstance(key, tuple):
            return self.dsem[key[1]]
        return self.sem[key]

    def _deps(self, e, R, W, is_dma):
        deps = {}

        def add(tok, same_ok):
            if tok is None:
                return
            key, val = tok
            if key == e and not same_ok and not is_dma:
                return
            if deps.get(key, 0) < val:
                deps[key] = val

        for t in R:
            add(t.lw, e != "pe")
        for t in W:
            add(t.lw, False)
            for k, v in t.rd.items():
                add((k, v), False)
        return deps

    def _emit_waits(self, e, deps):
        eng = self.eng[e]
        kn = self.known[e]
        for key, val in deps.items():
            if kn.get(key, 0) >= val:
                continue
            eng.wait_ge(self.semof(key), val)
            kn[key] = val

    def op(self, e, fn, R=(), W=(), sig=True):
        deps = self._deps(e, R, W, False)
        self._emit_waits(e, deps)
        ins = fn(self.eng[e])
        if sig:
            self.cnt[e] += 1
            ins.then_inc(self.sem[e], 1)
            tv = self.cnt[e]
        else:
            tv = self.cnt[e] + 1
        for t in R:
            if t.rd.get(e, 0) < tv:
                t.rd[e] = tv
        for t in W:
            t.lw = (e, tv)
            t.rd = {}

    def dma(self, q, out, in_, R=(), W=(), transpose=False):
        si = self.dnext
        self.dnext = (self.dnext + 1) % NDS
        key = ("d", si)
        deps = self._deps(q, R, W, True)
        if self.dcnt[si] > 0:
            deps[key] = max(deps.get(key, 0), self.dcnt[si])
        self._emit_waits(q, deps)
        eng = self.eng[q]
        if transpose:
            ins = eng.dma_start_transpose(out=out, in_=in_)
        else:
            ins = eng.dma_start(out=out, in_=in_)
        self.dcnt[si] += 16
        ins.then_inc(self.dsem[si], 16)
        tv = self.dcnt[si]
        for t in R:
            if t.rd.get(key, 0) < tv:
                t.rd[key] = tv
        for t in W:
            t.lw = (key, tv)
            t.rd = {}

    def finish(self):
        eng = self.eng["sp"]
        for i in range(NDS):
            if self.dcnt[i] > 0:
                eng.wait_ge(self.dsem[i], self.dcnt[i])
        for k in self.eng:
            if k != "sp" and self.cnt[k] > 0:
                eng.wait_ge(self.sem[k], self.cnt[k])
        self.es.close()

    def mm(self, out_t, out_ap, lhsT_t, lhsT_ap, rhs_t, rhs_ap, start, stop, sig=None):
        if sig is None:
            sig = True
        self.op("pe", lambda e: e.matmul(out_ap, lhsT=lhsT_ap, rhs=rhs_ap, start=start, stop=stop),
                R=[lhsT_t, rhs_t], W=[out_t], sig=sig)


class Ring:
    def __init__(self, kb, n, shape, dt, name):
        self.bufs = [kb.sb(shape, dt, "%s%d" % (name, i)) for i in range(n)]
        self.i = 0

    def next(self):
        b = self.bufs[self.i]
        self.i = (self.i + 1) % len(self.bufs)
        return b


class PRing:
    def __init__(self, kb, n, name):
        self.bufs = [kb.ps((128, 512), F32, "%s%d" % (name, i)) for i in range(n)]
        self.i = 0

    def next(self):
        b = self.bufs[self.i]
        self.i = (self.i + 1) % len(self.bufs)
        return b


def build_F(T):
    nc = bass.Bass("TRN2", target_bir_lowering=False)
    kb = KB(nc)
    NT = T // 512
    hT = kb.dram("hT", [D, T], BF16, "ExternalInput")
    xT = kb.dram("xT", [D, T], F32, "ExternalInput")
    wo_t = kb.dram("wo_t", [32, 128, 32, 128], F32, "ExternalInput")
    wg_t = kb.dram("wg_t", [NFC, 128, 32, 128], F32, "ExternalInput")
    wu_t = kb.dram("wu_t", [NFC, 128, 32, 128], F32, "ExternalInput")
    wd_t = kb.dram("wd_t", [32, 128, NFC, 128], F32, "ExternalInput")
    lnp = kb.dram("lnp", [128, 4, 32], F32, "ExternalInput")
    xoT = kb.dram("xoT", [D, T], F32, "ExternalOutput")
    zscr = kb.dram("zscr", [D, 512], F32)
    x1scr = kb.dram("x1scr", [D, 512], F32)

    ones = kb.sb([128, 128], F32, "ones")
    kb.op("dve", lambda e: e.memset(ones[:], 1.0), W=[ones])
    lneps = kb.sb([128, 1], F32, "lneps")
    kb.op("dve", lambda e: e.memset(lneps[:], LN_EPS), W=[lneps])
    lnsb = kb.sb([128, 4, 32], F32, "lnsb")
    kb.dma("sp", lnsb[:], lnp[:], W=[lnsb])

    actT = kb.sb([128, 32, 512], BF16, "actT")
    hid = kb.sb([128, NFC, 512], BF16, "hid")
    wring = Ring(kb, 4, [128, 43, 128], BF16, "w")
    xin_r = Ring(kb, 3, [128, 512], F32, "xin")
    z_r = Ring(kb, 3, [128, 512], F32, "z")
    sq_r = Ring(kb, 2, [128, 512], F32, "sq")
    sg_r = Ring(kb, 2, [128, 512], F32, "sg")
    mean = kb.sb([128, 512], F32, "mean")
    rstd = kb.sb([128, 512], F32, "rstd")
    nmr = kb.sb([128, 512], F32, "nmr")
    tmpa = kb.sb([128, 512], F32, "tmpa")
    pr = PRing(kb, 4, "pacc")
    S1 = kb.ps((128, 512), F32, "S1")
    S2 = kb.ps((128, 512), F32, "S2")

    def stats_add(zc, dc):
        kb.mm(S1, S1[:], ones, ones[:], zc, zc[:], dc == 0, dc == 31)
        sq = sq_r.next()
        kb.op("act", lambda e: e.activation(out=sq[:], in_=zc[:], func=AF.Square), R=[zc], W=[sq])
        kb.mm(S2, S2[:], ones, ones[:], sq, sq[:], dc == 0, dc == 31)

    def stats_finish():
        kb.op("act", lambda e: e.mul(out=mean[:], in_=S1[:], mul=1.0 / D), R=[S1], W=[mean])
        kb.op("dve", lambda e: e.tensor_tensor(out=tmpa[:], in0=mean[:], in1=mean[:], op=ALU.mult), R=[mean], W=[tmpa])
        kb.op("dve", lambda e: e.scalar_tensor_tensor(out=tmpa[:], in0=S2[:], scalar=1.0 / D, in1=tmpa[:],
                                                      op0=ALU.mult, op1=ALU.subtract), R=[S2, tmpa], W=[tmpa])
        kb.op("act", lambda e: e.activation(out=rstd[:], in_=tmpa[:], func=AF.Ln, bias=lneps[:, 0:1]), R=[tmpa, lneps], W=[rstd])
        kb.op("act", lambda e: e.activation(out=rstd[:], in_=rstd[:], func=AF.Exp, scale=-0.5), R=[rstd], W=[rstd])
        kb.op("dve", lambda e: e.scalar_tensor_tensor(out=nmr[:], in0=mean[:], scalar=-1.0, in1=rstd[:],
                                                      op0=ALU.mult, op1=ALU.mult), R=[mean, rstd], W=[nmr])

    def normalize(zsrc, gi, bi, sink):
        for dc in range(32):
            zc = z_r.next()
            kb.dma("sp", zc[:], zsrc[dc * 128:(dc + 1) * 128, :], R=[zsrc], W=[zc])
            kb.op("dve", lambda e: e.tensor_tensor(out=zc[:], in0=zc[:], in1=rstd[:], op=ALU.mult), R=[zc, rstd], W=[zc])
            kb.op("pool", lambda e: e.tensor_tensor(out=zc[:], in0=zc[:], in1=nmr[:], op=ALU.add), R=[zc, nmr], W=[zc])
            xc = xin_r.next()
            kb.op("act", lambda e: e.activation(out=xc[:], in_=zc[:], func=AF.Identity,
                                                scale=lnsb[:, gi, dc:dc + 1], bias=lnsb[:, bi, dc:dc + 1]),
                  R=[zc, lnsb], W=[xc])
            sink(dc, xc)

    def load_w(src_ap, n):
        w = wring.next()
        kb.dma("pool", w[:, 0:n, :], src_ap, W=[w])
        return w

    for tt in range(NT):
        ts = slice(tt * 512, (tt + 1) * 512)
        kb.dma("sp", actT[:], hT[:, ts].rearrange("(k p) t -> p k t", p=128), W=[actT])
        for dc in range(32):
            w = load_w(wo_t[dc], 32)
            acc = pr.next()
            for kc in range(32):
                kb.mm(acc, acc[:], w, w[:, kc, :], actT, actT[:, kc, :], kc == 0, kc == 31)
            xin = xin_r.next()
            kb.dma("sp", xin[:], xT[dc * 128:(dc + 1) * 128, ts], W=[xin])
            zc = z_r.next()
            kb.op("dve", lambda e: e.scalar_tensor_tensor(out=zc[:], in0=xin[:], scalar=ALPHA, in1=acc[:],
                                                          op0=ALU.mult, op1=ALU.add), R=[xin, acc], W=[zc])
            stats_add(zc, dc)
            kb.dma("sp", zscr[dc * 128:(dc + 1) * 128, :], zc[:], R=[zc], W=[zscr])
        stats_finish()

        def sink1(dc, xc):
            kb.op("pool", lambda e: e.tensor_copy(out=actT[:, dc, :], in_=xc[:]), R=[xc], W=[actT])
            kb.dma("sp", x1scr[dc * 128:(dc + 1) * 128, :], xc[:], R=[xc], W=[x1scr])
        normalize(zscr, 0, 1, sink1)
        for f in range(NFC):
            wg = load_w(wg_t[f], 32)
            wu = load_w(wu_t[f], 32)
            pg = pr.next()
            pu = pr.next()
            for kc in range(32):
                kb.mm(pg, pg[:], wg, wg[:, kc, :], actT, actT[:, kc, :], kc == 0, kc == 31)
            for kc in range(32):
                kb.mm(pu, pu[:], wu, wu[:, kc, :], actT, actT[:, kc, :], kc == 0, kc == 31)
            sg = sg_r.next()
            kb.op("act", lambda e: e.activation(out=sg[:], in_=pg[:], func=AF.Silu), R=[pg], W=[sg])
            kb.op("dve", lambda e: e.tensor_tensor(out=hid[:, f, :], in0=sg[:], in1=pu[:], op=ALU.mult),
                  R=[sg, pu], W=[hid])
        for dc in range(32):
            wa = load_w(wd_t[dc, :, 0:43, :], 43)
            wb = load_w(wd_t[dc, :, 43:86, :], 43)
            acc = pr.next()
            for fc in range(NFC):
                w = wa if fc < 43 else wb
                kb.mm(acc, acc[:], w, w[:, fc % 43, :], hid, hid[:, fc, :], fc == 0, fc == NFC - 1)
            xin = xin_r.next()
            kb.dma("sp", xin[:], x1scr[dc * 128:(dc + 1) * 128, :], R=[x1scr], W=[xin])
            zc = z_r.next()
            kb.op("dve", lambda e: e.scalar_tensor_tensor(out=zc[:], in0=xin[:], scalar=ALPHA, in1=acc[:],
                                                          op0=ALU.mult, op1=ALU.add), R=[xin, acc], W=[zc])
            stats_add(zc, dc)
            kb.dma("sp", zscr[dc * 128:(dc + 1) * 128, :], zc[:], R=[zc], W=[zscr])
        stats_finish()

        def sink2(dc, xc):
            kb.dma("sp", xoT[dc * 128:(dc + 1) * 128, ts], xc[:], R=[xc], W=[xoT])
        normalize(zscr, 2, 3, sink2)
    kb.finish()
    return nc


def tile_kxn(w, ncol=128):
    K, N = w.shape
    return np.ascontiguousarray(w.reshape(K // 128, 128, N // ncol, ncol).transpose(2, 1, 0, 3))


def ln_pack(vecs):
    return np.ascontiguousarray(np.stack([v.reshape(32, 128).T for v in vecs], axis=1))


_CACHE = {}


def run_F(hT_list, xT_list, w_o, w_gate, w_up, w_down, lnv):
    T = xT_list[0].shape[1]
    key = ("F", T)
    if key not in _CACHE:
        _CACHE[key] = build_F(T)
    nc = _CACHE[key]
    wo_t = tile_kxn(w_o)
    wg_t = tile_kxn(w_gate)
    wu_t = tile_kxn(w_up)
    wd_t = tile_kxn(w_down)
    lnp = ln_pack(lnv)
    in_maps = [{"hT": hT_list[c], "xT": xT_list[c], "wo_t": wo_t, "wg_t": wg_t, "wu_t": wu_t, "wd_t": wd_t,
                "lnp": lnp} for c in range(8)]
    res = run_bass_kernel_spmd(nc, in_maps, core_ids=list(range(8)))
    return [r["xoT"] for r in res.results]


M0_NW = 41
MLA_SCALE = 192.0 ** -0.5


def build_M0(stop_after=None):
    nc = bass.Bass("TRN2", target_bir_lowering=False)
    kb = KB(nc)
    S = SEQ
    xT = kb.dram("xT", [D, S], F32, "ExternalInput")
    wA = kb.dram("wA", [M0_NW, 128, 32, 128], F32, "ExternalInput")
    wuq = kb.dram("wuq", [128, 8, 2048], F32, "ExternalInput")
    wukv_k = kb.dram("wukv_k", [128, 4, 1024], F32, "ExternalInput")
    wukv_v = kb.dram("wukv_v", [128, 4, 1024], F32, "ExternalInput")
    smallp = kb.dram("smallp", [128, 24], F32, "ExternalInput")
    cosT = kb.dram("cosT", [64, S], F32, "ExternalInput")
    sinT = kb.dram("sinT", [64, S], F32, "ExternalInput")
    dmask = kb.dram("dmask", [4, 128, 512], BF16, "ExternalInput")
    ident_d = kb.dram("ident", [128, 128], F32, "ExternalInput")
    hTo = kb.dram("hTo", [2048, S], BF16, "ExternalOutput")
    qaT = kb.dram("qaT", [512, S], BF16)
    kaT = kb.dram("kaT", [512, S], BF16)
    ogT = kb.dram("ogT", [1024, S], BF16)
    cqT = kb.dram("cqT", [1024, S], BF16)
    ckvT = kb.dram("ckvT", [512, S], BF16)
    krT = kb.dram("krT", [128, S], BF16)
    va = kb.dram("va", [S, 1024], BF16)
    grep = kb.dram("grep", [4, 128, S], F32)
    qT = kb.dram("qT", [8, 192, S], BF16)
    knT = kb.dram("knT", [8, 128, S], BF16)
    kroT = kb.dram("kroT", [64, S], BF16)
    vmla = kb.dram("vmla", [S, 1024], BF16)
    fm_dst = [(qaT, i) for i in range(4)] + [(kaT, i) for i in range(4)] + [(ogT, i) for i in range(8)] + \
             [(cqT, i) for i in range(8)] + [(ckvT, i) for i in range(4)] + [(krT, 0)]

    P = [kb.ps((128, 512), F32, "P%d" % i) for i in range(8)]
    onesf = kb.sb([128, 128], F32, "onesf")
    onesb = kb.sb([128, 128], BF16, "onesb")
    kb.op("dve", lambda e: e.memset(onesf[:], 1.0), W=[onesf])
    kb.op("dve", lambda e: e.memset(onesb[:], 1.0), W=[onesb])
    sp_ = kb.sb([128, 24], F32, "smallp_sb")
    kb.dma("sp", sp_[:], smallp[:], W=[sp_])
    masks = kb.sb([128, 4, 512], BF16, "masks")
    kb.dma("sp", masks[:], dmask[:].rearrange("m p t -> p m t"), W=[masks])
    ident = kb.sb([128, 128], F32, "ident_sb")
    kb.dma("sp", ident[:], ident_d[:], W=[ident])

    kb.push()
    xTb = kb.sb([128, 32, 2048], BF16, "xTb")
    wring = Ring(kb, 6, [128, 32, 128], BF16, "wA")
    stg = Ring(kb, 3, [128, 512], BF16, "stgA")
    stgf = Ring(kb, 2, [128, 512], F32, "stgAf")
    pi = 0
    for st in range(2):
        for q4 in range(4):
            c0 = st * 2048 + q4 * 512
            kb.dma("pool", xTb[:, :, q4 * 512:(q4 + 1) * 512],
                   xT[:, c0:c0 + 512].rearrange("(k p) t -> p k t", p=128), W=[xTb])
        for wi in list(range(29)) + list(range(37, 41)):
            w = wring.next()
            kb.dma("pool", w[:], wA[wi], W=[w])
            for q4 in range(4):
                acc = P[pi % 4]
                pi += 1
                for kc in range(32):
                    kb.mm(acc, acc[:], w, w[:, kc, :], xTb, xTb[:, kc, q4 * 512:(q4 + 1) * 512], kc == 0, kc == 31)
                c0 = st * 2048 + q4 * 512
                if wi < 29:
                    dst, ci = fm_dst[wi]
                    s = stg.next()
                    kb.op("act", lambda e: e.copy(out=s[:], in_=acc[:]), R=[acc], W=[s])
                    kb.dma("sp", dst[ci * 128:(ci + 1) * 128, c0:c0 + 512], s[:], R=[s], W=[dst])
                else:
                    s = stgf.next()
                    kb.op("act", lambda e: e.copy(out=s[:], in_=acc[:]), R=[acc], W=[s])
                    kb.dma("sp", grep[wi - 37, :, c0:c0 + 512], s[:], R=[s], W=[grep])
        for g2 in range(2):
            ws = []
            for j in range(4):
                w = wring.next()
                kb.dma("pool", w[:], wA[29 + g2 * 4 + j], W=[w])
                ws.append(w)
            for tc in range(16):
                acc = P[pi % 4]
                pi += 1
                for j in range(4):
                    for kc in range(32):
                        kb.mm(acc, acc[:, j * 128:(j + 1) * 128], xTb, xTb[:, kc, tc * 128:(tc + 1) * 128],
                              ws[j], ws[j][:, kc, :], kc == 0, kc == 31)
                s = stg.next()
                kb.op("act", lambda e: e.copy(out=s[:], in_=acc[:]), R=[acc], W=[s])
                t0 = st * 2048 + tc * 128
                kb.dma("sp", va[t0:t0 + 128, g2 * 512:(g2 + 1) * 512], s[:], R=[s], W=[va])
    kb.pop()

    if stop_after == "A":
        kb.finish()
        return nc
    kb.push()
    wuq_s = kb.sb([128, 8, 2048], BF16, "wuq")
    wk_s = kb.sb([128, 4, 1024], BF16, "wk")
    wv_s = kb.sb([128, 4, 1024], BF16, "wv")
    kb.dma("pool", wuq_s[:], wuq[:], W=[wuq_s])
    kb.dma("pool", wk_s[:], wukv_k[:], W=[wk_s])
    kb.dma("pool", wv_s[:], wukv_v[:], W=[wv_s])
    cq_r = Ring(kb, 2, [128, 8, 512], BF16, "cq")
    ckv_r = Ring(kb, 2, [128, 4, 512], BF16, "ckv")
    cqn = kb.sb([128, 8, 512], BF16, "cqn")
    ckvn = kb.sb([128, 4, 512], BF16, "ckvn")
    sq_r = Ring(kb, 2, [128, 512], F32, "sqB")
    rq = kb.sb([128, 512], F32, "rq")
    rkv = kb.sb([128, 512], F32, "rkv")
    cos_r = Ring(kb, 2, [64, 512], F32, "cos")
    sin_r = Ring(kb, 2, [64, 512], F32, "sin")
    krA_r = Ring(kb, 2, [64, 512], BF16, "krA")
    krB_r = Ring(kb, 2, [64, 512], BF16, "krB")
    t1_r = Ring(kb, 2, [64, 512], F32, "t1")
    t2_r = Ring(kb, 2, [64, 512], F32, "t2")
    stg = Ring(kb, 4, [128, 512], BF16, "stgB")
    pi = 0
    for tt in range(8):
        ts = slice(tt * 512, (tt + 1) * 512)
        cq = cq_r.next()
        kb.dma("sp", cq[:], cqT[:, ts].rearrange("(c p) t -> p c t", p=128), R=[cqT], W=[cq])
        ckv = ckv_r.next()
        kb.dma("sp", ckv[:], ckvT[:, ts].rearrange("(c p) t -> p c t", p=128), R=[ckvT], W=[ckv])
        cs = cos_r.next()
        sn = sin_r.next()
        kb.dma("sp", cs[:], cosT[:, ts], W=[cs])
        kb.dma("sp", sn[:], sinT[:, ts], W=[sn])
        for (src, nch, dstr, nf) in ((cq, 8, rq, 1024.0), (ckv, 4, rkv, 512.0)):
            acc = P[4]
            for c in range(nch):
                sq = sq_r.next()
                kb.op("act", lambda e: e.activation(out=sq[:], in_=src[:, c, :], func=AF.Square), R=[src], W=[sq])
                kb.mm(acc, acc[:], onesf, onesf[:], sq, sq[:], c == 0, c == nch - 1)
            kb.op("dve", lambda e: e.tensor_scalar(out=dstr[:], in0=acc[:], scalar1=1.0 / nf, scalar2=RMS_EPS,
                                                   op0=ALU.mult, op1=ALU.add), R=[acc], W=[dstr])
            kb.op("act", lambda e: e.activation(out=dstr[:], in_=dstr[:], func=AF.Ln), R=[dstr], W=[dstr])
            kb.op("act", lambda e: e.activation(out=dstr[:], in_=dstr[:], func=AF.Exp, scale=-0.5), R=[dstr], W=[dstr])
        for c in range(8):
            kb.op("pool", lambda e: e.tensor_scalar(out=cqn[:, c, :], in0=cq[:, c, :], scalar1=sp_[:, 12 + c:13 + c],
                                                    scalar2=None, op0=ALU.mult), R=[cq, sp_], W=[cqn])
        for c in range(4):
            kb.op("dve", lambda e: e.scalar_tensor_tensor(out=ckvn[:, c, :], in0=ckv[:, c, :], scalar=sp_[:, 20 + c:21 + c],
                                                          in1=rkv[:], op0=ALU.mult, op1=ALU.mult),
                  R=[ckv, sp_, rkv], W=[ckvn])
        krA = krA_r.next()
        krB = krB_r.next()
        kb.dma("sp", krA[:], krT[0:64, ts], R=[krT], W=[krA])
        kb.dma("sp", krB[:], krT[64:128, ts], R=[krT], W=[krB])
        t1 = t1_r.next()
        t2 = t2_r.next()
        kb.op("dve", lambda e: e.tensor_tensor(out=t1[:], in0=krA[:], in1=cs[:], op=ALU.mult), R=[krA, cs], W=[t1])
        kb.op("pool", lambda e: e.tensor_tensor(out=t2[:], in0=krB[:], in1=sn[:], op=ALU.mult), R=[krB, sn], W=[t2])
        s = stg.next()
        kb.op("dve", lambda e: e.tensor_tensor(out=s[0:64, :], in0=t1[:], in1=t2[:], op=ALU.add), R=[t1, t2], W=[s])
        kb.dma("sp", kroT[:, ts], s[0:64, :], R=[s], W=[kroT])
        for h in range(8):
            acc = P[pi % 4]
            pi += 1
            for c in range(8):
                kb.mm(acc, acc[:], wuq_s, wuq_s[:, c, h * 256:h * 256 + 128], cqn, cqn[:, c, :], c == 0, c == 7)
            s = stg.next()
            kb.op("dve", lambda e: e.tensor_tensor(out=s[:], in0=acc[:], in1=rq[:], op=ALU.mult), R=[acc, rq], W=[s])
            kb.dma("sp", qT[h, 0:128, ts], s[:], R=[s], W=[qT])
            a1 = P[5]
            a2 = P[6]
            for c in range(8):
                kb.mm(a1, a1[0:64, :], wuq_s, wuq_s[:, c, h * 256 + 128:h * 256 + 192], cqn, cqn[:, c, :], c == 0, c == 7)
            for c in range(8):
                kb.mm(a2, a2[0:64, :], wuq_s, wuq_s[:, c, h * 256 + 192:h * 256 + 256], cqn, cqn[:, c, :], c == 0, c == 7)
            t1 = t1_r.next()
            t2 = t2_r.next()
            kb.op("dve", lambda e: e.tensor_tensor(out=t1[:], in0=a1[0:64, :], in1=cs[:], op=ALU.mult), R=[a1, cs], W=[t1])
            kb.op("dve", lambda e: e.tensor_tensor(out=t2[:], in0=a2[0:64, :], in1=sn[:], op=ALU.mult), R=[a2, sn], W=[t2])
            kb.op("pool", lambda e: e.tensor_tensor(out=t1[:], in0=t1[:], in1=t2[:], op=ALU.add), R=[t1, t2], W=[t1])
            s = stg.next()
            kb.op("dve", lambda e: e.tensor_tensor(out=s[0:64, :], in0=t1[:], in1=rq[0:64, :], op=ALU.mult), R=[t1, rq], W=[s])
            kb.dma("sp", qT[h, 128:192, ts], s[0:64, :], R=[s], W=[qT])
            acc = P[pi % 4]
            pi += 1
            for c in range(4):
                kb.mm(acc, acc[:], wk_s, wk_s[:, c, h * 128:(h + 1) * 128], ckvn, ckvn[:, c, :], c == 0, c == 3)
            s = stg.next()
            kb.op("act", lambda e: e.copy(out=s[:], in_=acc[:]), R=[acc], W=[s])
            kb.dma("sp", knT[h, :, ts], s[:], R=[s], W=[knT])
        for tc in range(4):
            for hf in range(2):
                acc = P[pi % 4]
                pi += 1
                for c in range(4):
                    kb.mm(acc, acc[:], ckvn, ckvn[:, c, tc * 128:(tc + 1) * 128], wv_s, wv_s[:, c, hf * 512:(hf + 1) * 512],
                          c == 0, c == 3)
                s = stg.next()
                kb.op("act", lambda e: e.copy(out=s[:], in_=acc[:]), R=[acc], W=[s])
                t0 = tt * 512 + tc * 128
                kb.dma("sp", vmla[t0:t0 + 128, hf * 512:(hf + 1) * 512], s[:], R=[s], W=[vmla])
    kb.pop()

    if stop_after == "B":
        kb.finish()
        return nc
    kb.push()
    kro = kb.sb([64, S], BF16, "kro")
    kb.dma("sp", kro[:], kroT[:], R=[kroT], W=[kro])
    kn_r = Ring(kb, 2, [128, S], BF16, "kn")
    v_r = Ring(kb, 2, [128, 32, 128], BF16, "vC")
    qn_r = Ring(kb, 2, [128, 512], BF16, "qn")
    qr_r = Ring(kb, 2, [64, 512], BF16, "qr")
    pT_r = Ring(kb, 3, [128, 512], BF16, "pT")
    rr_r = Ring(kb, 2, [128, 512], F32, "rrC")
    ho_r = Ring(kb, 2, [128, 512], BF16, "hoC")
    si = 0
    for h in range(8):
        kn = kn_r.next()
        kb.dma("sp", kn[:], knT[h], R=[knT], W=[kn])
        v = v_r.next()
        kb.dma("sp", v[:], vmla[:, h * 128:(h + 1) * 128].rearrange("(c p) d -> p c d", p=128), R=[vmla], W=[v])
        for j in range(8):
            ts = slice(j * 512, (j + 1) * 512)
            qn = qn_r.next()
            qr = qr_r.next()
            kb.dma("sp", qn[:], qT[h, 0:128, ts], R=[qT], W=[qn])
            kb.dma("sp", qr[:], qT[h, 128:192, ts], R=[qT], W=[qr])
            O = P[4 + (j % 2)]
            L = P[6 + (j % 2)]
            nk = 4 * (j + 1)
            for kc in range(nk):
                sps = P[si % 4]
                si += 1
                ks = slice(kc * 128, (kc + 1) * 128)
                kb.mm(sps, sps[:], kn, kn[:, ks], qn, qn[:], True, False)
                kb.mm(sps, sps[:], kro, kro[:, ks], qr, qr[:], False, True)
                pT = pT_r.next()
                kb.op("act", lambda e: e.activation(out=pT[:], in_=sps[:], func=AF.Exp, scale=MLA_SCALE), R=[sps], W=[pT])
                m = kc - 4 * j
                if m >= 0:
                    kb.op("pool", lambda e: e.tensor_tensor(out=pT[:], in0=pT[:], in1=masks[:, m, :], op=ALU.mult),
                          R=[pT, masks], W=[pT])
                kb.mm(O, O[:], v, v[:, kc, :], pT, pT[:], kc == 0, kc == nk - 1)
                kb.mm(L, L[:], onesb, onesb[:], pT, pT[:], kc == 0, kc == nk - 1)
            rr = rr_r.next()
            kb.op("dve", lambda e: e.reciprocal(out=rr[:], in_=L[:]), R=[L], W=[rr])
            ho = ho_r.next()
            kb.op("dve", lambda e: e.tensor_tensor(out=ho[:], in0=O[:], in1=rr[:], op=ALU.mult), R=[O, rr], W=[ho])
            kb.dma("sp", hTo[1024 + h * 128:1024 + (h + 1) * 128, ts], ho[:], R=[ho], W=[hTo])
    kb.pop()

    if stop_after == "C":
        kb.finish()
        return nc
    kb.push()
    ones4k = kb.sb([128, S], F32, "ones4k")
    kb.op("pool", lambda e: e.memset(ones4k[:], 1.0), W=[ones4k])
    gbias = kb.sb([128, 4], F32, "gbias")
    kb.op("dve", lambda e: e.tensor_scalar(out=gbias[:], in0=sp_[:, 0:4], scalar1=1.0 / 15.0, scalar2=None, op0=ALU.mult),
          R=[sp_], W=[gbias])
    ga = kb.sb([128, S], F32, "ga")
    gb = kb.sb([128, S], F32, "gb")
    nA = kb.sb([128, S], F32, "nA")
    e3 = kb.sb([128, S], F32, "e3")
    acol = kb.sb([128, 32], F32, "acol")
    kaS = kb.sb([128, 2, S], BF16, "kaS")
    vaS = kb.sb([128, 32, 512], BF16, "vaS")
    qa_r = Ring(kb, 2, [128, 2, 512], BF16, "qa")
    og_r = Ring(kb, 2, [128, 4, 512], BF16, "og")
    dm_r = Ring(kb, 2, [128, 512], F32, "dm")
    pT_r = Ring(kb, 3, [128, 512], BF16, "pTD")
    rr = kb.sb([128, 512], F32, "rrD")
    h0 = kb.sb([128, 4, 512], F32, "h0")
    sq_r = Ring(kb, 2, [128, 512], F32, "sqD")
    sg_r = Ring(kb, 2, [128, 512], F32, "sgD")
    ho_r = Ring(kb, 2, [128, 512], BF16, "hoD")
    si = 0
    for hd in range(2):
        kb.dma("sp", ga[:], grep[hd], R=[grep], W=[ga])
        kb.dma("sp", gb[:], grep[2 + hd], R=[grep], W=[gb])
        kb.dma("sp", kaS[:], kaT[hd * 256:(hd + 1) * 256, :].rearrange("(c p) t -> p c t", p=128), R=[kaT], W=[kaS])
        kb.dma("sp", vaS[:], va[:, hd * 512:(hd + 1) * 512].rearrange("(c p) d -> p c d", p=128), R=[va], W=[vaS])
        kb.op("act", lambda e: e.activation(out=ga[:], in_=ga[:], func=AF.Tanh, scale=1.0 / 15.0, bias=gbias[:, hd:hd + 1]),
              R=[ga, gbias], W=[ga])
        kb.op("pool", lambda e: e.tensor_scalar(out=ga[:], in0=ga[:], scalar1=15.0, scalar2=None, op0=ALU.mult), R=[ga], W=[ga])
        kb.op("act", lambda e: e.activation(out=gb[:], in_=gb[:], func=AF.Tanh, scale=1.0 / 15.0, bias=gbias[:, 2 + hd:3 + hd]),
              R=[gb, gbias], W=[gb])
        kb.op("act", lambda e: e.activation(out=gb[:], in_=gb[:], func=AF.Exp, scale=-15.0), R=[gb], W=[gb])
        kb.op("act", lambda e: e.activation(out=gb[:], in_=gb[:], func=AF.Ln, bias=1.0), R=[gb], W=[gb])
        kb.op("dve", lambda e: e.tensor_tensor_scan(out=gb[:], data0=ones4k[:], data1=gb[:], initial=0.0,
                                                    op0=ALU.mult, op1=ALU.add), R=[ones4k, gb], W=[gb])
        kb.op("dve", lambda e: e.tensor_tensor(out=ga[:], in0=ga[:], in1=gb[:], op=ALU.add), R=[ga, gb], W=[ga])
        kb.op("dve", lambda e: e.tensor_tensor_scan(out=nA[:], data0=ga[:], data1=ga[:], initial=0.0,
                                                    op0=ALU.max, op1=ALU.max), R=[ga], W=[nA])
        kb.op("pool", lambda e: e.tensor_scalar(out=nA[:], in0=nA[:], scalar1=-1.0, scalar2=None, op0=ALU.mult), R=[nA], W=[nA])
        kb.op("dve", lambda e: e.tensor_tensor(out=e3[:], in0=gb[:], in1=nA[:], op=ALU.add), R=[gb, nA], W=[e3])
        kb.op("act", lambda e: e.activation(out=e3[:], in_=e3[:], func=AF.Exp), R=[e3], W=[e3])
        for c in range(32):
            tp = P[si % 3]
            si += 1
            kb.op("pe", lambda e: e.transpose(out=tp[:, 0:128], in_=ga[:, c * 128:(c + 1) * 128], identity=ident[:]),
                  R=[ga, ident], W=[tp])
            kb.op("dve", lambda e: e.tensor_scalar(out=acol[:, c:c + 1], in0=tp[:, 0:1], scalar1=-math.log(16.0), scalar2=None,
                                                   op0=ALU.add), R=[tp], W=[acol])
        for j in range(8):
            ts = slice(j * 512, (j + 1) * 512)
            qa = qa_r.next()
            kb.dma("sp", qa[:], qaT[hd * 256:(hd + 1) * 256, ts].rearrange("(c p) t -> p c t", p=128), R=[qaT], W=[qa])
            og = og_r.next()
            kb.dma("sp", og[:], ogT[hd * 512:(hd + 1) * 512, ts].rearrange("(c p) t -> p c t", p=128), R=[ogT], W=[og])
            O = [P[3], P[4], P[5], P[6]]
            L = P[7]
            nk = 4 * (j + 1)
            for kc in range(nk):
                sps = P[si % 3]
                si += 1
                ks = slice(kc * 128, (kc + 1) * 128)
                kb.mm(sps, sps[:], kaS, kaS[:, 0, ks], qa, qa[:, 0, :], True, False)
                kb.mm(sps, sps[:], kaS, kaS[:, 1, ks], qa, qa[:, 1, :], False, True)
                dm = dm_r.next()
                kb.op("act", lambda e: e.activation(out=dm[:], in_=nA[:, ts], func=AF.Exp, bias=acol[:, kc:kc + 1]),
                      R=[nA, acol], W=[dm])
                m = kc - 4 * j
                if m >= 0:
                    kb.op("pool", lambda e: e.tensor_tensor(out=dm[:], in0=dm[:], in1=masks[:, m, :], op=ALU.mult),
                          R=[dm, masks], W=[dm])
                pT = pT_r.next()
                kb.op("dve", lambda e: e.tensor_tensor(out=pT[:], in0=sps[:], in1=dm[:], op=ALU.mult), R=[sps, dm], W=[pT])
                for c in range(4):
                    kb.mm(O[c], O[c][:], vaS, vaS[:, kc, c * 128:(c + 1) * 128], pT, pT[:], kc == 0, kc == nk - 1)
                kb.mm(L, L[:], onesb, onesb[:], pT, pT[:], kc == 0, kc == nk - 1)
            kb.op("act", lambda e: e.activation(out=rr[:], in_=L[:], func=AF.Abs), R=[L], W=[rr])
            kb.op("dve", lambda e: e.tensor_tensor(out=rr[:], in0=rr[:], in1=e3[:, ts], op=ALU.max), R=[rr, e3], W=[rr])
            kb.op("dve", lambda e: e.reciprocal(out=rr[:], in_=rr[:]), R=[rr], W=[rr])
            ssp = P[si % 3]
            si += 1
            for c in range(4):
                kb.op("dve", lambda e: e.tensor_tensor(out=h0[:, c, :], in0=O[c][:], in1=rr[:], op=ALU.mult),
                      R=[O[c], rr], W=[h0])
                sq = sq_r.next()
                kb.op("act", lambda e: e.activation(out=sq[:], in_=h0[:, c, :], func=AF.Square), R=[h0], W=[sq])
                kb.mm(ssp, ssp[:], onesf, onesf[:], sq, sq[:], c == 0, c == 3)
            kb.op("dve", lambda e: e.tensor_scalar(out=rr[:], in0=ssp[:], scalar1=1.0 / 512.0, scalar2=RMS_EPS,
                                                   op0=ALU.mult, op1=ALU.add), R=[ssp], W=[rr])
            kb.op("act", lambda e: e.activation(out=rr[:], in_=rr[:], func=AF.Ln), R=[rr], W=[rr])
            kb.op("act", lambda e: e.activation(out=rr[:], in_=rr[:], func=AF.Exp, scale=-0.5), R=[rr], W=[rr])
            for c in range(4):
                sg = sg_r.next()
                kb.op("act", lambda e: e.activation(out=sg[:], in_=og[:, c, :], func=AF.Sigmoid), R=[og], W=[sg])
                kb.op("pool", lambda e: e.tensor_tensor(out=sg[:], in0=sg[:], in1=rr[:], op=ALU.mult), R=[sg, rr], W=[sg])
                ho = ho_r.next()
                gi = 4 + hd * 4 + c
                kb.op("dve", lambda e: e.scalar_tensor_tensor(out=ho[:], in0=h0[:, c, :], scalar=sp_[:, gi:gi + 1], in1=sg[:],
                                                              op0=ALU.mult, op1=ALU.mult), R=[h0, sp_, sg], W=[ho])
                r0 = hd * 512 + c * 128
                kb.dma("sp", hTo[r0:r0 + 128, ts], ho[:], R=[ho], W=[hTo])
    kb.pop()
    kb.finish()
    return nc


AB_OFF = {"q_a": 0, "k_a": 1024, "v_a": 2048, "ig": 4096, "fg": 4100, "og": 4104, "cq": 6152, "ckv": 7176, "kr": 7688}


def prep_M0_consts():
    pos = np.arange(SEQ, dtype=np.float32)
    inv = (10000.0 ** (-np.arange(32, dtype=np.float32) / 32)).astype(np.float32)
    ang = pos[None, :] * inv[:, None]
    cos, sin = np.cos(ang).astype(np.float32), np.sin(ang).astype(np.float32)
    cosT = np.concatenate([cos, cos], 0)
    sinT = np.concatenate([-sin, sin], 0)
    kl = np.arange(128)[:, None]
    ql = np.arange(512)[None, :]
    dmask = np.stack([(128 * m + kl <= ql) for m in range(4)]).astype(np.float32).astype(ml_dtypes.bfloat16)
    return {"cosT": np.ascontiguousarray(cosT), "sinT": np.ascontiguousarray(sinT), "dmask": dmask,
            "ident": np.eye(128, dtype=np.float32)}


def prep_M0_weights(hh, w_in, b_ig, b_fg, mnorm, qnorm, kvnorm, w_uq, w_ukv):
    o = AB_OFF
    cols = []
    cols.append(w_in[:, o["q_a"] + hh * 512:o["q_a"] + hh * 512 + 512])
    cols.append(w_in[:, o["k_a"] + hh * 512:o["k_a"] + hh * 512 + 512])
    cols.append(w_in[:, o["og"] + hh * 1024:o["og"] + hh * 1024 + 1024])
    cols.append(w_in[:, o["cq"]:o["cq"] + 1024])
    cols.append(w_in[:, o["ckv"]:o["ckv"] + 512])
    kr = w_in[:, o["kr"]:o["kr"] + 64]
    cols.append(kr)
    cols.append(np.concatenate([kr[:, 32:64], kr[:, 0:32]], 1))
    cols.append(w_in[:, o["v_a"] + hh * 1024:o["v_a"] + hh * 1024 + 1024])
    for c in (o["ig"] + 2 * hh, o["ig"] + 2 * hh + 1, o["fg"] + 2 * hh, o["fg"] + 2 * hh + 1):
        cols.append(np.repeat(w_in[:, c:c + 1], 128, axis=1))
    wA = tile_kxn(np.concatenate(cols, 1))
    assert wA.shape[0] == M0_NW
    hs = range(hh * 8, hh * 8 + 8)
    uq = []
    for h in hs:
        blk = w_uq[:, h * 192:(h + 1) * 192]
        uq += [blk[:, 0:128], blk[:, 128:192], blk[:, 160:192], blk[:, 128:160]]
    uq = np.concatenate(uq, 1)
    wuq = np.ascontiguousarray(uq.reshape(8, 128, 2048).transpose(1, 0, 2))
    kvk = np.concatenate([w_ukv[:, h * 256:h * 256 + 128] for h in hs], 1)
    kvv = np.concatenate([w_ukv[:, h * 256 + 128:h * 256 + 256] for h in hs], 1)
    wk = np.ascontiguousarray(kvk.reshape(4, 128, 1024).transpose(1, 0, 2))
    wv = np.ascontiguousarray(kvv.reshape(4, 128, 1024).transpose(1, 0, 2))
    sp = np.zeros((128, 24), np.float32)
    sp[:, 0] = b_ig[2 * hh]
    sp[:, 1] = b_ig[2 * hh + 1]
    sp[:, 2] = b_fg[2 * hh]
    sp[:, 3] = b_fg[2 * hh + 1]
    sp[:, 4:12] = mnorm[hh * 1024:(hh + 1) * 1024].reshape(8, 128).T
    sp[:, 12:20] = qnorm.reshape(8, 128).T
    sp[:, 20:24] = kvnorm.reshape(4, 128).T
    return {"wA": wA, "wuq": wuq, "wukv_k": wk, "wukv_v": wv, "smallp": sp}


def run_M0(xT_list, wlist, consts, stop_after=None):
    if "M0" not in _CACHE:
        _CACHE["M0"] = build_M0(stop_after)
    nc = _CACHE["M0"]
    in_maps = []
    for c in range(8):
        m = {"xT": xT_list[c]}
        m.update(wlist[c])
        m.update(consts)
        in_maps.append(m)
    res = run_bass_kernel_spmd(nc, in_maps, core_ids=list(range(8)))
    return [r["hTo"] for r in res.results]


M1_NW = 29
NSA_SCALE = 128.0 ** -0.5
GELU_C = 1.5957691216057308


def build_M1(stop_after=None):
    nc = bass.Bass("TRN2", target_bir_lowering=False)
    kb = KB(nc)
    S = SEQ
    xT = kb.dram("xT", [D, S], F32, "ExternalInput")
    wA = kb.dram("wA", [M1_NW, 128, 32, 128], F32, "ExternalInput")
    w1k = kb.dram("w1k", [128, 32, 256], F32, "ExternalInput")
    w1v = kb.dram("w1v", [128, 32, 256], F32, "ExternalInput")
    w2k = kb.dram("w2k", [128, 2, 128], F32, "ExternalInput")
    w2v = kb.dram("w2v", [128, 2, 128], F32, "ExternalInput")
    peT = kb.dram("peT", [128, 2, 32], F32, "ExternalInput")
    gbias = kb.dram("gbias", [128, 1], F32, "ExternalInput")
    cmaskT = kb.dram("cmaskT", [256, S], BF16, "ExternalInput")
    ovl = kb.dram("ovl", [256, 64], BF16, "ExternalInput")
    KM = kb.dram("KM", [S, 64], F32, "ExternalInput")
    AM = kb.dram("AM", [S, 64], F32, "ExternalInput")
    Emat = kb.dram("Emat", [64, S], BF16, "ExternalInput")
    dmask = kb.dram("dmask", [4, 128, 512], BF16, "ExternalInput")
    wmask = kb.dram("wmask", [8, 128, 512], BF16, "ExternalInput")
    selm = kb.dram("selm", [48, 48, 128], F32, "ExternalInput")
    ident_d = kb.dram("ident", [128, 128], F32, "ExternalInput")
    hTo = kb.dram("hTo", [2048, S], BF16, "ExternalOutput")
    qT = kb.dram("qTs", [2048, S], BF16)
    kcT = kb.dram("kcT", [256, S], BF16)
    vcT = kb.dram("vcT", [256, S], BF16)
    ksT = kb.dram("ksT", [256, S], BF16)
    kwT = kb.dram("kwT", [256, S], BF16)
    gsT = kb.dram("gsT", [128, S], F32)
    vsw = kb.dram("vsw", [S, 512], BF16)
    fm_dst = [(qT, i) for i in range(16)] + [(kcT, 0), (kcT, 1), (vcT, 0), (vcT, 1), (ksT, 0), (ksT, 1), (kwT, 0), (kwT, 1)]

    P = [kb.ps((128, 512), F32, "P%d" % i) for i in range(8)]
    onesb = kb.sb([128, 128], BF16, "onesb")
    kb.op("dve", lambda e: e.memset(onesb[:], 1.0), W=[onesb])
    gb_s = kb.sb([128, 1], F32, "gb_s")
    kb.dma("sp", gb_s[:], gbias[:], W=[gb_s])
    masks = kb.sb([128, 4, 512], BF16, "masks")
    kb.dma("sp", masks[:], dmask[:].rearrange("m p t -> p m t"), W=[masks])
    wmasks = kb.sb([128, 8, 512], BF16, "wmasks")
    kb.dma("sp", wmasks[:], wmask[:].rearrange("m p t -> p m t"), W=[wmasks])
    ident = kb.sb([128, 128], F32, "ident")
    kb.dma("sp", ident[:], ident_d[:], W=[ident])

    kb.push()
    xTb = kb.sb([128, 32, 2048], BF16, "xTb")
    wring = Ring(kb, 6, [128, 32, 128], BF16, "wA")
    stg = Ring(kb, 3, [128, 512], BF16, "stgA")
    stgf = Ring(kb, 2, [128, 512], F32, "stgAf")
    pi = 0
    for st in range(2):
        for q4 in range(4):
            c0 = st * 2048 + q4 * 512
            kb.dma("pool", xTb[:, :, q4 * 512:(q4 + 1) * 512],
                   xT[:, c0:c0 + 512].rearrange("(k p) t -> p k t", p=128), W=[xTb])
        for wi in range(25):
            w = wring.next()
            kb.dma("pool", w[:], wA[wi], W=[w])
            for q4 in range(4):
                acc = P[pi % 4]
                pi += 1
                for kc in range(32):
                    kb.mm(acc, acc[:], w, w[:, kc, :], xTb, xTb[:, kc, q4 * 512:(q4 + 1) * 512], kc == 0, kc == 31)
                c0 = st * 2048 + q4 * 512
                if wi < 24:
                    dst, ci = fm_dst[wi]
                    s = stg.next()
                    kb.op("act", lambda e: e.copy(out=s[:], in_=acc[:]), R=[acc], W=[s])
                    kb.dma("sp", dst[ci * 128:(ci + 1) * 128, c0:c0 + 512], s[:], R=[s], W=[dst])
                else:
                    s = stgf.next()
                    kb.op("act", lambda e: e.activation(out=s[:], in_=acc[:], func=AF.Sigmoid, bias=gb_s[:, 0:1]),
                          R=[acc, gb_s], W=[s])
                    kb.dma("sp", gsT[:, c0:c0 + 512], s[:], R=[s], W=[gsT])
        ws = []
        for j in range(4):
            w = wring.next()
            kb.dma("pool", w[:], wA[25 + j], W=[w])
            ws.append(w)
        for tc in range(16):
            acc = P[pi % 4]
            pi += 1
            for j in range(4):
                for kc in range(32):
                    kb.mm(acc, acc[:, j * 128:(j + 1) * 128], xTb, xTb[:, kc, tc * 128:(tc + 1) * 128],
                          ws[j], ws[j][:, kc, :], kc == 0, kc == 31)
            s = stg.next()
            kb.op("act", lambda e: e.copy(out=s[:], in_=acc[:]), R=[acc], W=[s])
            t0 = st * 2048 + tc * 128
            kb.dma("sp", vsw[t0:t0 + 128, :], s[:], R=[s], W=[vsw])
    kb.pop()
    if stop_after == "A":
        kb.finish()
        return nc

    kcmpT = kb.sb([128, 2, 256], BF16, "kcmpT")
    vcmp = kb.sb([128, 2, 2, 128], BF16, "vcmp")
    kb.push()
    w1s = [kb.sb([128, 32, 256], BF16, "w1k"), kb.sb([128, 32, 256], BF16, "w1v")]
    w2s = [kb.sb([128, 2, 128], BF16, "w2k"), kb.sb([128, 2, 128], BF16, "w2v")]
    pes = kb.sb([128, 2, 32], BF16, "pes")
    kb.dma("pool", w1s[0][:], w1k[:], W=[w1s[0]])
    kb.dma("pool", w1s[1][:], w1v[:], W=[w1s[1]])
    kb.dma("pool", w2s[0][:], w2k[:], W=[w2s[0]])
    kb.dma("pool", w2s[1][:], w2v[:], W=[w2s[1]])
    kb.dma("pool", pes[:], peT[:], W=[pes])
    src_r = Ring(kb, 2, [128, S], BF16, "cmpsrc")
    c0s = kb.sb([128, 4], F32, "c0s")
    u_r = Ring(kb, 2, [128, 256], F32, "u")
    t_r = Ring(kb, 2, [128, 256], F32, "t")
    hid = kb.sb([128, 2, 256], BF16, "hidc")
    pi = 0
    for kv in range(2):
        for hc in range(2):
            acc = P[4]
            for l in range(32):
                kb.mm(acc, acc[:, 0:1], w1s[kv], w1s[kv][:, l, hc * 128:(hc + 1) * 128], pes, pes[:, kv, l:l + 1], l == 0, l == 31)
            kb.op("dve", lambda e: e.tensor_copy(out=c0s[:, kv * 2 + hc:kv * 2 + hc + 1], in_=acc[:, 0:1]), R=[acc], W=[c0s])
        for gl in range(2):
            src = src_r.next()
            sd = kcT if kv == 0 else vcT
            kb.dma("sp", src[:], sd[gl * 128:(gl + 1) * 128, :], R=[sd], W=[src])
            for hc in range(2):
                acc = P[pi % 4]
                pi += 1
                for l in range(32):
                    kb.mm(acc, acc[:, 0:255], w1s[kv], w1s[kv][:, l, hc * 128:(hc + 1) * 128], src, src[:, l:l + 16 * 254 + 1:16],
                          l == 0, l == 31)
                u = u_r.next()
                t = t_r.next()
                ci = kv * 2 + hc
                kb.op("dve", lambda e: e.tensor_scalar(out=u[:, 0:255], in0=acc[:, 0:255], scalar1=c0s[:, ci:ci + 1], scalar2=None,
                                                       op0=ALU.add), R=[acc, c0s], W=[u])
                kb.op("dve", lambda e: e.tensor_tensor(out=t[:, 0:255], in0=u[:, 0:255], in1=u[:, 0:255], op=ALU.mult), R=[u], W=[t])
                kb.op("dve", lambda e: e.tensor_scalar(out=t[:, 0:255], in0=t[:, 0:255], scalar1=0.044715, scalar2=1.0,
                                                       op0=ALU.mult, op1=ALU.add), R=[t], W=[t])
                kb.op("dve", lambda e: e.tensor_tensor(out=t[:, 0:255], in0=t[:, 0:255], in1=u[:, 0:255], op=ALU.mult), R=[t, u], W=[t])
                kb.op("act", lambda e: e.activation(out=t[:, 0:255], in_=t[:, 0:255], func=AF.Sigmoid, scale=GELU_C), R=[t], W=[t])
                kb.op("dve", lambda e: e.tensor_tensor(out=hid[:, hc, 0:255], in0=t[:, 0:255], in1=u[:, 0:255], op=ALU.mult),
                      R=[t, u], W=[hid])
            if kv == 0:
                acc = P[pi % 4]
                pi += 1
                for hc in range(2):
                    kb.mm(acc, acc[:, 0:255], w2s[0], w2s[0][:, hc, :], hid, hid[:, hc, 0:255], hc == 0, hc == 1)
                kb.op("act", lambda e: e.copy(out=kcmpT[:, gl, 0:255], in_=acc[:, 0:255]), R=[acc], W=[kcmpT])
            else:
                for ch in range(2):
                    nn = 128 if ch == 0 else 127
                    acc = P[pi % 4]
                    pi += 1
                    for hc in range(2):
                        kb.mm(acc, acc[0:nn, 0:128], hid, hid[:, hc, ch * 128:ch * 128 + nn], w2s[1], w2s[1][:, hc, :], hc == 0, hc == 1)
                    kb.op("act", lambda e: e.copy(out=vcmp[0:nn, gl, ch, :], in_=acc[0:nn, 0:128]), R=[acc], W=[vcmp])
    kb.pop()
    if stop_after == "B":
        kb.finish()
        return nc

    kb.push()
    zrow = kb.sb([1, 256], BF16, "zrow")
    kb.op("dve", lambda e: e.memset(zrow[:], 0.0), W=[zrow])
    selms = kb.sb([48, 48, 128], F32, "selms")
    kb.dma("sp", selms[:], selm[:], W=[selms])
    ovls = kb.sb([128, 2, 64], BF16, "ovls")
    kb.dma("sp", ovls[:], ovl[:].rearrange("(c p) s -> p c s", p=128), W=[ovls])
    Es = kb.sb([64, S], BF16, "Es")
    kb.dma("sp", Es[:], Emat[:], W=[Es])
    ksS = kb.sb([128, S], BF16, "ksS")
    kwS = kb.sb([128, S], BF16, "kwS")
    vsS = kb.sb([128, 32, 128], BF16, "vsS")
    vwS = kb.sb([128, 32, 128], BF16, "vwS")
    qS = kb.sb([128, 8, 512], BF16, "qS")
    gs = kb.sb([48, 512], F32, "gs")
    cm_r = Ring(kb, 2, [128, 2, 512], BF16, "cm")
    pT_r = Ring(kb, 4, [128, 512], BF16, "pT")
    pn_r = Ring(kb, 2, [128, 512], BF16, "pn")
    rr_r = Ring(kb, 3, [128, 512], F32, "rr")
    grep_r = Ring(kb, 3, [128, 512], F32, "grep")
    outacc = kb.sb([128, 8, 512], F32, "outacc")
    tmpo = Ring(kb, 2, [128, 512], F32, "tmpo")
    ho_r = Ring(kb, 2, [128, 512], BF16, "ho")
    imp = kb.sb([128, 64], F32, "imp")
    km_r = Ring(kb, 2, [128, 64], F32, "km")
    am_r = Ring(kb, 2, [128, 64], F32, "am")
    cmpb = kb.sb([128, 64, 64], BF16, "cmpb")
    rank = kb.sb([128, 64], F32, "rank")
    selT = kb.sb([64, 512], BF16, "selT")
    smask = kb.sb([128, 32, 512], BF16, "smask")
    si = 0
    mi = 0

    def gate_rep(br, gl, hg):
        nonlocal si
        r = br * 16 + gl * 8 + hg
        ps = P[si % 3]
        si += 1
        kb.mm(ps, ps[:], selms, selms[:, r, :], gs, gs[:], True, True)
        return ps

    for gl in range(2):
        kb.dma("sp", ksS[:], ksT[gl * 128:(gl + 1) * 128, :], R=[ksT], W=[ksS])
        kb.dma("sp", kwS[:], kwT[gl * 128:(gl + 1) * 128, :], R=[kwT], W=[kwS])
        kb.dma("sp", vsS[:], vsw[:, gl * 128:(gl + 1) * 128].rearrange("(c p) d -> p c d", p=128), R=[vsw], W=[vsS])
        kb.dma("sp", vwS[:], vsw[:, 256 + gl * 128:256 + (gl + 1) * 128].rearrange("(c p) d -> p c d", p=128), R=[vsw], W=[vwS])
        for j in range(8):
            ts = slice(j * 512, (j + 1) * 512)
            kb.dma("sp", qS[:], qT[gl * 1024:(gl + 1) * 1024, ts].rearrange("(c p) t -> p c t", p=128), R=[qT], W=[qS])
            kb.dma("sp", gs[:], gsT[0:48, ts], R=[gsT], W=[gs])
            nb = min(255, 32 * j + 31)
            chunks = [(0, min(nb, 128))] + ([(128, nb - 128)] if nb > 128 else [])
            cm = cm_r.next()
            for ci, (n0, nn) in enumerate(chunks):
                kb.dma("sp", cm[0:nn, ci, :], cmaskT[n0:n0 + nn, ts], W=[cm])
            impP = P[7]
            kb.mm(impP, impP[:, 0:256], zrow, zrow[0:1, 0:128], zrow, zrow[0:1, 0:256], True, False)
            for hg in range(8):
                O = P[3 + (hg % 2)]
                L = P[5 + (hg % 2)]
                pts = []
                for ci, (n0, nn) in enumerate(chunks):
                    sps = P[si % 3]
                    si += 1
                    kb.mm(sps, sps[0:nn, :], kcmpT, kcmpT[:, gl, n0:n0 + nn], qS, qS[:, hg, :], True, True)
                    pT = pT_r.next()
                    kb.op("act", lambda e: e.activation(out=pT[0:nn, :], in_=sps[0:nn, :], func=AF.Exp, scale=NSA_SCALE), R=[sps], W=[pT])
                    kb.op("pool", lambda e: e.tensor_tensor(out=pT[0:nn, :], in0=pT[0:nn, :], in1=cm[0:nn, ci, :], op=ALU.mult),
                          R=[pT, cm], W=[pT])
                    kb.mm(O, O[:], vcmp, vcmp[0:nn, gl, ci, :], pT, pT[0:nn, :], ci == 0, ci == len(chunks) - 1)
                    kb.mm(L, L[:], onesb, onesb[0:nn, :], pT, pT[0:nn, :], ci == 0, ci == len(chunks) - 1)
                    pts.append(pT)
                rr = rr_r.next()
                kb.op("dve", lambda e: e.tensor_scalar(out=rr[:], in0=L[:], scalar1=1e-30, scalar2=None, op0=ALU.max), R=[L], W=[rr])
                kb.op("dve", lambda e: e.reciprocal(out=rr[:], in_=rr[:]), R=[rr], W=[rr])
                for ci, (n0, nn) in enumerate(chunks):
                    pn = pn_r.next()
                    kb.op("dve", lambda e: e.tensor_tensor(out=pn[0:nn, :], in0=pts[ci][0:nn, :], in1=rr[0:nn, :], op=ALU.mult),
                          R=[pts[ci], rr], W=[pn])
                    for qi in range(4):
                        first = (hg == 0 and ci == 0)
                        last = (hg == 7 and ci == len(chunks) - 1)
                        kb.mm(impP, impP[:, qi * 64:(qi + 1) * 64], pn, pn[0:nn, qi * 128:(qi + 1) * 128], ovls, ovls[0:nn, ci, :],
                              False, last)
                gp = gate_rep(0, gl, hg)
                gr = grep_r.next()
                kb.op("dve", lambda e: e.tensor_tensor(out=gr[:], in0=gp[:], in1=rr[:], op=ALU.mult), R=[gp, rr], W=[gr])
                kb.op("dve", lambda e: e.tensor_tensor(out=outacc[:, hg, :], in0=O[:], in1=gr[:], op=ALU.mult), R=[O, gr], W=[outacc])
            for qi in range(4):
                q0 = j * 512 + qi * 128
                km = km_r.next()
                am = am_r.next()
                kb.dma("sp", km[:], KM[q0:q0 + 128, :], W=[km])
                kb.dma("sp", am[:], AM[q0:q0 + 128, :], W=[am])
                kb.op("dve", lambda e: e.tensor_tensor(out=imp[:], in0=impP[:, qi * 64:(qi + 1) * 64], in1=km[:], op=ALU.mult),
                      R=[impP, km], W=[imp])
                kb.op("dve", lambda e: e.tensor_tensor(out=imp[:], in0=imp[:], in1=am[:], op=ALU.add), R=[imp, am], W=[imp])
                kb.op("dve", lambda e: e.tensor_tensor(out=cmpb[:], in0=imp[:, :].unsqueeze(1).to_broadcast([128, 64, 64]),
                                                       in1=imp[:, :].unsqueeze(2).to_broadcast([128, 64, 64]), op=ALU.is_gt),
                      R=[imp], W=[cmpb])
                kb.op("dve", lambda e: e.tensor_reduce(out=rank[:], in_=cmpb[:], axis=AX.X, op=ALU.add), R=[cmpb], W=[rank])
                kb.op("dve", lambda e: e.tensor_scalar(out=rank[:], in0=rank[:], scalar1=15.5, scalar2=None, op0=ALU.is_lt),
                      R=[rank], W=[rank])
                tp = P[si % 3]
                si += 1
                kb.op("pe", lambda e: e.transpose(out=tp[0:64, 0:128], in_=rank[:, :], identity=ident[:]), R=[rank, ident], W=[tp])
                kb.op("act", lambda e: e.copy(out=selT[:, qi * 128:(qi + 1) * 128], in_=tp[0:64, 0:128]), R=[tp], W=[selT])
            nk = 4 * (j + 1)
            for kc in range(nk):
                ps = P[si % 3]
                si += 1
                kb.mm(ps, ps[:], Es, Es[:, kc * 128:(kc + 1) * 128], selT, selT[:], True, True)
                m = kc - 4 * j
                if m >= 0:
                    kb.op("dve", lambda e: e.tensor_tensor(out=smask[:, kc, :], in0=ps[:], in1=masks[:, m, :], op=ALU.mult),
                          R=[ps, masks], W=[smask])
                else:
                    kb.op("act", lambda e: e.copy(out=smask[:, kc, :], in_=ps[:]), R=[ps], W=[smask])
            for hg in range(8):
                O = P[3 + (hg % 2)]
                L = P[5 + (hg % 2)]
                for kc in range(nk):
                    sps = P[si % 3]
                    si += 1
                    ks = slice(kc * 128, (kc + 1) * 128)
                    kb.mm(sps, sps[:], ksS, ksS[:, ks], qS, qS[:, hg, :], True, True)
                    pT = pT_r.next()
                    kb.op("act", lambda e: e.activation(out=pT[:], in_=sps[:], func=AF.Exp, scale=NSA_SCALE), R=[sps], W=[pT])
                    eng = "pool" if (mi % 2 == 0) else "dve"
                    mi += 1
                    kb.op(eng, lambda e: e.tensor_tensor(out=pT[:], in0=pT[:], in1=smask[:, kc, :], op=ALU.mult), R=[pT, smask], W=[pT])
                    kb.mm(O, O[:], vsS, vsS[:, kc, :], pT, pT[:], kc == 0, kc == nk - 1)
                    kb.mm(L, L[:], onesb, onesb[:], pT, pT[:], kc == 0, kc == nk - 1)
                rr = rr_r.next()
                kb.op("dve", lambda e: e.reciprocal(out=rr[:], in_=L[:]), R=[L], W=[rr])
                gp = gate_rep(1, gl, hg)
                gr = grep_r.next()
                kb.op("dve", lambda e: e.tensor_tensor(out=gr[:], in0=gp[:], in1=rr[:], op=ALU.mult), R=[gp, rr], W=[gr])
                to = tmpo.next()
                kb.op("dve", lambda e: e.tensor_tensor(out=to[:], in0=O[:], in1=gr[:], op=ALU.mult), R=[O, gr], W=[to])
                kb.op("pool", lambda e: e.tensor_tensor(out=outacc[:, hg, :], in0=outacc[:, hg, :], in1=to[:], op=ALU.add),
                      R=[outacc, to], W=[outacc])
                O2 = P[3 + ((hg + 1) % 2)]
                L2 = P[5 + ((hg + 1) % 2)]
                wch = [c for c in range(8) if 4 * j - 4 + c >= 0]
                for wi_, c in enumerate(wch):
                    kc = 4 * j - 4 + c
                    sps = P[si % 3]
                    si += 1
                    ks = slice(kc * 128, (kc + 1) * 128)
                    kb.mm(sps, sps[:], kwS, kwS[:, ks], qS, qS[:, hg, :], True, True)
                    pT = pT_r.next()
                    kb.op("act", lambda e: e.activation(out=pT[:], in_=sps[:], func=AF.Exp, scale=NSA_SCALE), R=[sps], W=[pT])
                    eng = "pool" if (mi % 2 == 0) else "dve"
                    mi += 1
                    kb.op(eng, lambda e: e.tensor_tensor(out=pT[:], in0=pT[:], in1=wmasks[:, c, :], op=ALU.mult), R=[pT, wmasks], W=[pT])
                    kb.mm(O2, O2[:], vwS, vwS[:, kc, :], pT, pT[:], wi_ == 0, wi_ == len(wch) - 1)
                    kb.mm(L2, L2[:], onesb, onesb[:], pT, pT[:], wi_ == 0, wi_ == len(wch) - 1)
                rr = rr_r.next()
                kb.op("dve", lambda e: e.reciprocal(out=rr[:], in_=L2[:]), R=[L2], W=[rr])
                gp = gate_rep(2, gl, hg)
                gr = grep_r.next()
                kb.op("dve", lambda e: e.tensor_tensor(out=gr[:], in0=gp[:], in1=rr[:], op=ALU.mult), R=[gp, rr], W=[gr])
                to = tmpo.next()
                kb.op("dve", lambda e: e.tensor_tensor(out=to[:], in0=O2[:], in1=gr[:], op=ALU.mult), R=[O2, gr], W=[to])
                ho = ho_r.next()
                kb.op("pool", lambda e: e.tensor_tensor(out=ho[:], in0=outacc[:, hg, :], in1=to[:], op=ALU.add),
                      R=[outacc, to], W=[ho])
                r0 = (gl * 8 + hg) * 128
                kb.dma("sp", hTo[r0:r0 + 128, ts], ho[:], R=[ho], W=[hTo])
    kb.pop()
    kb.finish()
    return nc


C_OFF = {"q": 0, "kc": 4096, "vc": 4608, "ks": 5120, "vs": 5632, "kw": 6144, "vw": 6656, "gates": 7168}


def prep_M1_consts():
    S = SEQ
    n = np.arange(256)
    q = np.arange(S)
    cmaskT = ((16 * n[:, None] + 31) <= q[None, :]) & (n[:, None] < 255)
    cmp_start = np.arange(255) * 16
    sel_start = np.arange(64) * 64
    ov = np.zeros((256, 64), np.float32)
    ov[:255] = ((cmp_start[:, None] < sel_start[None, :] + 64) & (cmp_start[:, None] + 32 > sel_start[None, :]))
    cur = q // 64
    sb = np.arange(64)
    forced = (sb[None, :] == 0) | (sb[None, :] == cur[:, None]) | (sb[None, :] == cur[:, None] - 1)
    valid = sb[None, :] <= cur[:, None]
    KM = (valid & ~forced).astype(np.float32)
    AM = np.where(valid, np.where(forced, np.float32(1e9), np.float32(0.0)), np.float32(-1e30)).astype(np.float32)
    E = (np.arange(S)[None, :] // 64 == sb[:, None])
    kl = np.arange(128)[:, None]
    ql = np.arange(512)[None, :]
    dmask = np.stack([(128 * m + kl <= ql) for m in range(4)])
    wmask = np.stack([((128 * c + kl - 512 <= ql) & (128 * c + kl > ql)) for c in range(8)])
    selm = np.zeros((48, 48, 128), np.float32)
    for r in range(48):
        selm[r, r, :] = 1.0
    bf = lambda a: a.astype(np.float32).astype(ml_dtypes.bfloat16)
    return {"cmaskT": bf(cmaskT), "ovl": bf(ov), "KM": KM, "AM": AM, "Emat": bf(E), "dmask": bf(dmask), "wmask": bf(wmask),
            "selm": selm, "ident": np.eye(128, dtype=np.float32)}


def prep_M1_weights(gh, w_in, b_gate, pe_k, pe_v, w1k, w2k, w1v, w2v):
    o = C_OFF
    cols = [w_in[:, o["q"] + gh * 2048:o["q"] + (gh + 1) * 2048]]
    for nm in ("kc", "vc", "ks", "kw"):
        cols.append(w_in[:, o[nm] + gh * 256:o[nm] + (gh + 1) * 256])
    gidx = [o["gates"] + br * 32 + (2 * gh + gl) * 8 + hg for br in range(3) for gl in range(2) for hg in range(8)]
    gcols = np.zeros((D, 128), np.float32)
    gcols[:, 0:48] = w_in[:, gidx]
    cols.append(gcols)
    for nm in ("vs", "vw"):
        cols.append(w_in[:, o[nm] + gh * 256:o[nm] + (gh + 1) * 256])
    wA = tile_kxn(np.concatenate(cols, 1))
    assert wA.shape[0] == M1_NW
    gb = np.zeros((128, 1), np.float32)
    gb[0:48, 0] = b_gate[[i - o["gates"] for i in gidx]]
    lay1 = lambda w: np.ascontiguousarray(w.reshape(32, 128, 256).transpose(1, 0, 2))
    lay2 = lambda w: np.ascontiguousarray(w.reshape(2, 128, 128).transpose(1, 0, 2))
    peT = np.ascontiguousarray(np.stack([pe_k.T, pe_v.T], axis=1))
    return {"wA": wA, "w1k": lay1(w1k), "w1v": lay1(w1v), "w2k": lay2(w2k), "w2v": lay2(w2v), "peT": peT, "gbias": gb}


def run_M1(xT_list, wlist, consts, stop_after=None):
    if "M1" not in _CACHE:
        _CACHE["M1"] = build_M1(stop_after)
    nc = _CACHE["M1"]
    in_maps = []
    for c in range(8):
        m = {"xT": xT_list[c]}
        m.update(wlist[c])
        m.update(consts)
        in_maps.append(m)
    res = run_bass_kernel_spmd(nc, in_maps, core_ids=list(range(8)))
    return [r["hTo"] for r in res.results]


def kernel(x, ab_w_in, ab_b_igate, ab_b_fgate, ab_mlstm_norm, ab_q_norm, ab_kv_norm, ab_w_uq, ab_w_ukv,
           ab_w_o, c_w_in, c_b_gate, c_pe_k, c_pe_v, c_cmp_w1_k, c_cmp_w2_k, c_cmp_w1_v, c_cmp_w2_v, c_w_o,
           ffn_w_gate, ffn_w_up, ffn_w_down, ln_mix_g, ln_mix_b, ln_ffn_g, ln_ffn_b):
    f = lambda a: np.asarray(a, dtype=np.float32)
    x = f(x)
    B = x.shape[0]
    xT = [np.ascontiguousarray(x[b].T) for b in range(B)]
    halves = lambda a: [np.ascontiguousarray(a[:, 0:2048]), np.ascontiguousarray(a[:, 2048:4096])]

    c0 = prep_M0_consts()
    w0 = [prep_M0_weights(hh, f(ab_w_in)[0], f(ab_b_igate)[0], f(ab_b_fgate)[0], f(ab_mlstm_norm)[0], f(ab_q_norm)[0],
                          f(ab_kv_norm)[0], f(ab_w_uq)[0], f(ab_w_ukv)[0]) for hh in range(2)]
    o = run_M0([xT[c // 2] for c in range(8)], [w0[c % 2] for c in range(8)], c0)
    del w0
    hl, xl = [], []
    for b in range(B):
        a0, a1 = np.asarray(o[2 * b]), np.asarray(o[2 * b + 1])
        hT = np.concatenate([a0[:1024], a1[:1024], a0[1024:], a1[1024:]], 0)
        hl += halves(hT)
        xl += halves(xT[b])
    lnv = [f(ln_mix_g)[0], f(ln_mix_b)[0], f(ln_ffn_g)[0], f(ln_ffn_b)[0]]
    o = run_F(hl, xl, f(ab_w_o)[0], f(ffn_w_gate)[0], f(ffn_w_up)[0], f(ffn_w_down)[0], lnv)
    x1T = [np.concatenate([np.asarray(o[2 * b]), np.asarray(o[2 * b + 1])], 1) for b in range(B)]

    c1 = prep_M1_consts()
    w1 = [prep_M1_weights(gh, f(c_w_in)[0], f(c_b_gate)[0], f(c_pe_k)[0], f(c_pe_v)[0], f(c_cmp_w1_k)[0], f(c_cmp_w2_k)[0],
                          f(c_cmp_w1_v)[0], f(c_cmp_w2_v)[0]) for gh in range(2)]
    o = run_M1([x1T[c // 2] for c in range(8)], [w1[c % 2] for c in range(8)], c1)
    del w1
    hl, xl = [], []
    for b in range(B):
        hT = np.concatenate([np.asarray(o[2 * b]), np.asarray(o[2 * b + 1])], 0)
        hl += halves(hT)
        xl += halves(x1T[b])
    lnv = [f(ln_mix_g)[1], f(ln_mix_b)[1], f(ln_ffn_g)[1], f(ln_ffn_b)[1]]
    o = run_F(hl, xl, f(c_w_o)[0], f(ffn_w_gate)[1], f(ffn_w_up)[1], f(ffn_w_down)[1], lnv)
    out = np.empty((B, SEQ, D), np.float32)
    for b in range(B):
        out[b, 0:2048] = np.asarray(o[2 * b]).T
        out[b, 2048:4096] = np.asarray(o[2 * b + 1]).T
    return out
```

```python
import math
from contextlib import ExitStack

import numpy as np
import ml_dtypes
import concourse.bass as bass
import concourse.mybir as mybir
from concourse.bass_utils import run_bass_kernel_spmd

F32 = mybir.dt.float32
BF16 = mybir.dt.bfloat16
AF = mybir.ActivationFunctionType
ALU = mybir.AluOpType
AX = mybir.AxisListType

D = 4096
SEQ = 4096
FF = 11008
NFC = FF // 128
ALPHA = 4.0 ** 0.25
LN_EPS = 1e-5
RMS_EPS = 1e-6
NDS = 28


class Tk:
    __slots__ = ("h", "lw", "rd", "name")

    def __init__(self, h, name):
        self.h = h
        self.lw = None
        self.rd = {}
        self.name = name

    def __getitem__(self, idx):
        return self.h[idx]


class KB:
    def __init__(self, nc):
        self.nc = nc
        self.es = ExitStack()
        self.es_root = self.es
        self.eng = {"pe": nc.tensor, "act": nc.scalar, "dve": nc.vector, "pool": nc.gpsimd, "sp": nc.sync}
        self.sem = {}
        self.cnt = {}
        self.known = {}
        for k in self.eng:
            self.sem[k] = self.es.enter_context(nc.semaphore("s_" + k))
            self.cnt[k] = 0
            self.known[k] = {}
        self.dsem = [self.es.enter_context(nc.semaphore("d%d" % i)) for i in range(NDS)]
        self.dcnt = [0] * NDS
        self.dnext = 0
        self.nid = 0

    def sb(self, shape, dt, name=None):
        self.nid += 1
        name = "%s_s%d" % (name or "sb", self.nid)
        return Tk(self.es.enter_context(self.nc.sbuf_tensor(name, list(shape), dt)), name)

    def ps(self, shape=(128, 512), dt=F32, name=None):
        self.nid += 1
        name = "%s_p%d" % (name or "ps", self.nid)
        return Tk(self.es.enter_context(self.nc.psum_tensor(name, list(shape), dt)), name)

    def dram(self, name, shape, dt, kind="Internal"):
        name = getattr(self, "pfx", "") + name
        return Tk(self.nc.dram_tensor(name, list(shape), dt, kind=kind).ap(), name)

    def coll_allreduce(self, in_tk, in_ap, out_tk, out_ap):
        if not hasattr(self, "csem"):
            self.csem = self.es_root.enter_context(self.nc.semaphore("s_cc"))
            self.ccnt = 0
        deps = self._deps("pool", [in_tk], [out_tk], True)
        self._emit_waits("pool", deps)
        ins = self.eng["pool"].collective_compute("AllReduce", ALU.add, replica_groups=[[0, 1], [2, 3], [4, 5], [6, 7]],
                                                  ins=[in_ap], outs=[out_ap])
        self.ccnt += 1
        ins.then_inc(self.csem)
        key = "cc"
        if in_tk.rd.get(key, 0) < self.ccnt:
            in_tk.rd[key] = self.ccnt
        out_tk.lw = (key, self.ccnt)
        out_tk.rd = {}

    def barrier(self):
        for e, eng in self.eng.items():
            kn = self.known[e]
            for k in self.eng:
                if k != e and kn.get(k, 0) < self.cnt[k]:
                    eng.wait_ge(self.sem[k], self.cnt[k])
                    kn[k] = self.cnt[k]
            for i in range(NDS):
                key = ("d", i)
                if kn.get(key, 0) < self.dcnt[i]:
                    eng.wait_ge(self.dsem[i], self.dcnt[i])
                    kn[key] = self.dcnt[i]
            if getattr(self, "ccnt", 0) > kn.get("cc", 0):
                eng.wait_ge(self.csem, self.ccnt)
                kn["cc"] = self.ccnt

    def push(self):
        self._outer = getattr(self, "_outer", [])
        self._outer.append(self.es)
        self.es = ExitStack()

    def pop(self):
        self.barrier()
        self.es.close()
        self.es = self._outer.pop()

    def semof(self, key):
        if isinstance(key, tuple):
            return self.dsem[key[1]]
        if key == "cc":
            return self.csem
        return self.sem[key]

    def _deps(self, e, R, W, is_dma):
        deps = {}

        def add(tok, same_ok):
            if tok is None:
                return
            key, val = tok
            if key == e and not same_ok and not is_dma:
                return
            if deps.get(key, 0) < val:
                deps[key] = val

        for t in R:
            add(t.lw, e != "pe")
        for t in W:
            add(t.lw, False)
            for k, v in t.rd.items():
                add((k, v), False)
        return deps

    def _emit_waits(self, e, deps):
        eng = self.eng[e]
        kn = self.known[e]
        for key, val in deps.items():
            if kn.get(key, 0) >= val:
                continue
            eng.wait_ge(self.semof(key), val)
            kn[key] = val

    def op(self, e, fn, R=(), W=(), sig=True):
        deps = self._deps(e, R, W, False)
        self._emit_waits(e, deps)
        ins = fn(self.eng[e])
        if sig:
            self.cnt[e] += 1
            ins.then_inc(self.sem[e], 1)
            tv = self.cnt[e]
        else:
            tv = self.cnt[e] + 1
        for t in R:
            if t.rd.get(e, 0) < tv:
                t.rd[e] = tv
        for t in W:
            t.lw = (e, tv)
            t.rd = {}

    def dma(self, q, out, in_, R=(), W=(), transpose=False):
        si = self.dnext
        self.dnext = (self.dnext + 1) % NDS
        key = ("d", si)
        deps = self._deps(q, R, W, True)
        if self.dcnt[si] > 0:
            deps[key] = max(deps.get(key, 0), self.dcnt[si])
        self._emit_waits(q, deps)
        eng = self.eng[q]
        if transpose:
            ins = eng.dma_start_transpose(out=out, in_=in_)
        else:
            ins = eng.dma_start(out=out, in_=in_)
        self.dcnt[si] += 16
        ins.then_inc(self.dsem[si], 16)
        tv = self.dcnt[si]
        for t in R:
            if t.rd.get(key, 0) < tv:
                t.rd[key] = tv
        for t in W:
            t.lw = (key, tv)
            t.rd = {}

    def finish(self):
        eng = self.eng["sp"]
        for i in range(NDS):
            if self.dcnt[i] > 0:
                eng.wait_ge(self.dsem[i], self.dcnt[i])
        for k in self.eng:
            if k != "sp" and self.cnt[k] > 0:
                eng.wait_ge(self.sem[k], self.cnt[k])
        if getattr(self, "ccnt", 0) > 0:
            eng.wait_ge(self.csem, self.ccnt)
        self.es.close()

    def mm(self, out_t, out_ap, lhsT_t, lhsT_ap, rhs_t, rhs_ap, start, stop, sig=None):
        if sig is None:
            sig = True
        self.op("pe", lambda e: e.matmul(out_ap, lhsT=lhsT_ap, rhs=rhs_ap, start=start, stop=stop),
                R=[lhsT_t, rhs_t], W=[out_t], sig=sig)


def pipelined(n, emit_s, emit_rest, look=2):
    pend = {}
    for i in range(min(look, n)):
        pend[i] = emit_s(i)
    for i in range(n):
        emit_rest(i, pend.pop(i))
        if i + look < n:
            pend[i + look] = emit_s(i + look)


class Ring:
    def __init__(self, kb, n, shape, dt, name):
        self.bufs = [kb.sb(shape, dt, "%s%d" % (name, i)) for i in range(n)]
        self.i = 0

    def next(self):
        b = self.bufs[self.i]
        self.i = (self.i + 1) % len(self.bufs)
        return b


class PRing:
    def __init__(self, kb, n, name):
        self.bufs = [kb.ps((128, 512), F32, "%s%d" % (name, i)) for i in range(n)]
        self.i = 0

    def next(self):
        b = self.bufs[self.i]
        self.i = (self.i + 1) % len(self.bufs)
        return b


def tile_kxn(w, ncol=128):
    K, N = w.shape
    return np.ascontiguousarray(w.reshape(K // 128, 128, N // ncol, ncol).transpose(2, 1, 0, 3))


def ln_pack(vecs):
    return np.ascontiguousarray(np.stack([v.reshape(32, 128).T for v in vecs], axis=1))


_CACHE = {}


M0_NW = 41
MLA_SCALE = 192.0 ** -0.5


def emit_M0(kb, P, xT, hTo):
    S = SEQ
    kb.pfx = "m0_"
    kb.push()
    wA = kb.dram("wA", [M0_NW, 128, 32, 128], F32, "ExternalInput")
    wuq = kb.dram("wuq", [128, 8, 2048], F32, "ExternalInput")
    wukv_k = kb.dram("wukv_k", [128, 4, 1024], F32, "ExternalInput")
    wukv_v = kb.dram("wukv_v", [128, 4, 1024], F32, "ExternalInput")
    smallp = kb.dram("smallp", [128, 24], F32, "ExternalInput")
    cosT = kb.dram("cosT", [64, S], F32, "ExternalInput")
    sinT = kb.dram("sinT", [64, S], F32, "ExternalInput")
    dmask = kb.dram("dmask", [4, 128, 512], BF16, "ExternalInput")
    ident_d = kb.dram("ident", [128, 128], F32, "ExternalInput")
    qaT = kb.dram("qaT", [512, S], BF16)
    kaT = kb.dram("kaT", [512, S], BF16)
    ogT = kb.dram("ogT", [1024, S], BF16)
    cqT = kb.dram("cqT", [1024, S], BF16)
    ckvT = kb.dram("ckvT", [512, S], BF16)
    krT = kb.dram("krT", [128, S], BF16)
    va = kb.dram("va", [S, 1024], BF16)
    grep = kb.dram("grep", [4, 128, S], F32)
    qT = kb.dram("qT", [8, 192, S], BF16)
    knT = kb.dram("knT", [8, 128, S], BF16)
    kroT = kb.dram("kroT", [64, S], BF16)
    vmla = kb.dram("vmla", [S, 1024], BF16)
    fm_dst = [(qaT, i) for i in range(4)] + [(kaT, i) for i in range(4)] + [(ogT, i) for i in range(8)] + \
             [(cqT, i) for i in range(8)] + [(ckvT, i) for i in range(4)] + [(krT, 0)]

    onesf = kb.sb([128, 128], F32, "onesf")
    onesb = kb.sb([128, 128], BF16, "onesb")
    kb.op("dve", lambda e: e.memset(onesf[:], 1.0), W=[onesf])
    kb.op("dve", lambda e: e.memset(onesb[:], 1.0), W=[onesb])
    sp_ = kb.sb([128, 24], F32, "smallp_sb")
    kb.dma("sp", sp_[:], smallp[:], W=[sp_])
    masks = kb.sb([128, 4, 512], BF16, "masks")
    kb.dma("sp", masks[:], dmask[:].rearrange("m p t -> p m t"), W=[masks])
    ident = kb.sb([128, 128], F32, "ident_sb")
    kb.dma("sp", ident[:], ident_d[:], W=[ident])

    kb.push()
    xTb = kb.sb([128, 32, 2048], BF16, "xTb")
    wring = Ring(kb, 6, [128, 32, 128], BF16, "wA")
    stg = Ring(kb, 3, [128, 512], BF16, "stgA")
    stgf = Ring(kb, 2, [128, 512], F32, "stgAf")
    pi = 0
    for st in range(2):
        for q4 in range(4):
            c0 = st * 2048 + q4 * 512
            kb.dma("pool", xTb[:, :, q4 * 512:(q4 + 1) * 512],
                   xT[:, c0:c0 + 512].rearrange("(k p) t -> p k t", p=128), W=[xTb])
        for wi in list(range(29)) + list(range(37, 41)):
            w = wring.next()
            kb.dma("pool", w[:], wA[wi], W=[w])
            for q4 in range(4):
                acc = P[pi % 4]
                pi += 1
                for kc in range(32):
                    kb.mm(acc, acc[:], w, w[:, kc, :], xTb, xTb[:, kc, q4 * 512:(q4 + 1) * 512], kc == 0, kc == 31)
                c0 = st * 2048 + q4 * 512
                if wi < 29:
                    dst, ci = fm_dst[wi]
                    s = stg.next()
                    kb.op("act", lambda e: e.copy(out=s[:], in_=acc[:]), R=[acc], W=[s])
                    kb.dma("sp", dst[ci * 128:(ci + 1) * 128, c0:c0 + 512], s[:], R=[s], W=[dst])
                else:
                    s = stgf.next()
                    kb.op("act", lambda e: e.copy(out=s[:], in_=acc[:]), R=[acc], W=[s])
                    kb.dma("sp", grep[wi - 37, :, c0:c0 + 512], s[:], R=[s], W=[grep])
        for g2 in range(2):
            ws = []
            for j in range(4):
                w = wring.next()
                kb.dma("pool", w[:], wA[29 + g2 * 4 + j], W=[w])
                ws.append(w)
            for tc in range(16):
                acc = P[pi % 4]
                pi += 1
                for j in range(4):
                    for kc in range(32):
                        kb.mm(acc, acc[:, j * 128:(j + 1) * 128], xTb, xTb[:, kc, tc * 128:(tc + 1) * 128],
                              ws[j], ws[j][:, kc, :], kc == 0, kc == 31)
                s = stg.next()
                kb.op("act", lambda e: e.copy(out=s[:], in_=acc[:]), R=[acc], W=[s])
                t0 = st * 2048 + tc * 128
                kb.dma("sp", va[t0:t0 + 128, g2 * 512:(g2 + 1) * 512], s[:], R=[s], W=[va])
    kb.pop()

    kb.push()
    wuq_s = kb.sb([128, 8, 2048], BF16, "wuq")
    wk_s = kb.sb([128, 4, 1024], BF16, "wk")
    wv_s = kb.sb([128, 4, 1024], BF16, "wv")
    kb.dma("pool", wuq_s[:], wuq[:], W=[wuq_s])
    kb.dma("pool", wk_s[:], wukv_k[:], W=[wk_s])
    kb.dma("pool", wv_s[:], wukv_v[:], W=[wv_s])
    cq_r = Ring(kb, 2, [128, 8, 512], BF16, "cq")
    ckv_r = Ring(kb, 2, [128, 4, 512], BF16, "ckv")
    cqn = kb.sb([128, 8, 512], BF16, "cqn")
    ckvn = kb.sb([128, 4, 512], BF16, "ckvn")
    sq_r = Ring(kb, 2, [128, 512], F32, "sqB")
    rq = kb.sb([128, 512], F32, "rq")
    rkv = kb.sb([128, 512], F32, "rkv")
    cos_r = Ring(kb, 2, [64, 512], F32, "cos")
    sin_r = Ring(kb, 2, [64, 512], F32, "sin")
    krA_r = Ring(kb, 2, [64, 512], BF16, "krA")
    krB_r = Ring(kb, 2, [64, 512], BF16, "krB")
    t1_r = Ring(kb, 2, [64, 512], F32, "t1")
    t2_r = Ring(kb, 2, [64, 512], F32, "t2")
    stg = Ring(kb, 4, [128, 512], BF16, "stgB")
    pi = 0
    for tt in range(8):
        ts = slice(tt * 512, (tt + 1) * 512)
        cq = cq_r.next()
        kb.dma("sp", cq[:], cqT[:, ts].rearrange("(c p) t -> p c t", p=128), R=[cqT], W=[cq])
        ckv = ckv_r.next()
        kb.dma("sp", ckv[:], ckvT[:, ts].rearrange("(c p) t -> p c t", p=128), R=[ckvT], W=[ckv])
        cs = cos_r.next()
        sn = sin_r.next()
        kb.dma("sp", cs[:], cosT[:, ts], W=[cs])
        kb.dma("sp", sn[:], sinT[:, ts], W=[sn])
        for (src, nch, dstr, nf) in ((cq, 8, rq, 1024.0), (ckv, 4, rkv, 512.0)):
            acc = P[4]
            for c in range(nch):
                sq = sq_r.next()
                kb.op("act", lambda e: e.activation(out=sq[:], in_=src[:, c, :], func=AF.Square), R=[src], W=[sq])
                kb.mm(acc, acc[:], onesf, onesf[:], sq, sq[:], c == 0, c == nch - 1)
            kb.op("dve", lambda e: e.tensor_scalar(out=dstr[:], in0=acc[:], scalar1=1.0 / nf, scalar2=RMS_EPS,
                                                   op0=ALU.mult, op1=ALU.add), R=[acc], W=[dstr])
            kb.op("act", lambda e: e.activation(out=dstr[:], in_=dstr[:], func=AF.Ln), R=[dstr], W=[dstr])
            kb.op("act", lambda e: e.activation(out=dstr[:], in_=dstr[:], func=AF.Exp, scale=-0.5), R=[dstr], W=[dstr])
        for c in range(8):
            kb.op("pool", lambda e: e.tensor_scalar(out=cqn[:, c, :], in0=cq[:, c, :], scalar1=sp_[:, 12 + c:13 + c],
                                                    scalar2=None, op0=ALU.mult), R=[cq, sp_], W=[cqn])
        for c in range(4):
            kb.op("dve", lambda e: e.scalar_tensor_tensor(out=ckvn[:, c, :], in0=ckv[:, c, :], scalar=sp_[:, 20 + c:21 + c],
                                                          in1=rkv[:], op0=ALU.mult, op1=ALU.mult),
                  R=[ckv, sp_, rkv], W=[ckvn])
        krA = krA_r.next()
        krB = krB_r.next()
        kb.dma("sp", krA[:], krT[0:64, ts], R=[krT], W=[krA])
        kb.dma("sp", krB[:], krT[64:128, ts], R=[krT], W=[krB])
        t1 = t1_r.next()
        t2 = t2_r.next()
        kb.op("dve", lambda e: e.tensor_tensor(out=t1[:], in0=krA[:], in1=cs[:], op=ALU.mult), R=[krA, cs], W=[t1])
        kb.op("pool", lambda e: e.tensor_tensor(out=t2[:], in0=krB[:], in1=sn[:], op=ALU.mult), R=[krB, sn], W=[t2])
        s = stg.next()
        kb.op("dve", lambda e: e.tensor_tensor(out=s[0:64, :], in0=t1[:], in1=t2[:], op=ALU.add), R=[t1, t2], W=[s])
        kb.dma("sp", kroT[:, ts], s[0:64, :], R=[s], W=[kroT])
        for h in range(8):
            acc = P[pi % 4]
            pi += 1
            for c in range(8):
                kb.mm(acc, acc[:], wuq_s, wuq_s[:, c, h * 256:h * 256 + 128], cqn, cqn[:, c, :], c == 0, c == 7)
            s = stg.next()
            kb.op("dve", lambda e: e.tensor_tensor(out=s[:], in0=acc[:], in1=rq[:], op=ALU.mult), R=[acc, rq], W=[s])
            kb.dma("sp", qT[h, 0:128, ts], s[:], R=[s], W=[qT])
            a1 = P[5]
            a2 = P[6]
            for c in range(8):
                kb.mm(a1, a1[0:64, :], wuq_s, wuq_s[:, c, h * 256 + 128:h * 256 + 192], cqn, cqn[:, c, :], c == 0, c == 7)
            for c in range(8):
                kb.mm(a2, a2[0:64, :], wuq_s, wuq_s[:, c, h * 256 + 192:h * 256 + 256], cqn, cqn[:, c, :], c == 0, c == 7)
            t1 = t1_r.next()
            t2 = t2_r.next()
            kb.op("dve", lambda e: e.tensor_tensor(out=t1[:], in0=a1[0:64, :], in1=cs[:], op=ALU.mult), R=[a1, cs], W=[t1])
            kb.op("dve", lambda e: e.tensor_tensor(out=t2[:], in0=a2[0:64, :], in1=sn[:], op=ALU.mult), R=[a2, sn], W=[t2])
            kb.op("pool", lambda e: e.tensor_tensor(out=t1[:], in0=t1[:], in1=t2[:], op=ALU.add), R=[t1, t2], W=[t1])
            s = stg.next()
            kb.op("dve", lambda e: e.tensor_tensor(out=s[0:64, :], in0=t1[:], in1=rq[0:64, :], op=ALU.mult), R=[t1, rq], W=[s])
            kb.dma("sp", qT[h, 128:192, ts], s[0:64, :], R=[s], W=[qT])
            acc = P[pi % 4]
            pi += 1
            for c in range(4):
                kb.mm(acc, acc[:], wk_s, wk_s[:, c, h * 128:(h + 1) * 128], ckvn, ckvn[:, c, :], c == 0, c == 3)
            s = stg.next()
            kb.op("act", lambda e: e.copy(out=s[:], in_=acc[:]), R=[acc], W=[s])
            kb.dma("sp", knT[h, :, ts], s[:], R=[s], W=[knT])
        for tc in range(4):
            for hf in range(2):
                acc = P[pi % 4]
                pi += 1
                for c in range(4):
                    kb.mm(acc, acc[:], ckvn, ckvn[:, c, tc * 128:(tc + 1) * 128], wv_s, wv_s[:, c, hf * 512:(hf + 1) * 512],
                          c == 0, c == 3)
                s = stg.next()
                kb.op("act", lambda e: e.copy(out=s[:], in_=acc[:]), R=[acc], W=[s])
                t0 = tt * 512 + tc * 128
                kb.dma("sp", vmla[t0:t0 + 128, hf * 512:(hf + 1) * 512], s[:], R=[s], W=[vmla])
    kb.pop()

    kb.push()
    kro = kb.sb([64, S], BF16, "kro")
    kb.dma("sp", kro[:], kroT[:], R=[kroT], W=[kro])
    kn_r = Ring(kb, 2, [128, S], BF16, "kn")
    v_r = Ring(kb, 2, [128, 32, 128], BF16, "vC")
    qn_r = Ring(kb, 2, [128, 512], BF16, "qn")
    qr_r = Ring(kb, 2, [64, 512], BF16, "qr")
    pT_r = Ring(kb, 3, [128, 512], BF16, "pT")
    rr_r = Ring(kb, 2, [128, 512], F32, "rrC")
    ho_r = Ring(kb, 2, [128, 512], BF16, "hoC")
    si = 0
    for h in range(8):
        kn = kn_r.next()
        kb.dma("sp", kn[:], knT[h], R=[knT], W=[kn])
        v = v_r.next()
        kb.dma("sp", v[:], vmla[:, h * 128:(h + 1) * 128].rearrange("(c p) d -> p c d", p=128), R=[vmla], W=[v])
        for j in range(8):
            ts = slice(j * 512, (j + 1) * 512)
            qn = qn_r.next()
            qr = qr_r.next()
            kb.dma("sp", qn[:], qT[h, 0:128, ts], R=[qT], W=[qn])
            kb.dma("sp", qr[:], qT[h, 128:192, ts], R=[qT], W=[qr])
            O = P[4 + (j % 2)]
            L = P[6 + (j % 2)]
            nk = 4 * (j + 1)
            def emit_s(kc):
                nonlocal si
                sps = P[si % 4]
                si += 1
                ks = slice(kc * 128, (kc + 1) * 128)
                kb.mm(sps, sps[:], kn, kn[:, ks], qn, qn[:], True, False)
                kb.mm(sps, sps[:], kro, kro[:, ks], qr, qr[:], False, True)
                return sps

            def emit_rest(kc, sps):
                pT = pT_r.next()
                kb.op("act", lambda e: e.activation(out=pT[:], in_=sps[:], func=AF.Exp, scale=MLA_SCALE), R=[sps], W=[pT])
                m = kc - 4 * j
                if m >= 0:
                    kb.op("pool", lambda e: e.tensor_tensor(out=pT[:], in0=pT[:], in1=masks[:, m, :], op=ALU.mult),
                          R=[pT, masks], W=[pT])
                kb.mm(O, O[:], v, v[:, kc, :], pT, pT[:], kc == 0, kc == nk - 1)
                kb.mm(L, L[:], onesb, onesb[:], pT, pT[:], kc == 0, kc == nk - 1)
            pipelined(nk, emit_s, emit_rest, look=2)
            rr = rr_r.next()
            kb.op("dve", lambda e: e.reciprocal(out=rr[:], in_=L[:]), R=[L], W=[rr])
            ho = ho_r.next()
            kb.op("dve", lambda e: e.tensor_tensor(out=ho[:], in0=O[:], in1=rr[:], op=ALU.mult), R=[O, rr], W=[ho])
            kb.dma("sp", hTo[1024 + h * 128:1024 + (h + 1) * 128, ts], ho[:], R=[ho], W=[hTo])
    kb.pop()

    kb.push()
    ones4k = kb.sb([128, S], F32, "ones4k")
    kb.op("pool", lambda e: e.memset(ones4k[:], 1.0), W=[ones4k])
    gbias = kb.sb([128, 4], F32, "gbias")
    kb.op("dve", lambda e: e.tensor_scalar(out=gbias[:], in0=sp_[:, 0:4], scalar1=1.0 / 15.0, scalar2=None, op0=ALU.mult),
          R=[sp_], W=[gbias])
    ga = kb.sb([128, S], F32, "ga")
    gb = kb.sb([128, S], F32, "gb")
    nA = kb.sb([128, S], F32, "nA")
    e3 = kb.sb([128, S], F32, "e3")
    acol = kb.sb([128, 32], F32, "acol")
    kaS = kb.sb([128, 2, S], BF16, "kaS")
    vaS = kb.sb([128, 32, 512], BF16, "vaS")
    qa_r = Ring(kb, 2, [128, 2, 512], BF16, "qa")
    og_r = Ring(kb, 2, [128, 4, 512], BF16, "og")
    dm_r = Ring(kb, 2, [128, 512], F32, "dm")
    pT_r = Ring(kb, 3, [128, 512], BF16, "pTD")
    rr = kb.sb([128, 512], F32, "rrD")
    h0 = kb.sb([128, 4, 512], F32, "h0")
    sq_r = Ring(kb, 2, [128, 512], F32, "sqD")
    sg_r = Ring(kb, 2, [128, 512], F32, "sgD")
    ho_r = Ring(kb, 2, [128, 512], BF16, "hoD")
    si = 0
    for hd in range(2):
        kb.dma("sp", ga[:], grep[hd], R=[grep], W=[ga])
        kb.dma("sp", gb[:], grep[2 + hd], R=[grep], W=[gb])
        kb.dma("sp", kaS[:], kaT[hd * 256:(hd + 1) * 256, :].rearrange("(c p) t -> p c t", p=128), R=[kaT], W=[kaS])
        kb.dma("sp", vaS[:], va[:, hd * 512:(hd + 1) * 512].rearrange("(c p) d -> p c d", p=128), R=[va], W=[vaS])
        kb.op("act", lambda e: e.activation(out=ga[:], in_=ga[:], func=AF.Tanh, scale=1.0 / 15.0, bias=gbias[:, hd:hd + 1]),
              R=[ga, gbias], W=[ga])
        kb.op("pool", lambda e: e.tensor_scalar(out=ga[:], in0=ga[:], scalar1=15.0, scalar2=None, op0=ALU.mult), R=[ga], W=[ga])
        kb.op("act", lambda e: e.activation(out=gb[:], in_=gb[:], func=AF.Tanh, scale=1.0 / 15.0, bias=gbias[:, 2 + hd:3 + hd]),
              R=[gb, gbias], W=[gb])
        kb.op("act", lambda e: e.activation(out=gb[:], in_=gb[:], func=AF.Exp, scale=-15.0), R=[gb], W=[gb])
        kb.op("act", lambda e: e.activation(out=gb[:], in_=gb[:], func=AF.Ln, bias=1.0), R=[gb], W=[gb])
        kb.op("dve", lambda e: e.tensor_tensor_scan(out=gb[:], data0=ones4k[:], data1=gb[:], initial=0.0,
                                                    op0=ALU.mult, op1=ALU.add), R=[ones4k, gb], W=[gb])
        kb.op("dve", lambda e: e.tensor_tensor(out=ga[:], in0=ga[:], in1=gb[:], op=ALU.add), R=[ga, gb], W=[ga])
        kb.op("dve", lambda e: e.tensor_tensor_scan(out=nA[:], data0=ga[:], data1=ga[:], initial=0.0,
                                                    op0=ALU.max, op1=ALU.max), R=[ga], W=[nA])
        kb.op("pool", lambda e: e.tensor_scalar(out=nA[:], in0=nA[:], scalar1=-1.0, scalar2=None, op0=ALU.mult), R=[nA], W=[nA])
        kb.op("dve", lambda e: e.tensor_tensor(out=e3[:], in0=gb[:], in1=nA[:], op=ALU.add), R=[gb, nA], W=[e3])
        kb.op("act", lambda e: e.activation(out=e3[:], in_=e3[:], func=AF.Exp), R=[e3], W=[e3])
        for c in range(32):
            tp = P[si % 3]
            si += 1
            kb.op("pe", lambda e: e.transpose(out=tp[:, 0:128], in_=ga[:, c * 128:(c + 1) * 128], identity=ident[:]),
                  R=[ga, ident], W=[tp])
            kb.op("dve", lambda e: e.tensor_scalar(out=acol[:, c:c + 1], in0=tp[:, 0:1], scalar1=-math.log(16.0), scalar2=None,
                                                   op0=ALU.add), R=[tp], W=[acol])
        for j in range(8):
            ts = slice(j * 512, (j + 1) * 512)
            qa = qa_r.next()
            kb.dma("sp", qa[:], qaT[hd * 256:(hd + 1) * 256, ts].rearrange("(c p) t -> p c t", p=128), R=[qaT], W=[qa])
            og = og_r.next()
            kb.dma("sp", og[:], ogT[hd * 512:(hd + 1) * 512, ts].rearrange("(c p) t -> p c t", p=128), R=[ogT], W=[og])
            O = [P[3], P[4], P[5], P[6]]
            L = P[7]
            nk = 4 * (j + 1)
            def emit_s(kc):
                nonlocal si
                sps = P[si % 3]
                si += 1
                ks = slice(kc * 128, (kc + 1) * 128)
                kb.mm(sps, sps[:], kaS, kaS[:, 0, ks], qa, qa[:, 0, :], True, False)
                kb.mm(sps, sps[:], kaS, kaS[:, 1, ks], qa, qa[:, 1, :], False, True)
                return sps

            def emit_rest(kc, sps):
                dm = dm_r.next()
                kb.op("act", lambda e: e.activation(out=dm[:], in_=nA[:, ts], func=AF.Exp, bias=acol[:, kc:kc + 1]),
                      R=[nA, acol], W=[dm])
                m = kc - 4 * j
                if m >= 0:
                    kb.op("pool", lambda e: e.tensor_tensor(out=dm[:], in0=dm[:], in1=masks[:, m, :], op=ALU.mult),
                          R=[dm, masks], W=[dm])
                pT = pT_r.next()
                kb.op("dve", lambda e: e.tensor_tensor(out=pT[:], in0=sps[:], in1=dm[:], op=ALU.mult), R=[sps, dm], W=[pT])
                for c in range(4):
                    kb.mm(O[c], O[c][:], vaS, vaS[:, kc, c * 128:(c + 1) * 128], pT, pT[:], kc == 0, kc == nk - 1)
                kb.mm(L, L[:], onesb, onesb[:], pT, pT[:], kc == 0, kc == nk - 1)
            pipelined(nk, emit_s, emit_rest, look=2)
            kb.op("act", lambda e: e.activation(out=rr[:], in_=L[:], func=AF.Abs), R=[L], W=[rr])
            kb.op("dve", lambda e: e.tensor_tensor(out=rr[:], in0=rr[:], in1=e3[:, ts], op=ALU.max), R=[rr, e3], W=[rr])
            kb.op("dve", lambda e: e.reciprocal(out=rr[:], in_=rr[:]), R=[rr], W=[rr])
            ssp = P[si % 3]
            si += 1
            for c in range(4):
                kb.op("dve", lambda e: e.tensor_tensor(out=h0[:, c, :], in0=O[c][:], in1=rr[:], op=ALU.mult),
                      R=[O[c], rr], W=[h0])
                sq = sq_r.next()
                kb.op("act", lambda e: e.activation(out=sq[:], in_=h0[:, c, :], func=AF.Square), R=[h0], W=[sq])
                kb.mm(ssp, ssp[:], onesf, onesf[:], sq, sq[:], c == 0, c == 3)
            kb.op("dve", lambda e: e.tensor_scalar(out=rr[:], in0=ssp[:], scalar1=1.0 / 512.0, scalar2=RMS_EPS,
                                                   op0=ALU.mult, op1=ALU.add), R=[ssp], W=[rr])
            kb.op("act", lambda e: e.activation(out=rr[:], in_=rr[:], func=AF.Ln), R=[rr], W=[rr])
            kb.op("act", lambda e: e.activation(out=rr[:], in_=rr[:], func=AF.Exp, scale=-0.5), R=[rr], W=[rr])
            for c in range(4):
                sg = sg_r.next()
                kb.op("act", lambda e: e.activation(out=sg[:], in_=og[:, c, :], func=AF.Sigmoid), R=[og], W=[sg])
                kb.op("pool", lambda e: e.tensor_tensor(out=sg[:], in0=sg[:], in1=rr[:], op=ALU.mult), R=[sg, rr], W=[sg])
                ho = ho_r.next()
                gi = 4 + hd * 4 + c
                kb.op("dve", lambda e: e.scalar_tensor_tensor(out=ho[:], in0=h0[:, c, :], scalar=sp_[:, gi:gi + 1], in1=sg[:],
                                                              op0=ALU.mult, op1=ALU.mult), R=[h0, sp_, sg], W=[ho])
                r0 = hd * 512 + c * 128
                kb.dma("sp", hTo[r0:r0 + 128, ts], ho[:], R=[ho], W=[hTo])
    kb.pop()
    kb.pop()


AB_OFF = {"q_a": 0, "k_a": 1024, "v_a": 2048, "ig": 4096, "fg": 4100, "og": 4104, "cq": 6152, "ckv": 7176, "kr": 7688}


def prep_M0_consts():
    pos = np.arange(SEQ, dtype=np.float32)
    inv = (10000.0 ** (-np.arange(32, dtype=np.float32) / 32)).astype(np.float32)
    ang = pos[None, :] * inv[:, None]
    cos, sin = np.cos(ang).astype(np.float32), np.sin(ang).astype(np.float32)
    cosT = np.concatenate([cos, cos], 0)
    sinT = np.concatenate([-sin, sin], 0)
    kl = np.arange(128)[:, None]
    ql = np.arange(512)[None, :]
    dmask = np.stack([(128 * m + kl <= ql) for m in range(4)]).astype(np.float32).astype(ml_dtypes.bfloat16)
    return {"cosT": np.ascontiguousarray(cosT), "sinT": np.ascontiguousarray(sinT), "dmask": dmask,
            "ident": np.eye(128, dtype=np.float32)}


def prep_M0_weights(hh, w_in, b_ig, b_fg, mnorm, qnorm, kvnorm, w_uq, w_ukv):
    o = AB_OFF
    cols = []
    cols.append(w_in[:, o["q_a"] + hh * 512:o["q_a"] + hh * 512 + 512])
    cols.append(w_in[:, o["k_a"] + hh * 512:o["k_a"] + hh * 512 + 512])
    cols.append(w_in[:, o["og"] + hh * 1024:o["og"] + hh * 1024 + 1024])
    cols.append(w_in[:, o["cq"]:o["cq"] + 1024])
    cols.append(w_in[:, o["ckv"]:o["ckv"] + 512])
    kr = w_in[:, o["kr"]:o["kr"] + 64]
    cols.append(kr)
    cols.append(np.concatenate([kr[:, 32:64], kr[:, 0:32]], 1))
    cols.append(w_in[:, o["v_a"] + hh * 1024:o["v_a"] + hh * 1024 + 1024])
    for c in (o["ig"] + 2 * hh, o["ig"] + 2 * hh + 1, o["fg"] + 2 * hh, o["fg"] + 2 * hh + 1):
        cols.append(np.repeat(w_in[:, c:c + 1], 128, axis=1))
    wA = tile_kxn(np.concatenate(cols, 1))
    assert wA.shape[0] == M0_NW
    hs = range(hh * 8, hh * 8 + 8)
    uq = []
    for h in hs:
        blk = w_uq[:, h * 192:(h + 1) * 192]
        uq += [blk[:, 0:128], blk[:, 128:192], blk[:, 160:192], blk[:, 128:160]]
    uq = np.concatenate(uq, 1)
    wuq = np.ascontiguousarray(uq.reshape(8, 128, 2048).transpose(1, 0, 2))
    kvk = np.concatenate([w_ukv[:, h * 256:h * 256 + 128] for h in hs], 1)
    kvv = np.concatenate([w_ukv[:, h * 256 + 128:h * 256 + 256] for h in hs], 1)
    wk = np.ascontiguousarray(kvk.reshape(4, 128, 1024).transpose(1, 0, 2))
    wv = np.ascontiguousarray(kvv.reshape(4, 128, 1024).transpose(1, 0, 2))
    sp = np.zeros((128, 24), np.float32)
    sp[:, 0] = b_ig[2 * hh]
    sp[:, 1] = b_ig[2 * hh + 1]
    sp[:, 2] = b_fg[2 * hh]
    sp[:, 3] = b_fg[2 * hh + 1]
    sp[:, 4:12] = mnorm[hh * 1024:(hh + 1) * 1024].reshape(8, 128).T
    sp[:, 12:20] = qnorm.reshape(8, 128).T
    sp[:, 20:24] = kvnorm.reshape(4, 128).T
    return {"wA": wA, "wuq": wuq, "wukv_k": wk, "wukv_v": wv, "smallp": sp}


M1_NW = 29
NSA_SCALE = 128.0 ** -0.5
GELU_C = 1.5957691216057308


def emit_M1(kb, P, xT, hTo):
    S = SEQ
    kb.pfx = "m1_"
    kb.push()
    wA = kb.dram("wA", [M1_NW, 128, 32, 128], F32, "ExternalInput")
    w1k = kb.dram("w1k", [128, 32, 256], F32, "ExternalInput")
    w1v = kb.dram("w1v", [128, 32, 256], F32, "ExternalInput")
    w2k = kb.dram("w2k", [128, 2, 128], F32, "ExternalInput")
    w2v = kb.dram("w2v", [128, 2, 128], F32, "ExternalInput")
    peT = kb.dram("peT", [128, 2, 32], F32, "ExternalInput")
    gbias = kb.dram("gbias", [128, 1], F32, "ExternalInput")
    cmaskT = kb.dram("cmaskT", [256, S], BF16, "ExternalInput")
    ovl = kb.dram("ovl", [256, 64], BF16, "ExternalInput")
    KM = kb.dram("KM", [S, 64], F32, "ExternalInput")
    AM = kb.dram("AM", [S, 64], F32, "ExternalInput")
    Emat = kb.dram("Emat", [64, S], BF16, "ExternalInput")
    dmask = kb.dram("dmask", [4, 128, 512], BF16, "ExternalInput")
    wmask = kb.dram("wmask", [8, 128, 512], BF16, "ExternalInput")
    selm = kb.dram("selm", [48, 48, 128], F32, "ExternalInput")
    ident_d = kb.dram("ident", [128, 128], F32, "ExternalInput")
    qT = kb.dram("qTs", [2048, S], BF16)
    kcT = kb.dram("kcT", [256, S], BF16)
    vcT = kb.dram("vcT", [256, S], BF16)
    ksT = kb.dram("ksT", [256, S], BF16)
    kwT = kb.dram("kwT", [256, S], BF16)
    gsT = kb.dram("gsT", [128, S], F32)
    vsw = kb.dram("vsw", [S, 512], BF16)
    fm_dst = [(qT, i) for i in range(16)] + [(kcT, 0), (kcT, 1), (vcT, 0), (vcT, 1), (ksT, 0), (ksT, 1), (kwT, 0), (kwT, 1)]

    onesb = kb.sb([128, 128], BF16, "onesb")
    kb.op("dve", lambda e: e.memset(onesb[:], 1.0), W=[onesb])
    gb_s = kb.sb([128, 1], F32, "gb_s")
    kb.dma("sp", gb_s[:], gbias[:], W=[gb_s])
    masks = kb.sb([128, 4, 512], BF16, "masks")
    kb.dma("sp", masks[:], dmask[:].rearrange("m p t -> p m t"), W=[masks])
    wmasks = kb.sb([128, 8, 512], BF16, "wmasks")
    kb.dma("sp", wmasks[:], wmask[:].rearrange("m p t -> p m t"), W=[wmasks])
    ident = kb.sb([128, 128], F32, "ident")
    kb.dma("sp", ident[:], ident_d[:], W=[ident])

    kb.push()
    xTb = kb.sb([128, 32, 2048], BF16, "xTb")
    wring = Ring(kb, 6, [128, 32, 128], BF16, "wA")
    stg = Ring(kb, 3, [128, 512], BF16, "stgA")
    stgf = Ring(kb, 2, [128, 512], F32, "stgAf")
    pi = 0
    for st in range(2):
        for q4 in range(4):
            c0 = st * 2048 + q4 * 512
            kb.dma("pool", xTb[:, :, q4 * 512:(q4 + 1) * 512],
                   xT[:, c0:c0 + 512].rearrange("(k p) t -> p k t", p=128), W=[xTb])
        for wi in range(25):
            w = wring.next()
            kb.dma("pool", w[:], wA[wi], W=[w])
            for q4 in range(4):
                acc = P[pi % 4]
                pi += 1
                for kc in range(32):
                    kb.mm(acc, acc[:], w, w[:, kc, :], xTb, xTb[:, kc, q4 * 512:(q4 + 1) * 512], kc == 0, kc == 31)
                c0 = st * 2048 + q4 * 512
                if wi < 24:
                    dst, ci = fm_dst[wi]
                    s = stg.next()
                    kb.op("act", lambda e: e.copy(out=s[:], in_=acc[:]), R=[acc], W=[s])
                    kb.dma("sp", dst[ci * 128:(ci + 1) * 128, c0:c0 + 512], s[:], R=[s], W=[dst])
                else:
                    s = stgf.next()
                    kb.op("act", lambda e: e.activation(out=s[:], in_=acc[:], func=AF.Sigmoid, bias=gb_s[:, 0:1]),
                          R=[acc, gb_s], W=[s])
                    kb.dma("sp", gsT[:, c0:c0 + 512], s[:], R=[s], W=[gsT])
        ws = []
        for j in range(4):
            w = wring.next()
            kb.dma("pool", w[:], wA[25 + j], W=[w])
            ws.append(w)
        for tc in range(16):
            acc = P[pi % 4]
            pi += 1
            for j in range(4):
                for kc in range(32):
                    kb.mm(acc, acc[:, j * 128:(j + 1) * 128], xTb, xTb[:, kc, tc * 128:(tc + 1) * 128],
                          ws[j], ws[j][:, kc, :], kc == 0, kc == 31)
            s = stg.next()
            kb.op("act", lambda e: e.copy(out=s[:], in_=acc[:]), R=[acc], W=[s])
            t0 = st * 2048 + tc * 128
            kb.dma("sp", vsw[t0:t0 + 128, :], s[:], R=[s], W=[vsw])
    kb.pop()

    kcmpT = kb.sb([128, 2, 256], BF16, "kcmpT")
    vcmp = kb.sb([128, 2, 2, 128], BF16, "vcmp")
    kb.push()
    w1s = [kb.sb([128, 32, 256], BF16, "w1k"), kb.sb([128, 32, 256], BF16, "w1v")]
    w2s = [kb.sb([128, 2, 128], BF16, "w2k"), kb.sb([128, 2, 128], BF16, "w2v")]
    pes = kb.sb([128, 2, 32], BF16, "pes")
    kb.dma("pool", w1s[0][:], w1k[:], W=[w1s[0]])
    kb.dma("pool", w1s[1][:], w1v[:], W=[w1s[1]])
    kb.dma("pool", w2s[0][:], w2k[:], W=[w2s[0]])
    kb.dma("pool", w2s[1][:], w2v[:], W=[w2s[1]])
    kb.dma("pool", pes[:], peT[:], W=[pes])
    src_r = Ring(kb, 2, [128, S], BF16, "cmpsrc")
    c0s = kb.sb([128, 4], F32, "c0s")
    u_r = Ring(kb, 2, [128, 256], F32, "u")
    t_r = Ring(kb, 2, [128, 256], F32, "t")
    hid = kb.sb([128, 2, 256], BF16, "hidc")
    pi = 0
    for kv in range(2):
        for hc in range(2):
            acc = P[4]
            for l in range(32):
                kb.mm(acc, acc[:, 0:1], w1s[kv], w1s[kv][:, l, hc * 128:(hc + 1) * 128], pes, pes[:, kv, l:l + 1], l == 0, l == 31)
            kb.op("dve", lambda e: e.tensor_copy(out=c0s[:, kv * 2 + hc:kv * 2 + hc + 1], in_=acc[:, 0:1]), R=[acc], W=[c0s])
        for gl in range(2):
            src = src_r.next()
            sd = kcT if kv == 0 else vcT
            kb.dma("sp", src[:], sd[gl * 128:(gl + 1) * 128, :], R=[sd], W=[src])
            for hc in range(2):
                acc = P[pi % 4]
                pi += 1
                for l in range(32):
                    kb.mm(acc, acc[:, 0:255], w1s[kv], w1s[kv][:, l, hc * 128:(hc + 1) * 128], src, src[:, l:l + 16 * 254 + 1:16],
                          l == 0, l == 31)
                u = u_r.next()
                t = t_r.next()
                ci = kv * 2 + hc
                kb.op("dve", lambda e: e.tensor_scalar(out=u[:, 0:255], in0=acc[:, 0:255], scalar1=c0s[:, ci:ci + 1], scalar2=None,
                                                       op0=ALU.add), R=[acc, c0s], W=[u])
                kb.op("dve", lambda e: e.tensor_tensor(out=t[:, 0:255], in0=u[:, 0:255], in1=u[:, 0:255], op=ALU.mult), R=[u], W=[t])
                kb.op("dve", lambda e: e.tensor_scalar(out=t[:, 0:255], in0=t[:, 0:255], scalar1=0.044715, scalar2=1.0,
                                                       op0=ALU.mult, op1=ALU.add), R=[t], W=[t])
                kb.op("dve", lambda e: e.tensor_tensor(out=t[:, 0:255], in0=t[:, 0:255], in1=u[:, 0:255], op=ALU.mult), R=[t, u], W=[t])
                kb.op("act", lambda e: e.activation(out=t[:, 0:255], in_=t[:, 0:255], func=AF.Sigmoid, scale=GELU_C), R=[t], W=[t])
                kb.op("dve", lambda e: e.tensor_tensor(out=hid[:, hc, 0:255], in0=t[:, 0:255], in1=u[:, 0:255], op=ALU.mult),
                      R=[t, u], W=[hid])
            if kv == 0:
                acc = P[pi % 4]
                pi += 1
                for hc in range(2):
                    kb.mm(acc, acc[:, 0:255], w2s[0], w2s[0][:, hc, :], hid, hid[:, hc, 0:255], hc == 0, hc == 1)
                kb.op("act", lambda e: e.copy(out=kcmpT[:, gl, 0:255], in_=acc[:, 0:255]), R=[acc], W=[kcmpT])
            else:
                for ch in range(2):
                    nn = 128 if ch == 0 else 127
                    acc = P[pi % 4]
                    pi += 1
                    for hc in range(2):
                        kb.mm(acc, acc[0:nn, 0:128], hid, hid[:, hc, ch * 128:ch * 128 + nn], w2s[1], w2s[1][:, hc, :], hc == 0, hc == 1)
                    kb.op("act", lambda e: e.copy(out=vcmp[0:nn, gl, ch, :], in_=acc[0:nn, 0:128]), R=[acc], W=[vcmp])
    kb.pop()

    kb.push()
    zrow = kb.sb([1, 256], BF16, "zrow")
    kb.op("dve", lambda e: e.memset(zrow[:], 0.0), W=[zrow])
    selms = kb.sb([48, 48, 128], F32, "selms")
    kb.dma("sp", selms[:], selm[:], W=[selms])
    ovls = kb.sb([128, 2, 64], BF16, "ovls")
    kb.dma("sp", ovls[:], ovl[:].rearrange("(c p) s -> p c s", p=128), W=[ovls])
    Es = kb.sb([64, S], BF16, "Es")
    kb.dma("sp", Es[:], Emat[:], W=[Es])
    ksS = kb.sb([128, S], BF16, "ksS")
    kwS = kb.sb([128, S], BF16, "kwS")
    vsS = kb.sb([128, 32, 128], BF16, "vsS")
    vwS = kb.sb([128, 32, 128], BF16, "vwS")
    qS = kb.sb([128, 8, 512], BF16, "qS")
    gs = kb.sb([48, 512], F32, "gs")
    cm_r = Ring(kb, 2, [128, 2, 512], BF16, "cm")
    pT_r = Ring(kb, 4, [128, 512], BF16, "pT")
    pn_r = Ring(kb, 2, [128, 512], BF16, "pn")
    rr_r = Ring(kb, 3, [128, 512], F32, "rr")
    grep_r = Ring(kb, 3, [128, 512], F32, "grep")
    outacc = kb.sb([128, 8, 512], F32, "outacc")
    tmpo = Ring(kb, 2, [128, 512], F32, "tmpo")
    ho_r = Ring(kb, 2, [128, 512], BF16, "ho")
    imp = kb.sb([128, 64], F32, "imp")
    km_r = Ring(kb, 2, [128, 64], F32, "km")
    am_r = Ring(kb, 2, [128, 64], F32, "am")
    cmpb = kb.sb([128, 64, 64], BF16, "cmpb")
    rank = kb.sb([128, 64], F32, "rank")
    selT = kb.sb([64, 512], BF16, "selT")
    smask = kb.sb([128, 32, 512], BF16, "smask")
    si = 0
    mi = 0

    def gate_rep(br, gl, hg):
        nonlocal si
        r = br * 16 + gl * 8 + hg
        ps = P[si % 3]
        si += 1
        kb.mm(ps, ps[:], selms, selms[:, r, :], gs, gs[:], True, True)
        return ps

    for gl in range(2):
        kb.dma("sp", ksS[:], ksT[gl * 128:(gl + 1) * 128, :], R=[ksT], W=[ksS])
        kb.dma("sp", kwS[:], kwT[gl * 128:(gl + 1) * 128, :], R=[kwT], W=[kwS])
        kb.dma("sp", vsS[:], vsw[:, gl * 128:(gl + 1) * 128].rearrange("(c p) d -> p c d", p=128), R=[vsw], W=[vsS])
        kb.dma("sp", vwS[:], vsw[:, 256 + gl * 128:256 + (gl + 1) * 128].rearrange("(c p) d -> p c d", p=128), R=[vsw], W=[vwS])
        for j in range(8):
            ts = slice(j * 512, (j + 1) * 512)
            kb.dma("sp", qS[:], qT[gl * 1024:(gl + 1) * 1024, ts].rearrange("(c p) t -> p c t", p=128), R=[qT], W=[qS])
            kb.dma("sp", gs[:], gsT[0:48, ts], R=[gsT], W=[gs])
            nb = min(255, 32 * j + 31)
            chunks = [(0, min(nb, 128))] + ([(128, nb - 128)] if nb > 128 else [])
            cm = cm_r.next()
            for ci, (n0, nn) in enumerate(chunks):
                kb.dma("sp", cm[0:nn, ci, :], cmaskT[n0:n0 + nn, ts], W=[cm])
            impP = P[7]
            kb.mm(impP, impP[:, 0:256], zrow, zrow[0:1, 0:128], zrow, zrow[0:1, 0:256], True, False)
            for hg in range(8):
                O = P[3 + (hg % 2)]
                L = P[5 + (hg % 2)]
                pts = []
                for ci, (n0, nn) in enumerate(chunks):
                    sps = P[si % 3]
                    si += 1
                    kb.mm(sps, sps[0:nn, :], kcmpT, kcmpT[:, gl, n0:n0 + nn], qS, qS[:, hg, :], True, True)
                    pT = pT_r.next()
                    kb.op("act", lambda e: e.activation(out=pT[0:nn, :], in_=sps[0:nn, :], func=AF.Exp, scale=NSA_SCALE), R=[sps], W=[pT])
                    kb.op("pool", lambda e: e.tensor_tensor(out=pT[0:nn, :], in0=pT[0:nn, :], in1=cm[0:nn, ci, :], op=ALU.mult),
                          R=[pT, cm], W=[pT])
                    kb.mm(O, O[:], vcmp, vcmp[0:nn, gl, ci, :], pT, pT[0:nn, :], ci == 0, ci == len(chunks) - 1)
                    kb.mm(L, L[:], onesb, onesb[0:nn, :], pT, pT[0:nn, :], ci == 0, ci == len(chunks) - 1)
                    pts.append(pT)
                rr = rr_r.next()
                kb.op("dve", lambda e: e.tensor_scalar(out=rr[:], in0=L[:], scalar1=1e-30, scalar2=None, op0=ALU.max), R=[L], W=[rr])
                kb.op("dve", lambda e: e.reciprocal(out=rr[:], in_=rr[:]), R=[rr], W=[rr])
                for ci, (n0, nn) in enumerate(chunks):
                    pn = pn_r.next()
                    kb.op("dve", lambda e: e.tensor_tensor(out=pn[0:nn, :], in0=pts[ci][0:nn, :], in1=rr[0:nn, :], op=ALU.mult),
                          R=[pts[ci], rr], W=[pn])
                    for qi in range(4):
                        first = (hg == 0 and ci == 0)
                        last = (hg == 7 and ci == len(chunks) - 1)
                        kb.mm(impP, impP[:, qi * 64:(qi + 1) * 64], pn, pn[0:nn, qi * 128:(qi + 1) * 128], ovls, ovls[0:nn, ci, :],
                              False, last)
                gp = gate_rep(0, gl, hg)
                gr = grep_r.next()
                kb.op("dve", lambda e: e.tensor_tensor(out=gr[:], in0=gp[:], in1=rr[:], op=ALU.mult), R=[gp, rr], W=[gr])
                kb.op("dve", lambda e: e.tensor_tensor(out=outacc[:, hg, :], in0=O[:], in1=gr[:], op=ALU.mult), R=[O, gr], W=[outacc])
            for qi in range(4):
                q0 = j * 512 + qi * 128
                km = km_r.next()
                am = am_r.next()
                kb.dma("sp", km[:], KM[q0:q0 + 128, :], W=[km])
                kb.dma("sp", am[:], AM[q0:q0 + 128, :], W=[am])
                kb.op("dve", lambda e: e.tensor_tensor(out=imp[:], in0=impP[:, qi * 64:(qi + 1) * 64], in1=km[:], op=ALU.mult),
                      R=[impP, km], W=[imp])
                kb.op("dve", lambda e: e.tensor_tensor(out=imp[:], in0=imp[:], in1=am[:], op=ALU.add), R=[imp, am], W=[imp])
                kb.op("dve", lambda e: e.tensor_tensor(out=cmpb[:], in0=imp[:, :].unsqueeze(1).to_broadcast([128, 64, 64]),
                                                       in1=imp[:, :].unsqueeze(2).to_broadcast([128, 64, 64]), op=ALU.is_gt),
                      R=[imp], W=[cmpb])
                kb.op("dve", lambda e: e.tensor_reduce(out=rank[:], in_=cmpb[:], axis=AX.X, op=ALU.add), R=[cmpb], W=[rank])
                kb.op("dve", lambda e: e.tensor_scalar(out=rank[:], in0=rank[:], scalar1=15.5, scalar2=None, op0=ALU.is_lt),
                      R=[rank], W=[rank])
                tp = P[si % 3]
                si += 1
                kb.op("pe", lambda e: e.transpose(out=tp[0:64, 0:128], in_=rank[:, :], identity=ident[:]), R=[rank, ident], W=[tp])
                kb.op("act", lambda e: e.copy(out=selT[:, qi * 128:(qi + 1) * 128], in_=tp[0:64, 0:128]), R=[tp], W=[selT])
            nk = 4 * (j + 1)
            for kc in range(nk):
                ps = P[si % 3]
                si += 1
                kb.mm(ps, ps[:], Es, Es[:, kc * 128:(kc + 1) * 128], selT, selT[:], True, True)
                m = kc - 4 * j
                if m >= 0:
                    kb.op("dve", lambda e: e.tensor_tensor(out=smask[:, kc, :], in0=ps[:], in1=masks[:, m, :], op=ALU.mult),
                          R=[ps, masks], W=[smask])
                else:
                    kb.op("act", lambda e: e.copy(out=smask[:, kc, :], in_=ps[:]), R=[ps], W=[smask])
            for hg in range(8):
                O = P[3 + (hg % 2)]
                L = P[5 + (hg % 2)]
                def emit_s(kc):
                    nonlocal si
                    sps = P[si % 3]
                    si += 1
                    ks = slice(kc * 128, (kc + 1) * 128)
                    kb.mm(sps, sps[:], ksS, ksS[:, ks], qS, qS[:, hg, :], True, True)
                    return sps

                def emit_rest(kc, sps):
                    nonlocal mi
                    pT = pT_r.next()
                    kb.op("act", lambda e: e.activation(out=pT[:], in_=sps[:], func=AF.Exp, scale=NSA_SCALE), R=[sps], W=[pT])
                    eng = "pool" if (mi % 2 == 0) else "dve"
                    mi += 1
                    kb.op(eng, lambda e: e.tensor_tensor(out=pT[:], in0=pT[:], in1=smask[:, kc, :], op=ALU.mult), R=[pT, smask], W=[pT])
                    kb.mm(O, O[:], vsS, vsS[:, kc, :], pT, pT[:], kc == 0, kc == nk - 1)
                    kb.mm(L, L[:], onesb, onesb[:], pT, pT[:], kc == 0, kc == nk - 1)
                pipelined(nk, emit_s, emit_rest, look=2)
                rr = rr_r.next()
                kb.op("dve", lambda e: e.reciprocal(out=rr[:], in_=L[:]), R=[L], W=[rr])
                gp = gate_rep(1, gl, hg)
                gr = grep_r.next()
                kb.op("dve", lambda e: e.tensor_tensor(out=gr[:], in0=gp[:], in1=rr[:], op=ALU.mult), R=[gp, rr], W=[gr])
                to = tmpo.next()
                kb.op("dve", lambda e: e.tensor_tensor(out=to[:], in0=O[:], in1=gr[:], op=ALU.mult), R=[O, gr], W=[to])
                kb.op("pool", lambda e: e.tensor_tensor(out=outacc[:, hg, :], in0=outacc[:, hg, :], in1=to[:], op=ALU.add),
                      R=[outacc, to], W=[outacc])
                O2 = P[3 + ((hg + 1) % 2)]
                L2 = P[5 + ((hg + 1) % 2)]
                wch = [c for c in range(8) if 4 * j - 4 + c >= 0]
                def emit_s2(wi_):
                    nonlocal si
                    kc = 4 * j - 4 + wch[wi_]
                    sps = P[si % 3]
                    si += 1
                    ks = slice(kc * 128, (kc + 1) * 128)
                    kb.mm(sps, sps[:], kwS, kwS[:, ks], qS, qS[:, hg, :], True, True)
                    return sps

                def emit_rest2(wi_, sps):
                    nonlocal mi
                    c = wch[wi_]
                    kc = 4 * j - 4 + c
                    pT = pT_r.next()
                    kb.op("act", lambda e: e.activation(out=pT[:], in_=sps[:], func=AF.Exp, scale=NSA_SCALE), R=[sps], W=[pT])
                    eng = "pool" if (mi % 2 == 0) else "dve"
                    mi += 1
                    kb.op(eng, lambda e: e.tensor_tensor(out=pT[:], in0=pT[:], in1=wmasks[:, c, :], op=ALU.mult), R=[pT, wmasks], W=[pT])
                    kb.mm(O2, O2[:], vwS, vwS[:, kc, :], pT, pT[:], wi_ == 0, wi_ == len(wch) - 1)
                    kb.mm(L2, L2[:], onesb, onesb[:], pT, pT[:], wi_ == 0, wi_ == len(wch) - 1)
                pipelined(len(wch), emit_s2, emit_rest2, look=2)
                rr = rr_r.next()
                kb.op("dve", lambda e: e.reciprocal(out=rr[:], in_=L2[:]), R=[L2], W=[rr])
                gp = gate_rep(2, gl, hg)
                gr = grep_r.next()
                kb.op("dve", lambda e: e.tensor_tensor(out=gr[:], in0=gp[:], in1=rr[:], op=ALU.mult), R=[gp, rr], W=[gr])
                to = tmpo.next()
                kb.op("dve", lambda e: e.tensor_tensor(out=to[:], in0=O2[:], in1=gr[:], op=ALU.mult), R=[O2, gr], W=[to])
                ho = ho_r.next()
                kb.op("pool", lambda e: e.tensor_tensor(out=ho[:], in0=outacc[:, hg, :], in1=to[:], op=ALU.add),
                      R=[outacc, to], W=[ho])
                r0 = (gl * 8 + hg) * 128
                kb.dma("sp", hTo[r0:r0 + 128, ts], ho[:], R=[ho], W=[hTo])
    kb.pop()
    kb.pop()


C_OFF = {"q": 0, "kc": 4096, "vc": 4608, "ks": 5120, "vs": 5632, "kw": 6144, "vw": 6656, "gates": 7168}


def prep_M1_consts():
    S = SEQ
    n = np.arange(256)
    q = np.arange(S)
    cmaskT = ((16 * n[:, None] + 31) <= q[None, :]) & (n[:, None] < 255)
    cmp_start = np.arange(255) * 16
    sel_start = np.arange(64) * 64
    ov = np.zeros((256, 64), np.float32)
    ov[:255] = ((cmp_start[:, None] < sel_start[None, :] + 64) & (cmp_start[:, None] + 32 > sel_start[None, :]))
    cur = q // 64
    sb = np.arange(64)
    forced = (sb[None, :] == 0) | (sb[None, :] == cur[:, None]) | (sb[None, :] == cur[:, None] - 1)
    valid = sb[None, :] <= cur[:, None]
    KM = (valid & ~forced).astype(np.float32)
    AM = np.where(valid, np.where(forced, np.float32(1e9), np.float32(0.0)), np.float32(-1e30)).astype(np.float32)
    E = (np.arange(S)[None, :] // 64 == sb[:, None])
    kl = np.arange(128)[:, None]
    ql = np.arange(512)[None, :]
    dmask = np.stack([(128 * m + kl <= ql) for m in range(4)])
    wmask = np.stack([((128 * c + kl - 512 <= ql) & (128 * c + kl > ql)) for c in range(8)])
    selm = np.zeros((48, 48, 128), np.float32)
    for r in range(48):
        selm[r, r, :] = 1.0
    bf = lambda a: a.astype(np.float32).astype(ml_dtypes.bfloat16)
    return {"cmaskT": bf(cmaskT), "ovl": bf(ov), "KM": KM, "AM": AM, "Emat": bf(E), "dmask": bf(dmask), "wmask": bf(wmask),
            "selm": selm, "ident": np.eye(128, dtype=np.float32)}


def prep_M1_weights(gh, w_in, b_gate, pe_k, pe_v, w1k, w2k, w1v, w2v):
    o = C_OFF
    cols = [w_in[:, o["q"] + gh * 2048:o["q"] + (gh + 1) * 2048]]
    for nm in ("kc", "vc", "ks", "kw"):
        cols.append(w_in[:, o[nm] + gh * 256:o[nm] + (gh + 1) * 256])
    gidx = [o["gates"] + br * 32 + (2 * gh + gl) * 8 + hg for br in range(3) for gl in range(2) for hg in range(8)]
    gcols = np.zeros((D, 128), np.float32)
    gcols[:, 0:48] = w_in[:, gidx]
    cols.append(gcols)
    for nm in ("vs", "vw"):
        cols.append(w_in[:, o[nm] + gh * 256:o[nm] + (gh + 1) * 256])
    wA = tile_kxn(np.concatenate(cols, 1))
    assert wA.shape[0] == M1_NW
    gb = np.zeros((128, 1), np.float32)
    gb[0:48, 0] = b_gate[[i - o["gates"] for i in gidx]]
    lay1 = lambda w: np.ascontiguousarray(w.reshape(32, 128, 256).transpose(1, 0, 2))
    lay2 = lambda w: np.ascontiguousarray(w.reshape(2, 128, 128).transpose(1, 0, 2))
    peT = np.ascontiguousarray(np.stack([pe_k.T, pe_v.T], axis=1))
    return {"wA": wA, "w1k": lay1(w1k), "w1v": lay1(w1v), "w2k": lay2(w2k), "w2v": lay2(w2v), "peT": peT, "gbias": gb}


NFH = NFC // 2


def emit_F(kb, P, pfx, hT, xin, xout):
    S = SEQ
    kb.pfx = pfx
    wo_t = kb.dram("wo_t", [32, 128, 16, 128], F32, "ExternalInput")
    wg_t = kb.dram("wg_t", [NFH, 128, 32, 128], F32, "ExternalInput")
    wu_t = kb.dram("wu_t", [NFH, 128, 32, 128], F32, "ExternalInput")
    wd_t = kb.dram("wd_t", [32, 128, NFH, 128], F32, "ExternalInput")
    lnp = kb.dram("lnp", [128, 4, 32], F32, "ExternalInput")
    ypart = kb.dram("ypart", [D, S], F32)
    ysum = kb.dram("ysum", [D, S], F32)
    x1a = kb.dram("x1a", [D, S], F32)
    x1ab = kb.dram("x1ab", [D, S], BF16)
    yp2 = kb.dram("yp2", [8, D, 512], F32)
    ys2 = kb.dram("ys2", [8, D, 512], F32)
    ypart_dc = [Tk(ypart.h[dc * 128:(dc + 1) * 128, :], "ypart%d" % dc) for dc in range(32)]
    ysum_dc = [Tk(ysum.h[dc * 128:(dc + 1) * 128, :], "ysum%d" % dc) for dc in range(32)]
    yp2_t = [Tk(yp2.h[tt], "yp2_%d" % tt) for tt in range(8)]
    ys2_t = [Tk(ys2.h[tt], "ys2_%d" % tt) for tt in range(8)]

    kb.push()
    hTall = kb.sb([128, 16, S], BF16, "hTall")
    for q in range(4):
        kb.dma("sp", hTall[:, :, q * 1024:(q + 1) * 1024],
               hT[:, q * 1024:(q + 1) * 1024].rearrange("(k p) t -> p k t", p=128), R=[hT], W=[hTall])
    wring = Ring(kb, 3, [128, 16, 128], BF16, "wo")
    stg = Ring(kb, 3, [128, 512], F32, "stgo")
    pi = 0
    for dc in range(32):
        w = wring.next()
        kb.dma("pool", w[:], wo_t[dc], W=[w])
        for tt in range(8):
            acc = P[pi % 4]
            pi += 1
            for kc in range(16):
                kb.mm(acc, acc[:], w, w[:, kc, :], hTall, hTall[:, kc, tt * 512:(tt + 1) * 512], kc == 0, kc == 15)
            s = stg.next()
            kb.op("act", lambda e: e.copy(out=s[:], in_=acc[:]), R=[acc], W=[s])
            kb.dma("sp", ypart_dc[dc][:, tt * 512:(tt + 1) * 512], s[:], R=[s], W=[ypart_dc[dc]])
        kb.coll_allreduce(ypart_dc[dc], ypart_dc[dc][:, :], ysum_dc[dc], ysum_dc[dc][:, :])
    kb.pop()

    def ln_stage(get_y, xsrc, gi, bi, sink_factory):
        kb.push()
        ones = kb.sb([128, 128], F32, "ones")
        kb.op("dve", lambda e: e.memset(ones[:], 1.0), W=[ones])
        lneps = kb.sb([128, 1], F32, "lneps")
        kb.op("dve", lambda e: e.memset(lneps[:], LN_EPS), W=[lneps])
        lnsb = kb.sb([128, 4, 32], F32, "lnsb")
        kb.dma("sp", lnsb[:], lnp[:], W=[lnsb])
        xin_r = Ring(kb, 3, [128, 512], F32, "xin")
        y_r = Ring(kb, 3, [128, 512], F32, "yin")
        zT_r = Ring(kb, 2, [128, 32, 512], F32, "zT")
        zviews = {id(b): [Tk(b.h[:, dc, :], "zv%d" % dc) for dc in range(32)] for b in zT_r.bufs}
        xo_r = Ring(kb, 3, [128, 512], F32, "xo")
        sq_r = Ring(kb, 2, [128, 512], F32, "sq")
        mean = kb.sb([128, 512], F32, "mean")
        rstd = kb.sb([128, 512], F32, "rstd")
        nmr = kb.sb([128, 512], F32, "nmr")
        tmpa = kb.sb([128, 512], F32, "tmpa")
        S1, S2 = P[6], P[7]
        sink = sink_factory()
        for tt in range(8):
            ts = slice(tt * 512, (tt + 1) * 512)
            zT = zT_r.next()
            zv = zviews[id(zT)]
            for dc in range(32):
                xi = xin_r.next()
                kb.dma("sp", xi[:], xsrc[dc * 128:(dc + 1) * 128, ts], R=[xsrc], W=[xi])
                yi = y_r.next()
                ytk, yap = get_y(tt, dc)
                kb.dma("sp", yi[:], yap, R=[ytk], W=[yi])
                kb.op("dve", lambda e: e.scalar_tensor_tensor(out=zv[dc][:], in0=xi[:], scalar=ALPHA, in1=yi[:],
                                                              op0=ALU.mult, op1=ALU.add), R=[xi, yi], W=[zv[dc]])
                kb.mm(S1, S1[:], ones, ones[:], zv[dc], zv[dc][:], dc == 0, dc == 31)
                sq = sq_r.next()
                kb.op("act", lambda e: e.activation(out=sq[:], in_=zv[dc][:], func=AF.Square), R=[zv[dc]], W=[sq])
                kb.mm(S2, S2[:], ones, ones[:], sq, sq[:], dc == 0, dc == 31)
            kb.op("act", lambda e: e.mul(out=mean[:], in_=S1[:], mul=1.0 / D), R=[S1], W=[mean])
            kb.op("dve", lambda e: e.tensor_tensor(out=tmpa[:], in0=mean[:], in1=mean[:], op=ALU.mult), R=[mean], W=[tmpa])
            kb.op("dve", lambda e: e.scalar_tensor_tensor(out=tmpa[:], in0=S2[:], scalar=1.0 / D, in1=tmpa[:],
                                                          op0=ALU.mult, op1=ALU.subtract), R=[S2, tmpa], W=[tmpa])
            kb.op("act", lambda e: e.activation(out=rstd[:], in_=tmpa[:], func=AF.Ln, bias=lneps[:, 0:1]), R=[tmpa, lneps], W=[rstd])
            kb.op("act", lambda e: e.activation(out=rstd[:], in_=rstd[:], func=AF.Exp, scale=-0.5), R=[rstd], W=[rstd])
            kb.op("dve", lambda e: e.scalar_tensor_tensor(out=nmr[:], in0=mean[:], scalar=-1.0, in1=rstd[:],
                                                          op0=ALU.mult, op1=ALU.mult), R=[mean, rstd], W=[nmr])
            for dc in range(32):
                t1 = xin_r.next()
                kb.op("dve", lambda e: e.tensor_tensor(out=t1[:], in0=zv[dc][:], in1=rstd[:], op=ALU.mult), R=[zv[dc], rstd], W=[t1])
                kb.op("pool", lambda e: e.tensor_tensor(out=t1[:], in0=t1[:], in1=nmr[:], op=ALU.add), R=[t1, nmr], W=[t1])
                xc = xo_r.next()
                kb.op("act", lambda e: e.activation(out=xc[:], in_=t1[:], func=AF.Identity,
                                                    scale=lnsb[:, gi, dc:dc + 1], bias=lnsb[:, bi, dc:dc + 1]),
                      R=[t1, lnsb], W=[xc])
                sink(tt, dc, xc)
        kb.pop()

    x1a_dc = [Tk(x1a.h[dc * 128:(dc + 1) * 128, :], "x1a%d" % dc) for dc in range(32)]
    x1ab_dc = [Tk(x1ab.h[dc * 128:(dc + 1) * 128, :], "x1ab%d" % dc) for dc in range(32)]
    xout_dc = [Tk(xout.h[dc * 128:(dc + 1) * 128, :], "xout%d" % dc) for dc in range(32)]

    def sink1_factory():
        b_r = Ring(kb, 3, [128, 512], BF16, "xb")

        def sink(tt, dc, xc):
            ts = slice(tt * 512, (tt + 1) * 512)
            kb.dma("sp", x1a_dc[dc][:, ts], xc[:], R=[xc], W=[x1a_dc[dc]])
            xb = b_r.next()
            kb.op("pool", lambda e: e.tensor_copy(out=xb[:], in_=xc[:]), R=[xc], W=[xb])
            kb.dma("sp", x1ab_dc[dc][:, ts], xb[:], R=[xb], W=[x1ab_dc[dc]])
        return sink
    ln_stage(lambda tt, dc: (ysum_dc[dc], ysum_dc[dc][:, tt * 512:(tt + 1) * 512]), xin, 0, 1, sink1_factory)

    kb.push()
    actT = kb.sb([128, 32, 512], BF16, "actT")
    hid = kb.sb([128, NFH, 512], BF16, "hid")
    wring = Ring(kb, 4, [128, NFH, 128], BF16, "wf")
    sg_r = Ring(kb, 2, [128, 512], F32, "sg")
    stg = Ring(kb, 3, [128, 512], F32, "stgf")
    pi = 0
    for tt in range(8):
        ts = slice(tt * 512, (tt + 1) * 512)
        kb.dma("sp", actT[:], x1ab[:, ts].rearrange("(k p) t -> p k t", p=128), R=[x1ab], W=[actT])
        for f in range(NFH):
            wg = wring.next()
            kb.dma("pool", wg[:, 0:32, :], wg_t[f], W=[wg])
            wu = wring.next()
            kb.dma("pool", wu[:, 0:32, :], wu_t[f], W=[wu])
            pg = P[pi % 6]
            pu = P[(pi + 1) % 6]
            pi += 2
            for kc in range(32):
                kb.mm(pg, pg[:], wg, wg[:, kc, :], actT, actT[:, kc, :], kc == 0, kc == 31)
            for kc in range(32):
                kb.mm(pu, pu[:], wu, wu[:, kc, :], actT, actT[:, kc, :], kc == 0, kc == 31)
            sg = sg_r.next()
            kb.op("act", lambda e: e.activation(out=sg[:], in_=pg[:], func=AF.Silu), R=[pg], W=[sg])
            kb.op("dve", lambda e: e.tensor_tensor(out=hid[:, f, :], in0=sg[:], in1=pu[:], op=ALU.mult), R=[sg, pu], W=[hid])
        for dc in range(32):
            w = wring.next()
            kb.dma("pool", w[:], wd_t[dc], W=[w])
            acc = P[pi % 6]
            pi += 1
            for fc in range(NFH):
                kb.mm(acc, acc[:], w, w[:, fc, :], hid, hid[:, fc, :], fc == 0, fc == NFH - 1)
            s = stg.next()
            kb.op("act", lambda e: e.copy(out=s[:], in_=acc[:]), R=[acc], W=[s])
            kb.dma("sp", yp2_t[tt][dc * 128:(dc + 1) * 128, :], s[:], R=[s], W=[yp2_t[tt]])
        for i in range(4):
            kb.coll_allreduce(yp2_t[tt], yp2_t[tt][i * 1024:(i + 1) * 1024, :], ys2_t[tt], ys2_t[tt][i * 1024:(i + 1) * 1024, :])
    kb.pop()

    def sink2_factory():
        def sink(tt, dc, xc):
            ts = slice(tt * 512, (tt + 1) * 512)
            kb.dma("sp", xout_dc[dc][:, ts], xc[:], R=[xc], W=[xout_dc[dc]])
        return sink
    ln_stage(lambda tt, dc: (ys2_t[tt], ys2_t[tt][dc * 128:(dc + 1) * 128, :]), x1a, 2, 3, sink2_factory)


def build_fused():
    nc = bass.Bass("TRN2", target_bir_lowering=False)
    kb = KB(nc)
    kb.pfx = ""
    xT = kb.dram("xT", [D, SEQ], F32, "ExternalInput")
    xoT = kb.dram("xoT", [D, SEQ], F32, "ExternalOutput")
    h0 = kb.dram("h0T", [2048, SEQ], BF16)
    h1 = kb.dram("h1T", [2048, SEQ], BF16)
    x1T = kb.dram("x1T", [D, SEQ], F32)
    P = [kb.ps((128, 512), F32, "P%d" % i) for i in range(8)]
    emit_M0(kb, P, xT, h0)
    emit_F(kb, P, "f0_", h0, xT, x1T)
    emit_M1(kb, P, x1T, h1)
    emit_F(kb, P, "f1_", h1, x1T, xoT)
    kb.finish()
    return nc


def prep_F_weights(rows, half, w_o, w_gate, w_up, w_down, lnv):
    f0, f1 = half * NFH * 128, (half + 1) * NFH * 128
    return {"wo_t": tile_kxn(w_o[rows]), "wg_t": tile_kxn(w_gate[:, f0:f1]), "wu_t": tile_kxn(w_up[:, f0:f1]),
            "wd_t": tile_kxn(w_down[f0:f1]), "lnp": ln_pack(lnv)}


def kernel(x, ab_w_in, ab_b_igate, ab_b_fgate, ab_mlstm_norm, ab_q_norm, ab_kv_norm, ab_w_uq, ab_w_ukv,
           ab_w_o, c_w_in, c_b_gate, c_pe_k, c_pe_v, c_cmp_w1_k, c_cmp_w2_k, c_cmp_w1_v, c_cmp_w2_v, c_w_o,
           ffn_w_gate, ffn_w_up, ffn_w_down, ln_mix_g, ln_mix_b, ln_ffn_g, ln_ffn_b):
    f = lambda a: np.asarray(a, dtype=np.float32)
    x = f(x)
    B = x.shape[0]
    if "fused" not in _CACHE:
        _CACHE["fused"] = build_fused()
    nc = _CACHE["fused"]
    per_half = []
    c0 = prep_M0_consts()
    c1 = prep_M1_consts()
    for hh in range(2):
        m = {}
        w0 = prep_M0_weights(hh, f(ab_w_in)[0], f(ab_b_igate)[0], f(ab_b_fgate)[0], f(ab_mlstm_norm)[0], f(ab_q_norm)[0],
                             f(ab_kv_norm)[0], f(ab_w_uq)[0], f(ab_w_ukv)[0])
        for k, v in list(w0.items()) + list(c0.items()):
            m["m0_" + k] = v
        w1 = prep_M1_weights(hh, f(c_w_in)[0], f(c_b_gate)[0], f(c_pe_k)[0], f(c_pe_v)[0], f(c_cmp_w1_k)[0], f(c_cmp_w2_k)[0],
                             f(c_cmp_w1_v)[0], f(c_cmp_w2_v)[0])
        for k, v in list(w1.items()) + list(c1.items()):
            m["m1_" + k] = v
        rows0 = np.concatenate([np.arange(hh * 1024, (hh + 1) * 1024), np.arange(2048 + hh * 1024, 2048 + (hh + 1) * 1024)])
        lnv0 = [f(ln_mix_g)[0], f(ln_mix_b)[0], f(ln_ffn_g)[0], f(ln_ffn_b)[0]]
        for k, v in prep_F_weights(rows0, hh, f(ab_w_o)[0], f(ffn_w_gate)[0], f(ffn_w_up)[0], f(ffn_w_down)[0], lnv0).items():
            m["f0_" + k] = v
        rows1 = np.arange(hh * 2048, (hh + 1) * 2048)
        lnv1 = [f(ln_mix_g)[1], f(ln_mix_b)[1], f(ln_ffn_g)[1], f(ln_ffn_b)[1]]
        for k, v in prep_F_weights(rows1, hh, f(c_w_o)[0], f(ffn_w_gate)[1], f(ffn_w_up)[1], f(ffn_w_down)[1], lnv1).items():
            m["f1_" + k] = v
        per_half.append(m)
    xT = [np.ascontiguousarray(x[b].T) for b in range(B)]
    in_maps = []
    for c in range(8):
        m = dict(per_half[c % 2])
        m["xT"] = xT[c // 2]
        in_maps.append(m)
    res = run_bass_kernel_spmd(nc, in_maps, core_ids=list(range(8)))
    out = np.empty((B, SEQ, D), np.float32)
    for b in range(B):
        out[b] = np.asarray(res.results[2 * b]["xoT"]).T
    return out
```

```python
import math
from contextlib import ExitStack

import numpy as np
import ml_dtypes
import concourse.bass as bass
import concourse.mybir as mybir
from concourse.bass_utils import run_bass_kernel_spmd

F32 = mybir.dt.float32
BF16 = mybir.dt.bfloat16
AF = mybir.ActivationFunctionType
ALU = mybir.AluOpType
AX = mybir.AxisListType

D = 4096
SEQ = 4096
FF = 11008
NFC = FF // 128
ALPHA = 4.0 ** 0.25
LN_EPS = 1e-5
RMS_EPS = 1e-6
NDS = 28


class Tk:
    __slots__ = ("h", "lw", "rd", "name")

    def __init__(self, h, name):
        self.h = h
        self.lw = None
        self.rd = {}
        self.name = name

    def __getitem__(self, idx):
        return self.h[idx]


class KB:
    def __init__(self, nc):
        self.nc = nc
        self.es = ExitStack()
        self.es_root = self.es
        self.eng = {"pe": nc.tensor, "act": nc.scalar, "dve": nc.vector, "pool": nc.gpsimd, "sp": nc.sync}
        self.sem = {}
        self.cnt = {}
        self.known = {}
        for k in self.eng:
            self.sem[k] = self.es.enter_context(nc.semaphore("s_" + k))
            self.cnt[k] = 0
            self.known[k] = {}
        self.dsem = [self.es.enter_context(nc.semaphore("d%d" % i)) for i in range(NDS)]
        self.dcnt = [0] * NDS
        self.dnext = 0
        self.nid = 0

    def sb(self, shape, dt, name=None):
        self.nid += 1
        name = "%s_s%d" % (name or "sb", self.nid)
        return Tk(self.es.enter_context(self.nc.sbuf_tensor(name, list(shape), dt)), name)

    def ps(self, shape=(128, 512), dt=F32, name=None):
        self.nid += 1
        name = "%s_p%d" % (name or "ps", self.nid)
        return Tk(self.es.enter_context(self.nc.psum_tensor(name, list(shape), dt)), name)

    def dram(self, name, shape, dt, kind="Internal"):
        name = getattr(self, "pfx", "") + name
        return Tk(self.nc.dram_tensor(name, list(shape), dt, kind=kind).ap(), name)

    def coll_allreduce(self, in_tk, in_ap, out_tk, out_ap):
        if not hasattr(self, "csem"):
            self.csem = self.es_root.enter_context(self.nc.semaphore("s_cc"))
            self.ccnt = 0
        deps = self._deps("pool", [in_tk], [out_tk], True)
        self._emit_waits("pool", deps)
        ins = self.eng["pool"].collective_compute("AllReduce", ALU.add, replica_groups=[[0, 1], [2, 3], [4, 5], [6, 7]],
                                                  ins=[in_ap], outs=[out_ap])
        self.ccnt += 1
        ins.then_inc(self.csem)
        key = "cc"
        if in_tk.rd.get(key, 0) < self.ccnt:
            in_tk.rd[key] = self.ccnt
        out_tk.lw = (key, self.ccnt)
        out_tk.rd = {}

    def barrier(self):
        for e, eng in self.eng.items():
            kn = self.known[e]
            for k in self.eng:
                if k != e and kn.get(k, 0) < self.cnt[k]:
                    eng.wait_ge(self.sem[k], self.cnt[k])
                    kn[k] = self.cnt[k]
            for i in range(NDS):
                key = ("d", i)
                if kn.get(key, 0) < self.dcnt[i]:
                    eng.wait_ge(self.dsem[i], self.dcnt[i])
                    kn[key] = self.dcnt[i]
            if getattr(self, "ccnt", 0) > kn.get("cc", 0):
                eng.wait_ge(self.csem, self.ccnt)
                kn["cc"] = self.ccnt

    def push(self):
        self._outer = getattr(self, "_outer", [])
        self._outer.append(self.es)
        self.es = ExitStack()

    def pop(self):
        self.barrier()
        self.es.close()
        self.es = self._outer.pop()

    def semof(self, key):
        if isinstance(key, tuple):
            return self.dsem[key[1]]
        if key == "cc":
            return self.csem
        return self.sem[key]

    def _deps(self, e, R, W, is_dma):
        deps = {}

        def add(tok, same_ok):
            if tok is None:
                return
            key, val = tok
            if key == e and not same_ok and not is_dma:
                return
            if deps.get(key, 0) < val:
                deps[key] = val

        for t in R:
            add(t.lw, e != "pe")
        for t in W:
            add(t.lw, False)
            for k, v in t.rd.items():
                add((k, v), False)
        return deps

    def _emit_waits(self, e, deps):
        eng = self.eng[e]
        kn = self.known[e]
        for key, val in deps.items():
            if kn.get(key, 0) >= val:
                continue
            eng.wait_ge(self.semof(key), val)
            kn[key] = val

    def op(self, e, fn, R=(), W=(), sig=True):
        deps = self._deps(e, R, W, False)
        self._emit_waits(e, deps)
        ins = fn(self.eng[e])
        if sig:
            self.cnt[e] += 1
            ins.then_inc(self.sem[e], 1)
            tv = self.cnt[e]
        else:
            tv = self.cnt[e] + 1
        for t in R:
            if t.rd.get(e, 0) < tv:
                t.rd[e] = tv
        for t in W:
            t.lw = (e, tv)
            t.rd = {}

    def dma(self, q, out, in_, R=(), W=(), transpose=False):
        si = self.dnext
        self.dnext = (self.dnext + 1) % NDS
        key = ("d", si)
        deps = self._deps(q, R, W, True)
        if self.dcnt[si] > 0:
            deps[key] = max(deps.get(key, 0), self.dcnt[si])
        self._emit_waits(q, deps)
        eng = self.eng[q]
        if transpose:
            ins = eng.dma_start_transpose(out=out, in_=in_)
        else:
            ins = eng.dma_start(out=out, in_=in_)
        self.dcnt[si] += 16
        ins.then_inc(self.dsem[si], 16)
        tv = self.dcnt[si]
        for t in R:
            if t.rd.get(key, 0) < tv:
                t.rd[key] = tv
        for t in W:
            t.lw = (key, tv)
            t.rd = {}

    def finish(self):
        eng = self.eng["sp"]
        for i in range(NDS):
            if self.dcnt[i] > 0:
                eng.wait_ge(self.dsem[i], self.dcnt[i])
        for k in self.eng:
            if k != "sp" and self.cnt[k] > 0:
                eng.wait_ge(self.sem[k], self.cnt[k])
        if getattr(self, "ccnt", 0) > 0:
            eng.wait_ge(self.csem, self.ccnt)
        self.es.close()

    def mm(self, out_t, out_ap, lhsT_t, lhsT_ap, rhs_t, rhs_ap, start, stop, sig=None):
        if sig is None:
            sig = True
        self.op("pe", lambda e: e.matmul(out_ap, lhsT=lhsT_ap, rhs=rhs_ap, start=start, stop=stop),
                R=[lhsT_t, rhs_t], W=[out_t], sig=sig)


def pipelined(n, emit_s, emit_rest, look=2):
    pend = {}
    for i in range(min(look, n)):
        pend[i] = emit_s(i)
    for i in range(n):
        emit_rest(i, pend.pop(i))
        if i + look < n:
            pend[i + look] = emit_s(i + look)


class Ring:
    def __init__(self, kb, n, shape, dt, name):
        self.bufs = [kb.sb(shape, dt, "%s%d" % (name, i)) for i in range(n)]
        self.i = 0

    def next(self):
        b = self.bufs[self.i]
        self.i = (self.i + 1) % len(self.bufs)
        return b


class PRing:
    def __init__(self, kb, n, name):
        self.bufs = [kb.ps((128, 512), F32, "%s%d" % (name, i)) for i in range(n)]
        self.i = 0

    def next(self):
        b = self.bufs[self.i]
        self.i = (self.i + 1) % len(self.bufs)
        return b


def tile_kxn(w, ncol=128):
    K, N = w.shape
    return np.ascontiguousarray(w.reshape(K // 128, 128, N // ncol, ncol).transpose(2, 1, 0, 3))


def ln_pack(vecs):
    return np.ascontiguousarray(np.stack([v.reshape(32, 128).T for v in vecs], axis=1))


_CACHE = {}


M0_NW = 41
MLA_SCALE = 192.0 ** -0.5


def emit_M0(kb, P, xT, hTo):
    S = SEQ
    kb.pfx = "m0_"
    kb.push()
    wA = kb.dram("wA", [M0_NW, 128, 32, 128], F32, "ExternalInput")
    wuq = kb.dram("wuq", [128, 8, 2048], F32, "ExternalInput")
    wukv_k = kb.dram("wukv_k", [128, 4, 1024], F32, "ExternalInput")
    wukv_v = kb.dram("wukv_v", [128, 4, 1024], F32, "ExternalInput")
    smallp = kb.dram("smallp", [128, 24], F32, "ExternalInput")
    cosT = kb.dram("cosT", [64, S], F32, "ExternalInput")
    sinT = kb.dram("sinT", [64, S], F32, "ExternalInput")
    dmask = kb.dram("dmask", [4, 128, 512], BF16, "ExternalInput")
    ident_d = kb.dram("ident", [128, 128], F32, "ExternalInput")
    qaT = kb.dram("qaT", [512, S], BF16)
    kaT = kb.dram("kaT", [512, S], BF16)
    ogT = kb.dram("ogT", [1024, S], BF16)
    cqT = kb.dram("cqT", [1024, S], BF16)
    ckvT = kb.dram("ckvT", [512, S], BF16)
    krT = kb.dram("krT", [128, S], BF16)
    va = kb.dram("va", [S, 1024], BF16)
    grep = kb.dram("grep", [4, 128, S], F32)
    qT = kb.dram("qT", [8, 192, S], BF16)
    knT = kb.dram("knT", [8, 128, S], BF16)
    kroT = kb.dram("kroT", [64, S], BF16)
    vmla = kb.dram("vmla", [S, 1024], BF16)
    fm_dst = [(qaT, i) for i in range(4)] + [(kaT, i) for i in range(4)] + [(ogT, i) for i in range(8)] + \
             [(cqT, i) for i in range(8)] + [(ckvT, i) for i in range(4)] + [(krT, 0)]

    onesf = kb.sb([128, 128], F32, "onesf")
    onesb = kb.sb([128, 128], BF16, "onesb")
    kb.op("dve", lambda e: e.memset(onesf[:], 1.0), W=[onesf])
    kb.op("dve", lambda e: e.memset(onesb[:], 1.0), W=[onesb])
    sp_ = kb.sb([128, 24], F32, "smallp_sb")
    kb.dma("sp", sp_[:], smallp[:], W=[sp_])
    masks = kb.sb([128, 4, 512], BF16, "masks")
    kb.dma("sp", masks[:], dmask[:].rearrange("m p t -> p m t"), W=[masks])
    ident = kb.sb([128, 128], F32, "ident_sb")
    kb.dma("sp", ident[:], ident_d[:], W=[ident])

    kb.push()
    xTb = kb.sb([128, 32, 2048], BF16, "xTb")
    wring = Ring(kb, 6, [128, 32, 128], BF16, "wA")
    stg = Ring(kb, 3, [128, 512], BF16, "stgA")
    stgf = Ring(kb, 2, [128, 512], F32, "stgAf")
    pi = 0
    for st in range(2):
        for q4 in range(4):
            c0 = st * 2048 + q4 * 512
            kb.dma("pool", xTb[:, :, q4 * 512:(q4 + 1) * 512],
                   xT[:, c0:c0 + 512].rearrange("(k p) t -> p k t", p=128), W=[xTb])
        for wi in list(range(29)) + list(range(37, 41)):
            w = wring.next()
            kb.dma("pool", w[:], wA[wi], W=[w])
            for q4 in range(4):
                acc = P[pi % 4]
                pi += 1
                for kc in range(32):
                    kb.mm(acc, acc[:], w, w[:, kc, :], xTb, xTb[:, kc, q4 * 512:(q4 + 1) * 512], kc == 0, kc == 31)
                c0 = st * 2048 + q4 * 512
                if wi < 29:
                    dst, ci = fm_dst[wi]
                    s = stg.next()
                    kb.op("act", lambda e: e.copy(out=s[:], in_=acc[:]), R=[acc], W=[s])
                    kb.dma("sp", dst[ci * 128:(ci + 1) * 128, c0:c0 + 512], s[:], R=[s], W=[dst])
                else:
                    s = stgf.next()
                    kb.op("act", lambda e: e.copy(out=s[:], in_=acc[:]), R=[acc], W=[s])
                    kb.dma("sp", grep[wi - 37, :, c0:c0 + 512], s[:], R=[s], W=[grep])
        for g2 in range(2):
            ws = []
            for j in range(4):
                w = wring.next()
                kb.dma("pool", w[:], wA[29 + g2 * 4 + j], W=[w])
                ws.append(w)
            for tc in range(16):
                acc = P[pi % 4]
                pi += 1
                for j in range(4):
                    for kc in range(32):
                        kb.mm(acc, acc[:, j * 128:(j + 1) * 128], xTb, xTb[:, kc, tc * 128:(tc + 1) * 128],
                              ws[j], ws[j][:, kc, :], kc == 0, kc == 31)
                s = stg.next()
                kb.op("act", lambda e: e.copy(out=s[:], in_=acc[:]), R=[acc], W=[s])
                t0 = st * 2048 + tc * 128
                kb.dma("sp", va[t0:t0 + 128, g2 * 512:(g2 + 1) * 512], s[:], R=[s], W=[va])
    kb.pop()

    kb.push()
    wuq_s = kb.sb([128, 8, 2048], BF16, "wuq")
    wk_s = kb.sb([128, 4, 1024], BF16, "wk")
    wv_s = kb.sb([128, 4, 1024], BF16, "wv")
    kb.dma("pool", wuq_s[:], wuq[:], W=[wuq_s])
    kb.dma("pool", wk_s[:], wukv_k[:], W=[wk_s])
    kb.dma("pool", wv_s[:], wukv_v[:], W=[wv_s])
    cq_r = Ring(kb, 2, [128, 8, 512], BF16, "cq")
    ckv_r = Ring(kb, 2, [128, 4, 512], BF16, "ckv")
    cqn = kb.sb([128, 8, 512], BF16, "cqn")
    ckvn = kb.sb([128, 4, 512], BF16, "ckvn")
    sq_r = Ring(kb, 2, [128, 512], F32, "sqB")
    rq = kb.sb([128, 512], F32, "rq")
    rkv = kb.sb([128, 512], F32, "rkv")
    cos_r = Ring(kb, 2, [64, 512], F32, "cos")
    sin_r = Ring(kb, 2, [64, 512], F32, "sin")
    krA_r = Ring(kb, 2, [64, 512], BF16, "krA")
    krB_r = Ring(kb, 2, [64, 512], BF16, "krB")
    t1_r = Ring(kb, 2, [64, 512], F32, "t1")
    t2_r = Ring(kb, 2, [64, 512], F32, "t2")
    stg = Ring(kb, 4, [128, 512], BF16, "stgB")
    pi = 0
    for tt in range(8):
        ts = slice(tt * 512, (tt + 1) * 512)
        cq = cq_r.next()
        kb.dma("sp", cq[:], cqT[:, ts].rearrange("(c p) t -> p c t", p=128), R=[cqT], W=[cq])
        ckv = ckv_r.next()
        kb.dma("sp", ckv[:], ckvT[:, ts].rearrange("(c p) t -> p c t", p=128), R=[ckvT], W=[ckv])
        cs = cos_r.next()
        sn = sin_r.next()
        kb.dma("sp", cs[:], cosT[:, ts], W=[cs])
        kb.dma("sp", sn[:], sinT[:, ts], W=[sn])
        for (src, nch, dstr, nf) in ((cq, 8, rq, 1024.0), (ckv, 4, rkv, 512.0)):
            acc = P[4]
            for c in range(nch):
                sq = sq_r.next()
                kb.op("act", lambda e: e.activation(out=sq[:], in_=src[:, c, :], func=AF.Square), R=[src], W=[sq])
                kb.mm(acc, acc[:], onesf, onesf[:], sq, sq[:], c == 0, c == nch - 1)
            kb.op("dve", lambda e: e.tensor_scalar(out=dstr[:], in0=acc[:], scalar1=1.0 / nf, scalar2=RMS_EPS,
                                                   op0=ALU.mult, op1=ALU.add), R=[acc], W=[dstr])
            kb.op("act", lambda e: e.activation(out=dstr[:], in_=dstr[:], func=AF.Ln), R=[dstr], W=[dstr])
            kb.op("act", lambda e: e.activation(out=dstr[:], in_=dstr[:], func=AF.Exp, scale=-0.5), R=[dstr], W=[dstr])
        for c in range(8):
            kb.op("pool", lambda e: e.tensor_scalar(out=cqn[:, c, :], in0=cq[:, c, :], scalar1=sp_[:, 12 + c:13 + c],
                                                    scalar2=None, op0=ALU.mult), R=[cq, sp_], W=[cqn])
        for c in range(4):
            kb.op("dve", lambda e: e.scalar_tensor_tensor(out=ckvn[:, c, :], in0=ckv[:, c, :], scalar=sp_[:, 20 + c:21 + c],
                                                          in1=rkv[:], op0=ALU.mult, op1=ALU.mult),
                  R=[ckv, sp_, rkv], W=[ckvn])
        krA = krA_r.next()
        krB = krB_r.next()
        kb.dma("sp", krA[:], krT[0:64, ts], R=[krT], W=[krA])
        kb.dma("sp", krB[:], krT[64:128, ts], R=[krT], W=[krB])
        t1 = t1_r.next()
        t2 = t2_r.next()
        kb.op("dve", lambda e: e.tensor_tensor(out=t1[:], in0=krA[:], in1=cs[:], op=ALU.mult), R=[krA, cs], W=[t1])
        kb.op("pool", lambda e: e.tensor_tensor(out=t2[:], in0=krB[:], in1=sn[:], op=ALU.mult), R=[krB, sn], W=[t2])
        s = stg.next()
        kb.op("dve", lambda e: e.tensor_tensor(out=s[0:64, :], in0=t1[:], in1=t2[:], op=ALU.add), R=[t1, t2], W=[s])
        kb.dma("sp", kroT[:, ts], s[0:64, :], R=[s], W=[kroT])
        for h in range(8):
            acc = P[pi % 4]
            pi += 1
            for c in range(8):
                kb.mm(acc, acc[:], wuq_s, wuq_s[:, c, h * 256:h * 256 + 128], cqn, cqn[:, c, :], c == 0, c == 7)
            s = stg.next()
            kb.op("dve", lambda e: e.tensor_tensor(out=s[:], in0=acc[:], in1=rq[:], op=ALU.mult), R=[acc, rq], W=[s])
            kb.dma("sp", qT[h, 0:128, ts], s[:], R=[s], W=[qT])
            a1 = P[5]
            a2 = P[6]
            for c in range(8):
                kb.mm(a1, a1[0:64, :], wuq_s, wuq_s[:, c, h * 256 + 128:h * 256 + 192], cqn, cqn[:, c, :], c == 0, c == 7)
            for c in range(8):
                kb.mm(a2, a2[0:64, :], wuq_s, wuq_s[:, c, h * 256 + 192:h * 256 + 256], cqn, cqn[:, c, :], c == 0, c == 7)
            t1 = t1_r.next()
            t2 = t2_r.next()
            kb.op("dve", lambda e: e.tensor_tensor(out=t1[:], in0=a1[0:64, :], in1=cs[:], op=ALU.mult), R=[a1, cs], W=[t1])
            kb.op("dve", lambda e: e.tensor_tensor(out=t2[:], in0=a2[0:64, :], in1=sn[:], op=ALU.mult), R=[a2, sn], W=[t2])
            kb.op("pool", lambda e: e.tensor_tensor(out=t1[:], in0=t1[:], in1=t2[:], op=ALU.add), R=[t1, t2], W=[t1])
            s = stg.next()
            kb.op("dve", lambda e: e.tensor_tensor(out=s[0:64, :], in0=t1[:], in1=rq[0:64, :], op=ALU.mult), R=[t1, rq], W=[s])
            kb.dma("sp", qT[h, 128:192, ts], s[0:64, :], R=[s], W=[qT])
            acc = P[pi % 4]
            pi += 1
            for c in range(4):
                kb.mm(acc, acc[:], wk_s, wk_s[:, c, h * 128:(h + 1) * 128], ckvn, ckvn[:, c, :], c == 0, c == 3)
            s = stg.next()
            kb.op("act", lambda e: e.copy(out=s[:], in_=acc[:]), R=[acc], W=[s])
            kb.dma("sp", knT[h, :, ts], s[:], R=[s], W=[knT])
        for tc in range(4):
            for hf in range(2):
                acc = P[pi % 4]
                pi += 1
                for c in range(4):
                    kb.mm(acc, acc[:], ckvn, ckvn[:, c, tc * 128:(tc + 1) * 128], wv_s, wv_s[:, c, hf * 512:(hf + 1) * 512],
                          c == 0, c == 3)
                s = stg.next()
                kb.op("act", lambda e: e.copy(out=s[:], in_=acc[:]), R=[acc], W=[s])
                t0 = tt * 512 + tc * 128
                kb.dma("sp", vmla[t0:t0 + 128, hf * 512:(hf + 1) * 512], s[:], R=[s], W=[vmla])
    kb.pop()

    kb.push()
    kro = kb.sb([64, S], BF16, "kro")
    kb.dma("sp", kro[:], kroT[:], R=[kroT], W=[kro])
    kn_r = Ring(kb, 2, [128, S], BF16, "kn")
    v_r = Ring(kb, 2, [128, 32, 128], BF16, "vC")
    qn_r = Ring(kb, 2, [128, 512], BF16, "qn")
    qr_r = Ring(kb, 2, [64, 512], BF16, "qr")
    pT_r = Ring(kb, 5, [128, 512], BF16, "pT")
    rr_r = Ring(kb, 2, [128, 512], F32, "rrC")
    ho_r = Ring(kb, 2, [128, 512], BF16, "hoC")
    si = 0
    for h in range(8):
        kn = kn_r.next()
        kb.dma("sp", kn[:], knT[h], R=[knT], W=[kn])
        v = v_r.next()
        kb.dma("sp", v[:], vmla[:, h * 128:(h + 1) * 128].rearrange("(c p) d -> p c d", p=128), R=[vmla], W=[v])
        for j in range(8):
            ts = slice(j * 512, (j + 1) * 512)
            qn = qn_r.next()
            qr = qr_r.next()
            kb.dma("sp", qn[:], qT[h, 0:128, ts], R=[qT], W=[qn])
            kb.dma("sp", qr[:], qT[h, 128:192, ts], R=[qT], W=[qr])
            O = P[4 + (j % 2)]
            L = P[6 + (j % 2)]
            nk = 4 * (j + 1)
            def emit_s(kc):
                nonlocal si
                sps = P[si % 4]
                si += 1
                ks = slice(kc * 128, (kc + 1) * 128)
                kb.mm(sps, sps[:], kn, kn[:, ks], qn, qn[:], True, False)
                kb.mm(sps, sps[:], kro, kro[:, ks], qr, qr[:], False, True)
                return sps

            def emit_rest(kc, sps):
                pT = pT_r.next()
                kb.op("act", lambda e: e.activation(out=pT[:], in_=sps[:], func=AF.Exp, scale=MLA_SCALE), R=[sps], W=[pT])
                m = kc - 4 * j
                if m >= 0:
                    kb.op("pool", lambda e: e.tensor_tensor(out=pT[:], in0=pT[:], in1=masks[:, m, :], op=ALU.mult),
                          R=[pT, masks], W=[pT])
                kb.mm(O, O[:], v, v[:, kc, :], pT, pT[:], kc == 0, kc == nk - 1)
                kb.mm(L, L[:], onesb, onesb[:], pT, pT[:], kc == 0, kc == nk - 1)
            pipelined(nk, emit_s, emit_rest, look=3)
            rr = rr_r.next()
            kb.op("dve", lambda e: e.reciprocal(out=rr[:], in_=L[:]), R=[L], W=[rr])
            ho = ho_r.next()
            kb.op("dve", lambda e: e.tensor_tensor(out=ho[:], in0=O[:], in1=rr[:], op=ALU.mult), R=[O, rr], W=[ho])
            kb.dma("sp", hTo[1024 + h * 128:1024 + (h + 1) * 128, ts], ho[:], R=[ho], W=[hTo])
    kb.pop()

    kb.push()
    ones4k = kb.sb([128, S], F32, "ones4k")
    kb.op("pool", lambda e: e.memset(ones4k[:], 1.0), W=[ones4k])
    gbias = kb.sb([128, 4], F32, "gbias")
    kb.op("dve", lambda e: e.tensor_scalar(out=gbias[:], in0=sp_[:, 0:4], scalar1=1.0 / 15.0, scalar2=None, op0=ALU.mult),
          R=[sp_], W=[gbias])
    ga = kb.sb([128, S], F32, "ga")
    gb = kb.sb([128, S], F32, "gb")
    nA = kb.sb([128, S], F32, "nA")
    e3 = kb.sb([128, S], F32, "e3")
    acol = kb.sb([128, 32], F32, "acol")
    kaS = kb.sb([128, 2, S], BF16, "kaS")
    vaS = kb.sb([128, 32, 512], BF16, "vaS")
    qa_r = Ring(kb, 2, [128, 2, 512], BF16, "qa")
    og_r = Ring(kb, 2, [128, 4, 512], BF16, "og")
    dm_r = Ring(kb, 2, [128, 512], F32, "dm")
    pT_r = Ring(kb, 3, [128, 512], BF16, "pTD")
    rr = kb.sb([128, 512], F32, "rrD")
    h0 = kb.sb([128, 4, 512], F32, "h0")
    sq_r = Ring(kb, 2, [128, 512], F32, "sqD")
    sg_r = Ring(kb, 2, [128, 512], F32, "sgD")
    ho_r = Ring(kb, 2, [128, 512], BF16, "hoD")
    si = 0
    for hd in range(2):
        kb.dma("sp", ga[:], grep[hd], R=[grep], W=[ga])
        kb.dma("sp", gb[:], grep[2 + hd], R=[grep], W=[gb])
        kb.dma("sp", kaS[:], kaT[hd * 256:(hd + 1) * 256, :].rearrange("(c p) t -> p c t", p=128), R=[kaT], W=[kaS])
        kb.dma("sp", vaS[:], va[:, hd * 512:(hd + 1) * 512].rearrange("(c p) d -> p c d", p=128), R=[va], W=[vaS])
        kb.op("act", lambda e: e.activation(out=ga[:], in_=ga[:], func=AF.Tanh, scale=1.0 / 15.0, bias=gbias[:, hd:hd + 1]),
              R=[ga, gbias], W=[ga])
        kb.op("pool", lambda e: e.tensor_scalar(out=ga[:], in0=ga[:], scalar1=15.0, scalar2=None, op0=ALU.mult), R=[ga], W=[ga])
        kb.op("act", lambda e: e.activation(out=gb[:], in_=gb[:], func=AF.Tanh, scale=1.0 / 15.0, bias=gbias[:, 2 + hd:3 + hd]),
              R=[gb, gbias], W=[gb])
        kb.op("act", lambda e: e.activation(out=gb[:], in_=gb[:], func=AF.Exp, scale=-15.0), R=[gb], W=[gb])
        kb.op("act", lambda e: e.activation(out=gb[:], in_=gb[:], func=AF.Ln, bias=1.0), R=[gb], W=[gb])
        kb.op("dve", lambda e: e.tensor_tensor_scan(out=gb[:], data0=ones4k[:], data1=gb[:], initial=0.0,
                                                    op0=ALU.mult, op1=ALU.add), R=[ones4k, gb], W=[gb])
        kb.op("dve", lambda e: e.tensor_tensor(out=ga[:], in0=ga[:], in1=gb[:], op=ALU.add), R=[ga, gb], W=[ga])
        kb.op("dve", lambda e: e.tensor_tensor_scan(out=nA[:], data0=ga[:], data1=ga[:], initial=0.0,
                                                    op0=ALU.max, op1=ALU.max), R=[ga], W=[nA])
        kb.op("pool", lambda e: e.tensor_scalar(out=nA[:], in0=nA[:], scalar1=-1.0, scalar2=None, op0=ALU.mult), R=[nA], W=[nA])
        kb.op("dve", lambda e: e.tensor_tensor(out=e3[:], in0=gb[:], in1=nA[:], op=ALU.add), R=[gb, nA], W=[e3])
        kb.op("act", lambda e: e.activation(out=e3[:], in_=e3[:], func=AF.Exp), R=[e3], W=[e3])
        for c in range(32):
            tp = P[si % 3]
            si += 1
            kb.op("pe", lambda e: e.transpose(out=tp[:, 0:128], in_=ga[:, c * 128:(c + 1) * 128], identity=ident[:]),
                  R=[ga, ident], W=[tp])
            kb.op("dve", lambda e: e.tensor_scalar(out=acol[:, c:c + 1], in0=tp[:, 0:1], scalar1=-math.log(16.0), scalar2=None,
                                                   op0=ALU.add), R=[tp], W=[acol])
        for j in range(8):
            ts = slice(j * 512, (j + 1) * 512)
            qa = qa_r.next()
            kb.dma("sp", qa[:], qaT[hd * 256:(hd + 1) * 256, ts].rearrange("(c p) t -> p c t", p=128), R=[qaT], W=[qa])
            og = og_r.next()
            kb.dma("sp", og[:], ogT[hd * 512:(hd + 1) * 512, ts].rearrange("(c p) t -> p c t", p=128), R=[ogT], W=[og])
            O = [P[3], P[4], P[5], P[6]]
            L = P[7]
            nk = 4 * (j + 1)
            def emit_s(kc):
                nonlocal si
                sps = P[si % 3]
                si += 1
                ks = slice(kc * 128, (kc + 1) * 128)
                kb.mm(sps, sps[:], kaS, kaS[:, 0, ks], qa, qa[:, 0, :], True, False)
                kb.mm(sps, sps[:], kaS, kaS[:, 1, ks], qa, qa[:, 1, :], False, True)
                return sps

            def emit_rest(kc, sps):
                dm = dm_r.next()
                kb.op("act", lambda e: e.activation(out=dm[:], in_=nA[:, ts], func=AF.Exp, bias=acol[:, kc:kc + 1]),
                      R=[nA, acol], W=[dm])
                m = kc - 4 * j
                if m >= 0:
                    kb.op("pool", lambda e: e.tensor_tensor(out=dm[:], in0=dm[:], in1=masks[:, m, :], op=ALU.mult),
                          R=[dm, masks], W=[dm])
                pT = pT_r.next()
                kb.op("dve", lambda e: e.tensor_tensor(out=pT[:], in0=sps[:], in1=dm[:], op=ALU.mult), R=[sps, dm], W=[pT])
                for c in range(4):
                    kb.mm(O[c], O[c][:], vaS, vaS[:, kc, c * 128:(c + 1) * 128], pT, pT[:], kc == 0, kc == nk - 1)
                kb.mm(L, L[:], onesb, onesb[:], pT, pT[:], kc == 0, kc == nk - 1)
            pipelined(nk, emit_s, emit_rest, look=2)
            kb.op("act", lambda e: e.activation(out=rr[:], in_=L[:], func=AF.Abs), R=[L], W=[rr])
            kb.op("dve", lambda e: e.tensor_tensor(out=rr[:], in0=rr[:], in1=e3[:, ts], op=ALU.max), R=[rr, e3], W=[rr])
            kb.op("dve", lambda e: e.reciprocal(out=rr[:], in_=rr[:]), R=[rr], W=[rr])
            ssp = P[si % 3]
            si += 1
            for c in range(4):
                kb.op("dve", lambda e: e.tensor_tensor(out=h0[:, c, :], in0=O[c][:], in1=rr[:], op=ALU.mult),
                      R=[O[c], rr], W=[h0])
                sq = sq_r.next()
                kb.op("act", lambda e: e.activation(out=sq[:], in_=h0[:, c, :], func=AF.Square), R=[h0], W=[sq])
                kb.mm(ssp, ssp[:], onesf, onesf[:], sq, sq[:], c == 0, c == 3)
            kb.op("dve", lambda e: e.tensor_scalar(out=rr[:], in0=ssp[:], scalar1=1.0 / 512.0, scalar2=RMS_EPS,
                                                   op0=ALU.mult, op1=ALU.add), R=[ssp], W=[rr])
            kb.op("act", lambda e: e.activation(out=rr[:], in_=rr[:], func=AF.Ln), R=[rr], W=[rr])
            kb.op("act", lambda e: e.activation(out=rr[:], in_=rr[:], func=AF.Exp, scale=-0.5), R=[rr], W=[rr])
            for c in range(4):
                sg = sg_r.next()
                kb.op("act", lambda e: e.activation(out=sg[:], in_=og[:, c, :], func=AF.Sigmoid), R=[og], W=[sg])
                kb.op("pool", lambda e: e.tensor_tensor(out=sg[:], in0=sg[:], in1=rr[:], op=ALU.mult), R=[sg, rr], W=[sg])
                ho = ho_r.next()
                gi = 4 + hd * 4 + c
                kb.op("dve", lambda e: e.scalar_tensor_tensor(out=ho[:], in0=h0[:, c, :], scalar=sp_[:, gi:gi + 1], in1=sg[:],
                                                              op0=ALU.mult, op1=ALU.mult), R=[h0, sp_, sg], W=[ho])
                r0 = hd * 512 + c * 128
                kb.dma("sp", hTo[r0:r0 + 128, ts], ho[:], R=[ho], W=[hTo])
    kb.pop()
    kb.pop()


AB_OFF = {"q_a": 0, "k_a": 1024, "v_a": 2048, "ig": 4096, "fg": 4100, "og": 4104, "cq": 6152, "ckv": 7176, "kr": 7688}


def prep_M0_consts():
    pos = np.arange(SEQ, dtype=np.float32)
    inv = (10000.0 ** (-np.arange(32, dtype=np.float32) / 32)).astype(np.float32)
    ang = pos[None, :] * inv[:, None]
    cos, sin = np.cos(ang).astype(np.float32), np.sin(ang).astype(np.float32)
    cosT = np.concatenate([cos, cos], 0)
    sinT = np.concatenate([-sin, sin], 0)
    kl = np.arange(128)[:, None]
    ql = np.arange(512)[None, :]
    dmask = np.stack([(128 * m + kl <= ql) for m in range(4)]).astype(np.float32).astype(ml_dtypes.bfloat16)
    return {"cosT": np.ascontiguousarray(cosT), "sinT": np.ascontiguousarray(sinT), "dmask": dmask,
            "ident": np.eye(128, dtype=np.float32)}


def prep_M0_weights(hh, w_in, b_ig, b_fg, mnorm, qnorm, kvnorm, w_uq, w_ukv):
    o = AB_OFF
    cols = []
    cols.append(w_in[:, o["q_a"] + hh * 512:o["q_a"] + hh * 512 + 512])
    cols.append(w_in[:, o["k_a"] + hh * 512:o["k_a"] + hh * 512 + 512])
    cols.append(w_in[:, o["og"] + hh * 1024:o["og"] + hh * 1024 + 1024])
    cols.append(w_in[:, o["cq"]:o["cq"] + 1024])
    cols.append(w_in[:, o["ckv"]:o["ckv"] + 512])
    kr = w_in[:, o["kr"]:o["kr"] + 64]
    cols.append(kr)
    cols.append(np.concatenate([kr[:, 32:64], kr[:, 0:32]], 1))
    cols.append(w_in[:, o["v_a"] + hh * 1024:o["v_a"] + hh * 1024 + 1024])
    for c in (o["ig"] + 2 * hh, o["ig"] + 2 * hh + 1, o["fg"] + 2 * hh, o["fg"] + 2 * hh + 1):
        cols.append(np.repeat(w_in[:, c:c + 1], 128, axis=1))
    wA = tile_kxn(np.concatenate(cols, 1))
    assert wA.shape[0] == M0_NW
    hs = range(hh * 8, hh * 8 + 8)
    uq = []
    for h in hs:
        blk = w_uq[:, h * 192:(h + 1) * 192]
        uq += [blk[:, 0:128], blk[:, 128:192], blk[:, 160:192], blk[:, 128:160]]
    uq = np.concatenate(uq, 1)
    wuq = np.ascontiguousarray(uq.reshape(8, 128, 2048).transpose(1, 0, 2))
    kvk = np.concatenate([w_ukv[:, h * 256:h * 256 + 128] for h in hs], 1)
    kvv = np.concatenate([w_ukv[:, h * 256 + 128:h * 256 + 256] for h in hs], 1)
    wk = np.ascontiguousarray(kvk.reshape(4, 128, 1024).transpose(1, 0, 2))
    wv = np.ascontiguousarray(kvv.reshape(4, 128, 1024).transpose(1, 0, 2))
    sp = np.zeros((128, 24), np.float32)
    sp[:, 0] = b_ig[2 * hh]
    sp[:, 1] = b_ig[2 * hh + 1]
    sp[:, 2] = b_fg[2 * hh]
    sp[:, 3] = b_fg[2 * hh + 1]
    sp[:, 4:12] = mnorm[hh * 1024:(hh + 1) * 1024].reshape(8, 128).T
    sp[:, 12:20] = qnorm.reshape(8, 128).T
    sp[:, 20:24] = kvnorm.reshape(4, 128).T
    return {"wA": wA, "wuq": wuq, "wukv_k": wk, "wukv_v": wv, "smallp": sp}


M1_NW = 29
NSA_SCALE = 128.0 ** -0.5
GELU_C = 1.5957691216057308


def emit_M1(kb, P, xT, hTo):
    S = SEQ
    kb.pfx = "m1_"
    kb.push()
    wA = kb.dram("wA", [M1_NW, 128, 32, 128], F32, "ExternalInput")
    w1k = kb.dram("w1k", [128, 32, 256], F32, "ExternalInput")
    w1v = kb.dram("w1v", [128, 32, 256], F32, "ExternalInput")
    w2k = kb.dram("w2k", [128, 2, 128], F32, "ExternalInput")
    w2v = kb.dram("w2v", [128, 2, 128], F32, "ExternalInput")
    peT = kb.dram("peT", [128, 2, 32], F32, "ExternalInput")
    gbias = kb.dram("gbias", [128, 1], F32, "ExternalInput")
    cmaskT = kb.dram("cmaskT", [256, S], BF16, "ExternalInput")
    ovl = kb.dram("ovl", [256, 64], BF16, "ExternalInput")
    KM = kb.dram("KM", [S, 64], F32, "ExternalInput")
    AM = kb.dram("AM", [S, 64], F32, "ExternalInput")
    Emat = kb.dram("Emat", [64, S], BF16, "ExternalInput")
    dmask = kb.dram("dmask", [4, 128, 512], BF16, "ExternalInput")
    wmask = kb.dram("wmask", [8, 128, 512], BF16, "ExternalInput")
    selm = kb.dram("selm", [48, 48, 128], F32, "ExternalInput")
    ident_d = kb.dram("ident", [128, 128], F32, "ExternalInput")
    qT = kb.dram("qTs", [2048, S], BF16)
    kcT = kb.dram("kcT", [256, S], BF16)
    vcT = kb.dram("vcT", [256, S], BF16)
    ksT = kb.dram("ksT", [256, S], BF16)
    kwT = kb.dram("kwT", [256, S], BF16)
    gsT = kb.dram("gsT", [128, S], F32)
    vsw = kb.dram("vsw", [S, 512], BF16)
    fm_dst = [(qT, i) for i in range(16)] + [(kcT, 0), (kcT, 1), (vcT, 0), (vcT, 1), (ksT, 0), (ksT, 1), (kwT, 0), (kwT, 1)]

    onesb = kb.sb([128, 128], BF16, "onesb")
    kb.op("dve", lambda e: e.memset(onesb[:], 1.0), W=[onesb])
    gb_s = kb.sb([128, 1], F32, "gb_s")
    kb.dma("sp", gb_s[:], gbias[:], W=[gb_s])
    masks = kb.sb([128, 4, 512], BF16, "masks")
    kb.dma("sp", masks[:], dmask[:].rearrange("m p t -> p m t"), W=[masks])
    wmasks = kb.sb([128, 8, 512], BF16, "wmasks")
    kb.dma("sp", wmasks[:], wmask[:].rearrange("m p t -> p m t"), W=[wmasks])
    ident = kb.sb([128, 128], F32, "ident")
    kb.dma("sp", ident[:], ident_d[:], W=[ident])

    kb.push()
    xTb = kb.sb([128, 32, 2048], BF16, "xTb")
    wring = Ring(kb, 6, [128, 32, 128], BF16, "wA")
    stg = Ring(kb, 3, [128, 512], BF16, "stgA")
    stgf = Ring(kb, 2, [128, 512], F32, "stgAf")
    pi = 0
    for st in range(2):
        for q4 in range(4):
            c0 = st * 2048 + q4 * 512
            kb.dma("pool", xTb[:, :, q4 * 512:(q4 + 1) * 512],
                   xT[:, c0:c0 + 512].rearrange("(k p) t -> p k t", p=128), W=[xTb])
        for wi in range(25):
            w = wring.next()
            kb.dma("pool", w[:], wA[wi], W=[w])
            for q4 in range(4):
                acc = P[pi % 4]
                pi += 1
                for kc in range(32):
                    kb.mm(acc, acc[:], w, w[:, kc, :], xTb, xTb[:, kc, q4 * 512:(q4 + 1) * 512], kc == 0, kc == 31)
                c0 = st * 2048 + q4 * 512
                if wi < 24:
                    dst, ci = fm_dst[wi]
                    s = stg.next()
                    kb.op("act", lambda e: e.copy(out=s[:], in_=acc[:]), R=[acc], W=[s])
                    kb.dma("sp", dst[ci * 128:(ci + 1) * 128, c0:c0 + 512], s[:], R=[s], W=[dst])
                else:
                    s = stgf.next()
                    kb.op("act", lambda e: e.activation(out=s[:], in_=acc[:], func=AF.Sigmoid, bias=gb_s[:, 0:1]),
                          R=[acc, gb_s], W=[s])
                    kb.dma("sp", gsT[:, c0:c0 + 512], s[:], R=[s], W=[gsT])
        ws = []
        for j in range(4):
            w = wring.next()
            kb.dma("pool", w[:], wA[25 + j], W=[w])
            ws.append(w)
        for tc in range(16):
            acc = P[pi % 4]
            pi += 1
            for j in range(4):
                for kc in range(32):
                    kb.mm(acc, acc[:, j * 128:(j + 1) * 128], xTb, xTb[:, kc, tc * 128:(tc + 1) * 128],
                          ws[j], ws[j][:, kc, :], kc == 0, kc == 31)
            s = stg.next()
            kb.op("act", lambda e: e.copy(out=s[:], in_=acc[:]), R=[acc], W=[s])
            t0 = st * 2048 + tc * 128
            kb.dma("sp", vsw[t0:t0 + 128, :], s[:], R=[s], W=[vsw])
    kb.pop()

    kcmpT = kb.sb([128, 2, 256], BF16, "kcmpT")
    vcmp = kb.sb([128, 2, 2, 128], BF16, "vcmp")
    kb.push()
    w1s = [kb.sb([128, 32, 256], BF16, "w1k"), kb.sb([128, 32, 256], BF16, "w1v")]
    w2s = [kb.sb([128, 2, 128], BF16, "w2k"), kb.sb([128, 2, 128], BF16, "w2v")]
    pes = kb.sb([128, 2, 32], BF16, "pes")
    kb.dma("pool", w1s[0][:], w1k[:], W=[w1s[0]])
    kb.dma("pool", w1s[1][:], w1v[:], W=[w1s[1]])
    kb.dma("pool", w2s[0][:], w2k[:], W=[w2s[0]])
    kb.dma("pool", w2s[1][:], w2v[:], W=[w2s[1]])
    kb.dma("pool", pes[:], peT[:], W=[pes])
    src_r = Ring(kb, 2, [128, S], BF16, "cmpsrc")
    c0s = kb.sb([128, 4], F32, "c0s")
    u_r = Ring(kb, 2, [128, 256], F32, "u")
    t_r = Ring(kb, 2, [128, 256], F32, "t")
    hid = kb.sb([128, 2, 256], BF16, "hidc")
    pi = 0
    for kv in range(2):
        for hc in range(2):
            acc = P[4]
            for l in range(32):
                kb.mm(acc, acc[:, 0:1], w1s[kv], w1s[kv][:, l, hc * 128:(hc + 1) * 128], pes, pes[:, kv, l:l + 1], l == 0, l == 31)
            kb.op("dve", lambda e: e.tensor_copy(out=c0s[:, kv * 2 + hc:kv * 2 + hc + 1], in_=acc[:, 0:1]), R=[acc], W=[c0s])
        for gl in range(2):
            src = src_r.next()
            sd = kcT if kv == 0 else vcT
            kb.dma("sp", src[:], sd[gl * 128:(gl + 1) * 128, :], R=[sd], W=[src])
            for hc in range(2):
                acc = P[pi % 4]
                pi += 1
                for l in range(32):
                    kb.mm(acc, acc[:, 0:255], w1s[kv], w1s[kv][:, l, hc * 128:(hc + 1) * 128], src, src[:, l:l + 16 * 254 + 1:16],
                          l == 0, l == 31)
                u = u_r.next()
                t = t_r.next()
                ci = kv * 2 + hc
                kb.op("dve", lambda e: e.tensor_scalar(out=u[:, 0:255], in0=acc[:, 0:255], scalar1=c0s[:, ci:ci + 1], scalar2=None,
                                                       op0=ALU.add), R=[acc, c0s], W=[u])
                kb.op("dve", lambda e: e.tensor_tensor(out=t[:, 0:255], in0=u[:, 0:255], in1=u[:, 0:255], op=ALU.mult), R=[u], W=[t])
                kb.op("dve", lambda e: e.tensor_scalar(out=t[:, 0:255], in0=t[:, 0:255], scalar1=0.044715, scalar2=1.0,
                                                       op0=ALU.mult, op1=ALU.add), R=[t], W=[t])
                kb.op("dve", lambda e: e.tensor_tensor(out=t[:, 0:255], in0=t[:, 0:255], in1=u[:, 0:255], op=ALU.mult), R=[t, u], W=[t])
                kb.op("act", lambda e: e.activation(out=t[:, 0:255], in_=t[:, 0:255], func=AF.Sigmoid, scale=GELU_C), R=[t], W=[t])
                kb.op("dve", lambda e: e.tensor_tensor(out=hid[:, hc, 0:255], in0=t[:, 0:255], in1=u[:, 0:255], op=ALU.mult),
                      R=[t, u], W=[hid])
            if kv == 0:
                acc = P[pi % 4]
                pi += 1
                for hc in range(2):
                    kb.mm(acc, acc[:, 0:255], w2s[0], w2s[0][:, hc, :], hid, hid[:, hc, 0:255], hc == 0, hc == 1)
                kb.op("act", lambda e: e.copy(out=kcmpT[:, gl, 0:255], in_=acc[:, 0:255]), R=[acc], W=[kcmpT])
            else:
                for ch in range(2):
                    nn = 128 if ch == 0 else 127
                    acc = P[pi % 4]
                    pi += 1
                    for hc in range(2):
                        kb.mm(acc, acc[0:nn, 0:128], hid, hid[:, hc, ch * 128:ch * 128 + nn], w2s[1], w2s[1][:, hc, :], hc == 0, hc == 1)
                    kb.op("act", lambda e: e.copy(out=vcmp[0:nn, gl, ch, :], in_=acc[0:nn, 0:128]), R=[acc], W=[vcmp])
    kb.pop()

    kb.push()
    zrow = kb.sb([1, 256], BF16, "zrow")
    kb.op("dve", lambda e: e.memset(zrow[:], 0.0), W=[zrow])
    selms = kb.sb([48, 48, 128], F32, "selms")
    kb.dma("sp", selms[:], selm[:], W=[selms])
    ovls = kb.sb([128, 2, 64], BF16, "ovls")
    kb.dma("sp", ovls[:], ovl[:].rearrange("(c p) s -> p c s", p=128), W=[ovls])
    Es = kb.sb([64, S], BF16, "Es")
    kb.dma("sp", Es[:], Emat[:], W=[Es])
    ksS = kb.sb([128, S], BF16, "ksS")
    kwS = kb.sb([128, S], BF16, "kwS")
    vsS = kb.sb([128, 32, 128], BF16, "vsS")
    vwS = kb.sb([128, 32, 128], BF16, "vwS")
    qS = kb.sb([128, 8, 512], BF16, "qS")
    gs = kb.sb([48, 512], F32, "gs")
    cm_r = Ring(kb, 2, [128, 2, 512], BF16, "cm")
    pT_r = Ring(kb, 6, [128, 512], BF16, "pT")
    pn_r = Ring(kb, 2, [128, 512], BF16, "pn")
    rr_r = Ring(kb, 3, [128, 512], F32, "rr")
    grep_r = Ring(kb, 3, [128, 512], F32, "grep")
    outacc = kb.sb([128, 8, 512], F32, "outacc")
    tmpo = Ring(kb, 2, [128, 512], F32, "tmpo")
    ho_r = Ring(kb, 2, [128, 512], BF16, "ho")
    imp = kb.sb([128, 64], F32, "imp")
    km_r = Ring(kb, 2, [128, 64], F32, "km")
    am_r = Ring(kb, 2, [128, 64], F32, "am")
    cmpb = kb.sb([128, 64, 64], BF16, "cmpb")
    rank = kb.sb([128, 64], F32, "rank")
    selT = kb.sb([64, 512], BF16, "selT")
    smask = kb.sb([128, 32, 512], BF16, "smask")
    si = 0
    mi = 0
    s4 = 0

    def gate_rep(br, gl, hg):
        nonlocal si
        r = br * 16 + gl * 8 + hg
        ps = P[si % 3]
        si += 1
        kb.mm(ps, ps[:], selms, selms[:, r, :], gs, gs[:], True, True)
        return ps

    for gl in range(2):
        kb.dma("sp", ksS[:], ksT[gl * 128:(gl + 1) * 128, :], R=[ksT], W=[ksS])
        kb.dma("sp", kwS[:], kwT[gl * 128:(gl + 1) * 128, :], R=[kwT], W=[kwS])
        kb.dma("sp", vsS[:], vsw[:, gl * 128:(gl + 1) * 128].rearrange("(c p) d -> p c d", p=128), R=[vsw], W=[vsS])
        kb.dma("sp", vwS[:], vsw[:, 256 + gl * 128:256 + (gl + 1) * 128].rearrange("(c p) d -> p c d", p=128), R=[vsw], W=[vwS])
        for j in range(8):
            ts = slice(j * 512, (j + 1) * 512)
            kb.dma("sp", qS[:], qT[gl * 1024:(gl + 1) * 1024, ts].rearrange("(c p) t -> p c t", p=128), R=[qT], W=[qS])
            kb.dma("sp", gs[:], gsT[0:48, ts], R=[gsT], W=[gs])
            nb = min(255, 32 * j + 31)
            chunks = [(0, min(nb, 128))] + ([(128, nb - 128)] if nb > 128 else [])
            cm = cm_r.next()
            for ci, (n0, nn) in enumerate(chunks):
                kb.dma("sp", cm[0:nn, ci, :], cmaskT[n0:n0 + nn, ts], W=[cm])
            impP = P[7]
            kb.mm(impP, impP[:, 0:256], zrow, zrow[0:1, 0:128], zrow, zrow[0:1, 0:256], True, False)
            for hg in range(8):
                O = P[3 + (hg % 2)]
                L = P[5 + (hg % 2)]
                pts = []
                for ci, (n0, nn) in enumerate(chunks):
                    sps = P[si % 3]
                    si += 1
                    kb.mm(sps, sps[0:nn, :], kcmpT, kcmpT[:, gl, n0:n0 + nn], qS, qS[:, hg, :], True, True)
                    pT = pT_r.next()
                    kb.op("act", lambda e: e.activation(out=pT[0:nn, :], in_=sps[0:nn, :], func=AF.Exp, scale=NSA_SCALE), R=[sps], W=[pT])
                    kb.op("pool", lambda e: e.tensor_tensor(out=pT[0:nn, :], in0=pT[0:nn, :], in1=cm[0:nn, ci, :], op=ALU.mult),
                          R=[pT, cm], W=[pT])
                    kb.mm(O, O[:], vcmp, vcmp[0:nn, gl, ci, :], pT, pT[0:nn, :], ci == 0, ci == len(chunks) - 1)
                    kb.mm(L, L[:], onesb, onesb[0:nn, :], pT, pT[0:nn, :], ci == 0, ci == len(chunks) - 1)
                    pts.append(pT)
                rr = rr_r.next()
                kb.op("dve", lambda e: e.tensor_scalar(out=rr[:], in0=L[:], scalar1=1e-30, scalar2=None, op0=ALU.max), R=[L], W=[rr])
                kb.op("dve", lambda e: e.reciprocal(out=rr[:], in_=rr[:]), R=[rr], W=[rr])
                for ci, (n0, nn) in enumerate(chunks):
                    pn = pn_r.next()
                    kb.op("dve", lambda e: e.tensor_tensor(out=pn[0:nn, :], in0=pts[ci][0:nn, :], in1=rr[0:nn, :], op=ALU.mult),
                          R=[pts[ci], rr], W=[pn])
                    for qi in range(4):
                        first = (hg == 0 and ci == 0)
                        last = (hg == 7 and ci == len(chunks) - 1)
                        kb.mm(impP, impP[:, qi * 64:(qi + 1) * 64], pn, pn[0:nn, qi * 128:(qi + 1) * 128], ovls, ovls[0:nn, ci, :],
                              False, last)
                gp = gate_rep(0, gl, hg)
                gr = grep_r.next()
                kb.op("dve", lambda e: e.tensor_tensor(out=gr[:], in0=gp[:], in1=rr[:], op=ALU.mult), R=[gp, rr], W=[gr])
                kb.op("dve", lambda e: e.tensor_tensor(out=outacc[:, hg, :], in0=O[:], in1=gr[:], op=ALU.mult), R=[O, gr], W=[outacc])
            for qi in range(4):
                q0 = j * 512 + qi * 128
                km = km_r.next()
                am = am_r.next()
                kb.dma("sp", km[:], KM[q0:q0 + 128, :], W=[km])
                kb.dma("sp", am[:], AM[q0:q0 + 128, :], W=[am])
                kb.op("dve", lambda e: e.tensor_tensor(out=imp[:], in0=impP[:, qi * 64:(qi + 1) * 64], in1=km[:], op=ALU.mult),
                      R=[impP, km], W=[imp])
                kb.op("dve", lambda e: e.tensor_tensor(out=imp[:], in0=imp[:], in1=am[:], op=ALU.add), R=[imp, am], W=[imp])
                kb.op("dve", lambda e: e.tensor_tensor(out=cmpb[:], in0=imp[:, :].unsqueeze(1).to_broadcast([128, 64, 64]),
                                                       in1=imp[:, :].unsqueeze(2).to_broadcast([128, 64, 64]), op=ALU.is_gt),
                      R=[imp], W=[cmpb])
                kb.op("dve", lambda e: e.tensor_reduce(out=rank[:], in_=cmpb[:], axis=AX.X, op=ALU.add), R=[cmpb], W=[rank])
                kb.op("dve", lambda e: e.tensor_scalar(out=rank[:], in0=rank[:], scalar1=15.5, scalar2=None, op0=ALU.is_lt),
                      R=[rank], W=[rank])
                tp = P[si % 3]
                si += 1
                kb.op("pe", lambda e: e.transpose(out=tp[0:64, 0:128], in_=rank[:, :], identity=ident[:]), R=[rank, ident], W=[tp])
                kb.op("act", lambda e: e.copy(out=selT[:, qi * 128:(qi + 1) * 128], in_=tp[0:64, 0:128]), R=[tp], W=[selT])
            nk = 4 * (j + 1)
            for kc in range(nk):
                ps = P[si % 3]
                si += 1
                kb.mm(ps, ps[:], Es, Es[:, kc * 128:(kc + 1) * 128], selT, selT[:], True, True)
                m = kc - 4 * j
                if m >= 0:
                    kb.op("dve", lambda e: e.tensor_tensor(out=smask[:, kc, :], in0=ps[:], in1=masks[:, m, :], op=ALU.mult),
                          R=[ps, masks], W=[smask])
                else:
                    kb.op("act", lambda e: e.copy(out=smask[:, kc, :], in_=ps[:]), R=[ps], W=[smask])
            for hg in range(8):
                O = P[4]
                L = P[5]
                def emit_s(kc):
                    nonlocal s4
                    sps = P[s4 % 4]
                    s4 += 1
                    ks = slice(kc * 128, (kc + 1) * 128)
                    kb.mm(sps, sps[:], ksS, ksS[:, ks], qS, qS[:, hg, :], True, True)
                    return sps

                def emit_rest(kc, sps):
                    nonlocal mi
                    pT = pT_r.next()
                    kb.op("act", lambda e: e.activation(out=pT[:], in_=sps[:], func=AF.Exp, scale=NSA_SCALE), R=[sps], W=[pT])
                    eng = "pool" if (mi % 2 == 0) else "dve"
                    mi += 1
                    kb.op(eng, lambda e: e.tensor_tensor(out=pT[:], in0=pT[:], in1=smask[:, kc, :], op=ALU.mult), R=[pT, smask], W=[pT])
                    kb.mm(O, O[:], vsS, vsS[:, kc, :], pT, pT[:], kc == 0, kc == nk - 1)
                    kb.mm(L, L[:], onesb, onesb[:], pT, pT[:], kc == 0, kc == nk - 1)
                pipelined(nk, emit_s, emit_rest, look=3)
                rr = rr_r.next()
                kb.op("dve", lambda e: e.reciprocal(out=rr[:], in_=L[:]), R=[L], W=[rr])
                gp = gate_rep(1, gl, hg)
                gr = grep_r.next()
                kb.op("dve", lambda e: e.tensor_tensor(out=gr[:], in0=gp[:], in1=rr[:], op=ALU.mult), R=[gp, rr], W=[gr])
                to = tmpo.next()
                kb.op("dve", lambda e: e.tensor_tensor(out=to[:], in0=O[:], in1=gr[:], op=ALU.mult), R=[O, gr], W=[to])
                kb.op("pool", lambda e: e.tensor_tensor(out=outacc[:, hg, :], in0=outacc[:, hg, :], in1=to[:], op=ALU.add),
                      R=[outacc, to], W=[outacc])
                O2 = P[6]
                L2 = P[7]
                wch = [c for c in range(8) if 4 * j - 4 + c >= 0]
                def emit_s2(wi_):
                    nonlocal s4
                    kc = 4 * j - 4 + wch[wi_]
                    sps = P[s4 % 4]
                    s4 += 1
                    ks = slice(kc * 128, (kc + 1) * 128)
                    kb.mm(sps, sps[:], kwS, kwS[:, ks], qS, qS[:, hg, :], True, True)
                    return sps

                def emit_rest2(wi_, sps):
                    nonlocal mi
                    c = wch[wi_]
                    kc = 4 * j - 4 + c
                    pT = pT_r.next()
                    kb.op("act", lambda e: e.activation(out=pT[:], in_=sps[:], func=AF.Exp, scale=NSA_SCALE), R=[sps], W=[pT])
                    eng = "pool" if (mi % 2 == 0) else "dve"
                    mi += 1
                    kb.op(eng, lambda e: e.tensor_tensor(out=pT[:], in0=pT[:], in1=wmasks[:, c, :], op=ALU.mult), R=[pT, wmasks], W=[pT])
                    kb.mm(O2, O2[:], vwS, vwS[:, kc, :], pT, pT[:], wi_ == 0, wi_ == len(wch) - 1)
                    kb.mm(L2, L2[:], onesb, onesb[:], pT, pT[:], wi_ == 0, wi_ == len(wch) - 1)
                pipelined(len(wch), emit_s2, emit_rest2, look=3)
                rr = rr_r.next()
                kb.op("dve", lambda e: e.reciprocal(out=rr[:], in_=L2[:]), R=[L2], W=[rr])
                gp = gate_rep(2, gl, hg)
                gr = grep_r.next()
                kb.op("dve", lambda e: e.tensor_tensor(out=gr[:], in0=gp[:], in1=rr[:], op=ALU.mult), R=[gp, rr], W=[gr])
                to = tmpo.next()
                kb.op("dve", lambda e: e.tensor_tensor(out=to[:], in0=O2[:], in1=gr[:], op=ALU.mult), R=[O2, gr], W=[to])
                ho = ho_r.next()
                kb.op("pool", lambda e: e.tensor_tensor(out=ho[:], in0=outacc[:, hg, :], in1=to[:], op=ALU.add),
                      R=[outacc, to], W=[ho])
                r0 = (gl * 8 + hg) * 128
                kb.dma("sp", hTo[r0:r0 + 128, ts], ho[:], R=[ho], W=[hTo])
    kb.pop()
    kb.pop()


C_OFF = {"q": 0, "kc": 4096, "vc": 4608, "ks": 5120, "vs": 5632, "kw": 6144, "vw": 6656, "gates": 7168}


def prep_M1_consts():
    S = SEQ
    n = np.arange(256)
    q = np.arange(S)
    cmaskT = ((16 * n[:, None] + 31) <= q[None, :]) & (n[:, None] < 255)
    cmp_start = np.arange(255) * 16
    sel_start = np.arange(64) * 64
    ov = np.zeros((256, 64), np.float32)
    ov[:255] = ((cmp_start[:, None] < sel_start[None, :] + 64) & (cmp_start[:, None] + 32 > sel_start[None, :]))
    cur = q // 64
    sb = np.arange(64)
    forced = (sb[None, :] == 0) | (sb[None, :] == cur[:, None]) | (sb[None, :] == cur[:, None] - 1)
    valid = sb[None, :] <= cur[:, None]
    KM = (valid & ~forced).astype(np.float32)
    AM = np.where(valid, np.where(forced, np.float32(1e9), np.float32(0.0)), np.float32(-1e30)).astype(np.float32)
    E = (np.arange(S)[None, :] // 64 == sb[:, None])
    kl = np.arange(128)[:, None]
    ql = np.arange(512)[None, :]
    dmask = np.stack([(128 * m + kl <= ql) for m in range(4)])
    wmask = np.stack([((128 * c + kl - 512 <= ql) & (128 * c + kl > ql)) for c in range(8)])
    selm = np.zeros((48, 48, 128), np.float32)
    for r in range(48):
        selm[r, r, :] = 1.0
    bf = lambda a: a.astype(np.float32).astype(ml_dtypes.bfloat16)
    return {"cmaskT": bf(cmaskT), "ovl": bf(ov), "KM": KM, "AM": AM, "Emat": bf(E), "dmask": bf(dmask), "wmask": bf(wmask),
            "selm": selm, "ident": np.eye(128, dtype=np.float32)}


def prep_M1_weights(gh, w_in, b_gate, pe_k, pe_v, w1k, w2k, w1v, w2v):
    o = C_OFF
    cols = [w_in[:, o["q"] + gh * 2048:o["q"] + (gh + 1) * 2048]]
    for nm in ("kc", "vc", "ks", "kw"):
        cols.append(w_in[:, o[nm] + gh * 256:o[nm] + (gh + 1) * 256])
    gidx = [o["gates"] + br * 32 + (2 * gh + gl) * 8 + hg for br in range(3) for gl in range(2) for hg in range(8)]
    gcols = np.zeros((D, 128), np.float32)
    gcols[:, 0:48] = w_in[:, gidx]
    cols.append(gcols)
    for nm in ("vs", "vw"):
        cols.append(w_in[:, o[nm] + gh * 256:o[nm] + (gh + 1) * 256])
    wA = tile_kxn(np.concatenate(cols, 1))
    assert wA.shape[0] == M1_NW
    gb = np.zeros((128, 1), np.float32)
    gb[0:48, 0] = b_gate[[i - o["gates"] for i in gidx]]
    lay1 = lambda w: np.ascontiguousarray(w.reshape(32, 128, 256).transpose(1, 0, 2))
    lay2 = lambda w: np.ascontiguousarray(w.reshape(2, 128, 128).transpose(1, 0, 2))
    peT = np.ascontiguousarray(np.stack([pe_k.T, pe_v.T], axis=1))
    return {"wA": wA, "w1k": lay1(w1k), "w1v": lay1(w1v), "w2k": lay2(w2k), "w2v": lay2(w2v), "peT": peT, "gbias": gb}


NFH = NFC // 2


def emit_F(kb, P, pfx, hT, xin, xout):
    S = SEQ
    kb.pfx = pfx
    wo_t = kb.dram("wo_t", [32, 128, 16, 128], F32, "ExternalInput")
    wg_t = kb.dram("wg_t", [NFH, 128, 32, 128], F32, "ExternalInput")
    wu_t = kb.dram("wu_t", [NFH, 128, 32, 128], F32, "ExternalInput")
    wd_t = kb.dram("wd_t", [32, 128, NFH, 128], F32, "ExternalInput")
    lnp = kb.dram("lnp", [128, 4, 32], F32, "ExternalInput")
    ypart = kb.dram("ypart", [D, S], F32)
    ysum = kb.dram("ysum", [D, S], F32)
    x1a = kb.dram("x1a", [D, S], F32)
    x1ab = kb.dram("x1ab", [D, S], BF16)
    yp2 = kb.dram("yp2", [8, D, 512], F32)
    ys2 = kb.dram("ys2", [8, D, 512], F32)
    ypart_dc = [Tk(ypart.h[dc * 128:(dc + 1) * 128, :], "ypart%d" % dc) for dc in range(32)]
    ysum_dc = [Tk(ysum.h[dc * 128:(dc + 1) * 128, :], "ysum%d" % dc) for dc in range(32)]
    yp2_t = [Tk(yp2.h[tt], "yp2_%d" % tt) for tt in range(8)]
    ys2_t = [Tk(ys2.h[tt], "ys2_%d" % tt) for tt in range(8)]

    kb.push()
    hTall = kb.sb([128, 16, S], BF16, "hTall")
    for q in range(4):
        kb.dma("sp", hTall[:, :, q * 1024:(q + 1) * 1024],
               hT[:, q * 1024:(q + 1) * 1024].rearrange("(k p) t -> p k t", p=128), R=[hT], W=[hTall])
    wring = Ring(kb, 3, [128, 16, 128], BF16, "wo")
    stg = Ring(kb, 3, [128, 512], F32, "stgo")
    pi = 0
    for dc in range(32):
        w = wring.next()
        kb.dma("pool", w[:], wo_t[dc], W=[w])
        for tt in range(8):
            acc = P[pi % 4]
            pi += 1
            for kc in range(16):
                kb.mm(acc, acc[:], w, w[:, kc, :], hTall, hTall[:, kc, tt * 512:(tt + 1) * 512], kc == 0, kc == 15)
            s = stg.next()
            kb.op("act", lambda e: e.copy(out=s[:], in_=acc[:]), R=[acc], W=[s])
            kb.dma("sp", ypart_dc[dc][:, tt * 512:(tt + 1) * 512], s[:], R=[s], W=[ypart_dc[dc]])
        kb.coll_allreduce(ypart_dc[dc], ypart_dc[dc][:, :], ysum_dc[dc], ysum_dc[dc][:, :])
    kb.pop()

    def ln_stage(get_y, xsrc, gi, bi, sink_factory):
        kb.push()
        ones = kb.sb([128, 128], F32, "ones")
        kb.op("dve", lambda e: e.memset(ones[:], 1.0), W=[ones])
        lneps = kb.sb([128, 1], F32, "lneps")
        kb.op("dve", lambda e: e.memset(lneps[:], LN_EPS), W=[lneps])
        lnsb = kb.sb([128, 4, 32], F32, "lnsb")
        kb.dma("sp", lnsb[:], lnp[:], W=[lnsb])
        xin_r = Ring(kb, 3, [128, 512], F32, "xin")
        y_r = Ring(kb, 3, [128, 512], F32, "yin")
        zT_r = Ring(kb, 2, [128, 32, 512], F32, "zT")
        zviews = {id(b): [Tk(b.h[:, dc, :], "zv%d" % dc) for dc in range(32)] for b in zT_r.bufs}
        xo_r = Ring(kb, 3, [128, 512], F32, "xo")
        sq_r = Ring(kb, 2, [128, 512], F32, "sq")
        mean = kb.sb([128, 512], F32, "mean")
        rstd = kb.sb([128, 512], F32, "rstd")
        nmr = kb.sb([128, 512], F32, "nmr")
        tmpa = kb.sb([128, 512], F32, "tmpa")
        S1, S2 = P[6], P[7]
        sink = sink_factory()
        for tt in range(8):
            ts = slice(tt * 512, (tt + 1) * 512)
            zT = zT_r.next()
            zv = zviews[id(zT)]
            for dc in range(32):
                xi = xin_r.next()
                kb.dma("sp", xi[:], xsrc[dc * 128:(dc + 1) * 128, ts], R=[xsrc], W=[xi])
                yi = y_r.next()
                ytk, yap = get_y(tt, dc)
                kb.dma("sp", yi[:], yap, R=[ytk], W=[yi])
                kb.op("dve", lambda e: e.scalar_tensor_tensor(out=zv[dc][:], in0=xi[:], scalar=ALPHA, in1=yi[:],
                                                              op0=ALU.mult, op1=ALU.add), R=[xi, yi], W=[zv[dc]])
                kb.mm(S1, S1[:], ones, ones[:], zv[dc], zv[dc][:], dc == 0, dc == 31)
                sq = sq_r.next()
                kb.op("act", lambda e: e.activation(out=sq[:], in_=zv[dc][:], func=AF.Square), R=[zv[dc]], W=[sq])
                kb.mm(S2, S2[:], ones, ones[:], sq, sq[:], dc == 0, dc == 31)
            kb.op("act", lambda e: e.mul(out=mean[:], in_=S1[:], mul=1.0 / D), R=[S1], W=[mean])
            kb.op("dve", lambda e: e.tensor_tensor(out=tmpa[:], in0=mean[:], in1=mean[:], op=ALU.mult), R=[mean], W=[tmpa])
            kb.op("dve", lambda e: e.scalar_tensor_tensor(out=tmpa[:], in0=S2[:], scalar=1.0 / D, in1=tmpa[:],
                                                          op0=ALU.mult, op1=ALU.subtract), R=[S2, tmpa], W=[tmpa])
            kb.op("act", lambda e: e.activation(out=rstd[:], in_=tmpa[:], func=AF.Ln, bias=lneps[:, 0:1]), R=[tmpa, lneps], W=[rstd])
            kb.op("act", lambda e: e.activation(out=rstd[:], in_=rstd[:], func=AF.Exp, scale=-0.5), R=[rstd], W=[rstd])
            kb.op("dve", lambda e: e.scalar_tensor_tensor(out=nmr[:], in0=mean[:], scalar=-1.0, in1=rstd[:],
                                                          op0=ALU.mult, op1=ALU.mult), R=[mean, rstd], W=[nmr])
            for dc in range(32):
                t1 = xin_r.next()
                kb.op("dve", lambda e: e.tensor_tensor(out=t1[:], in0=zv[dc][:], in1=rstd[:], op=ALU.mult), R=[zv[dc], rstd], W=[t1])
                kb.op("dve", lambda e: e.tensor_tensor(out=t1[:], in0=t1[:], in1=nmr[:], op=ALU.add), R=[t1, nmr], W=[t1])
                xc = xo_r.next()
                kb.op("act", lambda e: e.activation(out=xc[:], in_=t1[:], func=AF.Identity,
                                                    scale=lnsb[:, gi, dc:dc + 1], bias=lnsb[:, bi, dc:dc + 1]),
                      R=[t1, lnsb], W=[xc])
                sink(tt, dc, xc)
        kb.pop()

    x1a_dc = [Tk(x1a.h[dc * 128:(dc + 1) * 128, :], "x1a%d" % dc) for dc in range(32)]
    x1ab_dc = [Tk(x1ab.h[dc * 128:(dc + 1) * 128, :], "x1ab%d" % dc) for dc in range(32)]
    xout_dc = [Tk(xout.h[dc * 128:(dc + 1) * 128, :], "xout%d" % dc) for dc in range(32)]

    def sink1_factory():
        b_r = Ring(kb, 3, [128, 512], BF16, "xb")

        def sink(tt, dc, xc):
            ts = slice(tt * 512, (tt + 1) * 512)
            kb.dma("sp", x1a_dc[dc][:, ts], xc[:], R=[xc], W=[x1a_dc[dc]])
            xb = b_r.next()
            kb.op("pool", lambda e: e.tensor_copy(out=xb[:], in_=xc[:]), R=[xc], W=[xb])
            kb.dma("sp", x1ab_dc[dc][:, ts], xb[:], R=[xb], W=[x1ab_dc[dc]])
        return sink
    ln_stage(lambda tt, dc: (ysum_dc[dc], ysum_dc[dc][:, tt * 512:(tt + 1) * 512]), xin, 0, 1, sink1_factory)

    kb.push()
    actT = kb.sb([128, 32, 512], BF16, "actT")
    hid = kb.sb([128, NFH, 512], BF16, "hid")
    wring = Ring(kb, 4, [128, NFH, 128], BF16, "wf")
    sg_r = Ring(kb, 2, [128, 512], F32, "sg")
    stg = Ring(kb, 3, [128, 512], F32, "stgf")
    pi = 0
    for tt in range(8):
        ts = slice(tt * 512, (tt + 1) * 512)
        kb.dma("sp", actT[:], x1ab[:, ts].rearrange("(k p) t -> p k t", p=128), R=[x1ab], W=[actT])
        for f in range(NFH):
            wg = wring.next()
            kb.dma("pool", wg[:, 0:32, :], wg_t[f], W=[wg])
            wu = wring.next()
            kb.dma("pool", wu[:, 0:32, :], wu_t[f], W=[wu])
            pg = P[pi % 6]
            pu = P[(pi + 1) % 6]
            pi += 2
            for kc in range(32):
                kb.mm(pg, pg[:], wg, wg[:, kc, :], actT, actT[:, kc, :], kc == 0, kc == 31)
            for kc in range(32):
                kb.mm(pu, pu[:], wu, wu[:, kc, :], actT, actT[:, kc, :], kc == 0, kc == 31)
            sg = sg_r.next()
            kb.op("act", lambda e: e.activation(out=sg[:], in_=pg[:], func=AF.Silu), R=[pg], W=[sg])
            kb.op("dve", lambda e: e.tensor_tensor(out=hid[:, f, :], in0=sg[:], in1=pu[:], op=ALU.mult), R=[sg, pu], W=[hid])
            if f == 20 and tt > 0:
                for i in range(4):
                    kb.coll_allreduce(yp2_t[tt - 1], yp2_t[tt - 1][i * 1024:(i + 1) * 1024, :],
                                      ys2_t[tt - 1], ys2_t[tt - 1][i * 1024:(i + 1) * 1024, :])
        for dc in range(32):
            w = wring.next()
            kb.dma("pool", w[:], wd_t[dc], W=[w])
            acc = P[pi % 6]
            pi += 1
            for fc in range(NFH):
                kb.mm(acc, acc[:], w, w[:, fc, :], hid, hid[:, fc, :], fc == 0, fc == NFH - 1)
            s = stg.next()
            kb.op("act", lambda e: e.copy(out=s[:], in_=acc[:]), R=[acc], W=[s])
            kb.dma("sp", yp2_t[tt][dc * 128:(dc + 1) * 128, :], s[:], R=[s], W=[yp2_t[tt]])
    for i in range(4):
        kb.coll_allreduce(yp2_t[7], yp2_t[7][i * 1024:(i + 1) * 1024, :], ys2_t[7], ys2_t[7][i * 1024:(i + 1) * 1024, :])
    kb.pop()

    def sink2_factory():
        def sink(tt, dc, xc):
            ts = slice(tt * 512, (tt + 1) * 512)
            kb.dma("sp", xout_dc[dc][:, ts], xc[:], R=[xc], W=[xout_dc[dc]])
        return sink
    ln_stage(lambda tt, dc: (ys2_t[tt], ys2_t[tt][dc * 128:(dc + 1) * 128, :]), x1a, 2, 3, sink2_factory)


def build_fused():
    nc = bass.Bass("TRN2", target_bir_lowering=False)
    kb = KB(nc)
    kb.pfx = ""
    xT = kb.dram("xT", [D, SEQ], F32, "ExternalInput")
    xoT = kb.dram("xoT", [D, SEQ], F32, "ExternalOutput")
    h0 = kb.dram("h0T", [2048, SEQ], BF16)
    h1 = kb.dram("h1T", [2048, SEQ], BF16)
    x1T = kb.dram("x1T", [D, SEQ], F32)
    P = [kb.ps((128, 512), F32, "P%d" % i) for i in range(8)]
    emit_M0(kb, P, xT, h0)
    emit_F(kb, P, "f0_", h0, xT, x1T)
    emit_M1(kb, P, x1T, h1)
    emit_F(kb, P, "f1_", h1, x1T, xoT)
    kb.finish()
    return nc


def prep_F_weights(rows, half, w_o, w_gate, w_up, w_down, lnv):
    f0, f1 = half * NFH * 128, (half + 1) * NFH * 128
    return {"wo_t": tile_kxn(w_o[rows]), "wg_t": tile_kxn(w_gate[:, f0:f1]), "wu_t": tile_kxn(w_up[:, f0:f1]),
            "wd_t": tile_kxn(w_down[f0:f1]), "lnp": ln_pack(lnv)}


def kernel(x, ab_w_in, ab_b_igate, ab_b_fgate, ab_mlstm_norm, ab_q_norm, ab_kv_norm, ab_w_uq, ab_w_ukv,
           ab_w_o, c_w_in, c_b_gate, c_pe_k, c_pe_v, c_cmp_w1_k, c_cmp_w2_k, c_cmp_w1_v, c_cmp_w2_v, c_w_o,
           ffn_w_gate, ffn_w_up, ffn_w_down, ln_mix_g, ln_mix_b, ln_ffn_g, ln_ffn_b):
    f = lambda a: np.asarray(a, dtype=np.float32)
    x = f(x)
    B = x.shape[0]
    if "fused" not in _CACHE:
        _CACHE["fused"] = build_fused()
    nc = _CACHE["fused"]
    per_half = []
    c0 = prep_M0_consts()
    c1 = prep_M1_consts()
    for hh in range(2):
        m = {}
        w0 = prep_M0_weights(hh, f(ab_w_in)[0], f(ab_b_igate)[0], f(ab_b_fgate)[0], f(ab_mlstm_norm)[0], f(ab_q_norm)[0],
                             f(ab_kv_norm)[0], f(ab_w_uq)[0], f(ab_w_ukv)[0])
        for k, v in list(w0.items()) + list(c0.items()):
            m["m0_" + k] = v
        w1 = prep_M1_weights(hh, f(c_w_in)[0], f(c_b_gate)[0], f(c_pe_k)[0], f(c_pe_v)[0], f(c_cmp_w1_k)[0], f(c_cmp_w2_k)[0],
                             f(c_cmp_w1_v)[0], f(c_cmp_w2_v)[0])
        for k, v in list(w1.items()) + list(c1.items()):
            m["m1_" + k] = v
        rows0 = np.concatenate([np.arange(hh * 1024, (hh + 1) * 1024), np.arange(2048 + hh * 1024, 2048 + (hh + 1) * 1024)])
        lnv0 = [f(ln_mix_g)[0], f(ln_mix_b)[0], f(ln_ffn_g)[0], f(ln_ffn_b)[0]]
        for k, v in prep_F_weights(rows0, hh, f(ab_w_o)[0], f(ffn_w_gate)[0], f(ffn_w_up)[0], f(ffn_w_down)[0], lnv0).items():
            m["f0_" + k] = v
        rows1 = np.arange(hh * 2048, (hh + 1) * 2048)
        lnv1 = [f(ln_mix_g)[1], f(ln_mix_b)[1], f(ln_ffn_g)[1], f(ln_ffn_b)[1]]
        for k, v in prep_F_weights(rows1, hh, f(c_w_o)[0], f(ffn_w_gate)[1], f(ffn_w_up)[1], f(ffn_w_down)[1], lnv1).items():
            m["f1_" + k] = v
        per_half.append(m)
    xT = [np.ascontiguousarray(x[b].T) for b in range(B)]
    in_maps = []
    for c in range(8):
        m = dict(per_half[c % 2])
        m["xT"] = xT[c // 2]
        in_maps.append(m)
    res = run_bass_kernel_spmd(nc, in_maps, core_ids=list(range(8)))
    out = np.empty((B, SEQ, D), np.float32)
    for b in range(B):
        out[b] = np.asarray(res.results[2 * b]["xoT"]).T
    return out
```

```python
import math
from contextlib import ExitStack

import numpy as np
import ml_dtypes
import concourse.bass as bass
import concourse.mybir as mybir
from concourse.bass_utils import run_bass_kernel_spmd

F32 = mybir.dt.float32
BF16 = mybir.dt.bfloat16
AF = mybir.ActivationFunctionType
ALU = mybir.AluOpType
AX = mybir.AxisListType

D = 4096
SEQ = 4096
FF = 11008
NFC = FF // 128
ALPHA = 4.0 ** 0.25
LN_EPS = 1e-5
RMS_EPS = 1e-6
NDS = 28


class Tk:
    __slots__ = ("h", "lw", "rd", "name")

    def __init__(self, h, name):
        self.h = h
        self.lw = None
        self.rd = {}
        self.name = name

    def __getitem__(self, idx):
        return self.h[idx]


class KB:
    def __init__(self, nc):
        self.nc = nc
        self.es = ExitStack()
        self.es_root = self.es
        self.eng = {"pe": nc.tensor, "act": nc.scalar, "dve": nc.vector, "pool": nc.gpsimd, "sp": nc.sync}
        self.sem = {}
        self.cnt = {}
        self.known = {}
        for k in self.eng:
            self.sem[k] = self.es.enter_context(nc.semaphore("s_" + k))
            self.cnt[k] = 0
            self.known[k] = {}
        self.dsem = [self.es.enter_context(nc.semaphore("d%d" % i)) for i in range(NDS)]
        self.dcnt = [0] * NDS
        self.dnext = 0
        self.nid = 0

    def sb(self, shape, dt, name=None):
        self.nid += 1
        name = "%s_s%d" % (name or "sb", self.nid)
        return Tk(self.es.enter_context(self.nc.sbuf_tensor(name, list(shape), dt)), name)

    def ps(self, shape=(128, 512), dt=F32, name=None):
        self.nid += 1
        name = "%s_p%d" % (name or "ps", self.nid)
        return Tk(self.es.enter_context(self.nc.psum_tensor(name, list(shape), dt)), name)

    def dram(self, name, shape, dt, kind="Internal"):
        name = getattr(self, "pfx", "") + name
        return Tk(self.nc.dram_tensor(name, list(shape), dt, kind=kind).ap(), name)

    def coll_allreduce(self, in_tk, in_ap, out_tk, out_ap):
        if not hasattr(self, "csem"):
            self.csem = self.es_root.enter_context(self.nc.semaphore("s_cc"))
            self.ccnt = 0
        deps = self._deps("pool", [in_tk], [out_tk], True)
        self._emit_waits("pool", deps)
        ins = self.eng["pool"].collective_compute("AllReduce", ALU.add, replica_groups=[[0, 1], [2, 3], [4, 5], [6, 7]],
                                                  ins=[in_ap], outs=[out_ap])
        self.ccnt += 1
        ins.then_inc(self.csem)
        key = "cc"
        if in_tk.rd.get(key, 0) < self.ccnt:
            in_tk.rd[key] = self.ccnt
        out_tk.lw = (key, self.ccnt)
        out_tk.rd = {}

    def barrier(self):
        for e, eng in self.eng.items():
            kn = self.known[e]
            for k in self.eng:
                if k != e and kn.get(k, 0) < self.cnt[k]:
                    eng.wait_ge(self.sem[k], self.cnt[k])
                    kn[k] = self.cnt[k]
            for i in range(NDS):
                key = ("d", i)
                if kn.get(key, 0) < self.dcnt[i]:
                    eng.wait_ge(self.dsem[i], self.dcnt[i])
                    kn[key] = self.dcnt[i]
            if getattr(self, "ccnt", 0) > kn.get("cc", 0):
                eng.wait_ge(self.csem, self.ccnt)
                kn["cc"] = self.ccnt

    def push(self):
        self._outer = getattr(self, "_outer", [])
        self._outer.append(self.es)
        self.es = ExitStack()

    def pop(self):
        self.barrier()
        self.es.close()
        self.es = self._outer.pop()

    def semof(self, key):
        if isinstance(key, tuple):
            return self.dsem[key[1]]
        if key == "cc":
            return self.csem
        return self.sem[key]

    def _deps(self, e, R, W, is_dma):
        deps = {}

        def add(tok, same_ok):
            if tok is None:
                return
            key, val = tok
            if key == e and not same_ok and not is_dma:
                return
            if deps.get(key, 0) < val:
                deps[key] = val

        for t in R:
            add(t.lw, e != "pe")
        for t in W:
            add(t.lw, False)
            for k, v in t.rd.items():
                add((k, v), False)
        return deps

    def _emit_waits(self, e, deps):
        eng = self.eng[e]
        kn = self.known[e]
        for key, val in deps.items():
            if kn.get(key, 0) >= val:
                continue
            eng.wait_ge(self.semof(key), val)
            kn[key] = val

    def op(self, e, fn, R=(), W=(), sig=True):
        deps = self._deps(e, R, W, False)
        self._emit_waits(e, deps)
        ins = fn(self.eng[e])
        if sig:
            self.cnt[e] += 1
            ins.then_inc(self.sem[e], 1)
            tv = self.cnt[e]
        else:
            tv = self.cnt[e] + 1
        for t in R:
            if t.rd.get(e, 0) < tv:
                t.rd[e] = tv
        for t in W:
            t.lw = (e, tv)
            t.rd = {}

    def dma(self, q, out, in_, R=(), W=(), transpose=False):
        si = self.dnext
        self.dnext = (self.dnext + 1) % NDS
        key = ("d", si)
        deps = self._deps(q, R, W, True)
        if self.dcnt[si] > 0:
            deps[key] = max(deps.get(key, 0), self.dcnt[si])
        self._emit_waits(q, deps)
        eng = self.eng[q]
        if transpose:
            ins = eng.dma_start_transpose(out=out, in_=in_)
        else:
            ins = eng.dma_start(out=out, in_=in_)
        self.dcnt[si] += 16
        ins.then_inc(self.dsem[si], 16)
        tv = self.dcnt[si]
        for t in R:
            if t.rd.get(key, 0) < tv:
                t.rd[key] = tv
        for t in W:
            t.lw = (key, tv)
            t.rd = {}

    def finish(self):
        eng = self.eng["sp"]
        for i in range(NDS):
            if self.dcnt[i] > 0:
                eng.wait_ge(self.dsem[i], self.dcnt[i])
        for k in self.eng:
            if k != "sp" and self.cnt[k] > 0:
                eng.wait_ge(self.sem[k], self.cnt[k])
        if getattr(self, "ccnt", 0) > 0:
            eng.wait_ge(self.csem, self.ccnt)
        self.es.close()

    def mm(self, out_t, out_ap, lhsT_t, lhsT_ap, rhs_t, rhs_ap, start, stop, sig=None):
        if sig is None:
            sig = True
        self.op("pe", lambda e: e.matmul(out_ap, lhsT=lhsT_ap, rhs=rhs_ap, start=start, stop=stop),
                R=[lhsT_t, rhs_t], W=[out_t], sig=sig)


def pipelined(n, emit_s, emit_rest, look=2):
    pend = {}
    for i in range(min(look, n)):
        pend[i] = emit_s(i)
    for i in range(n):
        emit_rest(i, pend.pop(i))
        if i + look < n:
            pend[i + look] = emit_s(i + look)


class Ring:
    def __init__(self, kb, n, shape, dt, name):
        self.bufs = [kb.sb(shape, dt, "%s%d" % (name, i)) for i in range(n)]
        self.i = 0

    def next(self):
        b = self.bufs[self.i]
        self.i = (self.i + 1) % len(self.bufs)
        return b


class PRing:
    def __init__(self, kb, n, name):
        self.bufs = [kb.ps((128, 512), F32, "%s%d" % (name, i)) for i in range(n)]
        self.i = 0

    def next(self):
        b = self.bufs[self.i]
        self.i = (self.i + 1) % len(self.bufs)
        return b


def tile_kxn(w, ncol=128):
    K, N = w.shape
    return np.ascontiguousarray(w.reshape(K // 128, 128, N // ncol, ncol).transpose(2, 1, 0, 3))


def ln_pack(vecs):
    return np.ascontiguousarray(np.stack([v.reshape(32, 128).T for v in vecs], axis=1))


_CACHE = {}


M0_NW = 41
MLA_SCALE = 192.0 ** -0.5


def emit_M0(kb, P, xT, hTo):
    S = SEQ
    kb.pfx = "m0_"
    kb.push()
    wA = kb.dram("wA", [M0_NW, 128, 32, 128], F32, "ExternalInput")
    wuq = kb.dram("wuq", [128, 8, 2048], F32, "ExternalInput")
    wukv_k = kb.dram("wukv_k", [128, 4, 1024], F32, "ExternalInput")
    wukv_v = kb.dram("wukv_v", [128, 4, 1024], F32, "ExternalInput")
    smallp = kb.dram("smallp", [128, 24], F32, "ExternalInput")
    cosT = kb.dram("cosT", [64, S], F32, "ExternalInput")
    sinT = kb.dram("sinT", [64, S], F32, "ExternalInput")
    dmask = kb.dram("dmask", [4, 128, 512], BF16, "ExternalInput")
    ident_d = kb.dram("ident", [128, 128], F32, "ExternalInput")
    qaT = kb.dram("qaT", [512, S], BF16)
    kaT = kb.dram("kaT", [512, S], BF16)
    ogT = kb.dram("ogT", [1024, S], BF16)
    cqT = kb.dram("cqT", [1024, S], BF16)
    ckvT = kb.dram("ckvT", [512, S], BF16)
    krT = kb.dram("krT", [128, S], BF16)
    va = kb.dram("va", [S, 1024], BF16)
    grep = kb.dram("grep", [4, 128, S], F32)
    qT = kb.dram("qT", [8, 192, S], BF16)
    knT = kb.dram("knT", [8, 128, S], BF16)
    kroT = kb.dram("kroT", [64, S], BF16)
    vmla = kb.dram("vmla", [S, 1024], BF16)
    fm_dst = [(qaT, i) for i in range(4)] + [(kaT, i) for i in range(4)] + [(ogT, i) for i in range(8)] + \
             [(cqT, i) for i in range(8)] + [(ckvT, i) for i in range(4)] + [(krT, 0)]

    onesf = kb.sb([128, 128], F32, "onesf")
    onesb = kb.sb([128, 128], BF16, "onesb")
    kb.op("dve", lambda e: e.memset(onesf[:], 1.0), W=[onesf])
    kb.op("dve", lambda e: e.memset(onesb[:], 1.0), W=[onesb])
    sp_ = kb.sb([128, 24], F32, "smallp_sb")
    kb.dma("sp", sp_[:], smallp[:], W=[sp_])
    masks = kb.sb([128, 4, 512], BF16, "masks")
    kb.dma("sp", masks[:], dmask[:].rearrange("m p t -> p m t"), W=[masks])
    ident = kb.sb([128, 128], F32, "ident_sb")
    kb.dma("sp", ident[:], ident_d[:], W=[ident])

    kb.push()
    xTb = kb.sb([128, 32, 2048], BF16, "xTb")
    wring = Ring(kb, 6, [128, 32, 128], BF16, "wA")
    stg = Ring(kb, 3, [128, 512], BF16, "stgA")
    stgf = Ring(kb, 2, [128, 512], F32, "stgAf")
    pi = 0
    for st in range(2):
        for q4 in range(4):
            c0 = st * 2048 + q4 * 512
            kb.dma("pool", xTb[:, :, q4 * 512:(q4 + 1) * 512],
                   xT[:, c0:c0 + 512].rearrange("(k p) t -> p k t", p=128), W=[xTb])
        for wi in list(range(29)) + list(range(37, 41)):
            w = wring.next()
            kb.dma("pool", w[:], wA[wi], W=[w])
            for q4 in range(4):
                acc = P[pi % 4]
                pi += 1
                for kc in range(32):
                    kb.mm(acc, acc[:], w, w[:, kc, :], xTb, xTb[:, kc, q4 * 512:(q4 + 1) * 512], kc == 0, kc == 31)
                c0 = st * 2048 + q4 * 512
                if wi < 29:
                    dst, ci = fm_dst[wi]
                    s = stg.next()
                    kb.op("act", lambda e: e.copy(out=s[:], in_=acc[:]), R=[acc], W=[s])
                    kb.dma("sp", dst[ci * 128:(ci + 1) * 128, c0:c0 + 512], s[:], R=[s], W=[dst])
                else:
                    s = stgf.next()
                    kb.op("act", lambda e: e.copy(out=s[:], in_=acc[:]), R=[acc], W=[s])
                    kb.dma("sp", grep[wi - 37, :, c0:c0 + 512], s[:], R=[s], W=[grep])
        for g2 in range(2):
            ws = []
            for j in range(4):
                w = wring.next()
                kb.dma("pool", w[:], wA[29 + g2 * 4 + j], W=[w])
                ws.append(w)
            for tc in range(16):
                acc = P[pi % 4]
                pi += 1
                for j in range(4):
                    for kc in range(32):
                        kb.mm(acc, acc[:, j * 128:(j + 1) * 128], xTb, xTb[:, kc, tc * 128:(tc + 1) * 128],
                              ws[j], ws[j][:, kc, :], kc == 0, kc == 31)
                s = stg.next()
                kb.op("act", lambda e: e.copy(out=s[:], in_=acc[:]), R=[acc], W=[s])
                t0 = st * 2048 + tc * 128
                kb.dma("sp", va[t0:t0 + 128, g2 * 512:(g2 + 1) * 512], s[:], R=[s], W=[va])
    kb.pop()

    kb.push()
    wuq_s = kb.sb([128, 8, 2048], BF16, "wuq")
    wk_s = kb.sb([128, 4, 1024], BF16, "wk")
    wv_s = kb.sb([128, 4, 1024], BF16, "wv")
    kb.dma("pool", wuq_s[:], wuq[:], W=[wuq_s])
    kb.dma("pool", wk_s[:], wukv_k[:], W=[wk_s])
    kb.dma("pool", wv_s[:], wukv_v[:], W=[wv_s])
    cq_r = Ring(kb, 2, [128, 8, 512], BF16, "cq")
    ckv_r = Ring(kb, 2, [128, 4, 512], BF16, "ckv")
    cqn = kb.sb([128, 8, 512], BF16, "cqn")
    ckvn = kb.sb([128, 4, 512], BF16, "ckvn")
    sq_r = Ring(kb, 2, [128, 512], F32, "sqB")
    rq = kb.sb([128, 512], F32, "rq")
    rkv = kb.sb([128, 512], F32, "rkv")
    cos_r = Ring(kb, 2, [64, 512], F32, "cos")
    sin_r = Ring(kb, 2, [64, 512], F32, "sin")
    krA_r = Ring(kb, 2, [64, 512], BF16, "krA")
    krB_r = Ring(kb, 2, [64, 512], BF16, "krB")
    t1_r = Ring(kb, 2, [64, 512], F32, "t1")
    t2_r = Ring(kb, 2, [64, 512], F32, "t2")
    stg = Ring(kb, 4, [128, 512], BF16, "stgB")
    pi = 0
    for tt in range(8):
        ts = slice(tt * 512, (tt + 1) * 512)
        cq = cq_r.next()
        kb.dma("sp", cq[:], cqT[:, ts].rearrange("(c p) t -> p c t", p=128), R=[cqT], W=[cq])
        ckv = ckv_r.next()
        kb.dma("sp", ckv[:], ckvT[:, ts].rearrange("(c p) t -> p c t", p=128), R=[ckvT], W=[ckv])
        cs = cos_r.next()
        sn = sin_r.next()
        kb.dma("sp", cs[:], cosT[:, ts], W=[cs])
        kb.dma("sp", sn[:], sinT[:, ts], W=[sn])
        for (src, nch, dstr, nf) in ((cq, 8, rq, 1024.0), (ckv, 4, rkv, 512.0)):
            acc = P[4]
            for c in range(nch):
                sq = sq_r.next()
                kb.op("act", lambda e: e.activation(out=sq[:], in_=src[:, c, :], func=AF.Square), R=[src], W=[sq])
                kb.mm(acc, acc[:], onesf, onesf[:], sq, sq[:], c == 0, c == nch - 1)
            kb.op("dve", lambda e: e.tensor_scalar(out=dstr[:], in0=acc[:], scalar1=1.0 / nf, scalar2=RMS_EPS,
                                                   op0=ALU.mult, op1=ALU.add), R=[acc], W=[dstr])
            kb.op("act", lambda e: e.activation(out=dstr[:], in_=dstr[:], func=AF.Ln), R=[dstr], W=[dstr])
            kb.op("act", lambda e: e.activation(out=dstr[:], in_=dstr[:], func=AF.Exp, scale=-0.5), R=[dstr], W=[dstr])
        for c in range(8):
            kb.op("pool", lambda e: e.tensor_scalar(out=cqn[:, c, :], in0=cq[:, c, :], scalar1=sp_[:, 12 + c:13 + c],
                                                    scalar2=None, op0=ALU.mult), R=[cq, sp_], W=[cqn])
        for c in range(4):
            kb.op("dve", lambda e: e.scalar_tensor_tensor(out=ckvn[:, c, :], in0=ckv[:, c, :], scalar=sp_[:, 20 + c:21 + c],
                                                          in1=rkv[:], op0=ALU.mult, op1=ALU.mult),
                  R=[ckv, sp_, rkv], W=[ckvn])
        krA = krA_r.next()
        krB = krB_r.next()
        kb.dma("sp", krA[:], krT[0:64, ts], R=[krT], W=[krA])
        kb.dma("sp", krB[:], krT[64:128, ts], R=[krT], W=[krB])
        t1 = t1_r.next()
        t2 = t2_r.next()
        kb.op("dve", lambda e: e.tensor_tensor(out=t1[:], in0=krA[:], in1=cs[:], op=ALU.mult), R=[krA, cs], W=[t1])
        kb.op("pool", lambda e: e.tensor_tensor(out=t2[:], in0=krB[:], in1=sn[:], op=ALU.mult), R=[krB, sn], W=[t2])
        s = stg.next()
        kb.op("dve", lambda e: e.tensor_tensor(out=s[0:64, :], in0=t1[:], in1=t2[:], op=ALU.add), R=[t1, t2], W=[s])
        kb.dma("sp", kroT[:, ts], s[0:64, :], R=[s], W=[kroT])
        for h in range(8):
            acc = P[pi % 4]
            pi += 1
            for c in range(8):
                kb.mm(acc, acc[:], wuq_s, wuq_s[:, c, h * 256:h * 256 + 128], cqn, cqn[:, c, :], c == 0, c == 7)
            s = stg.next()
            kb.op("dve", lambda e: e.tensor_tensor(out=s[:], in0=acc[:], in1=rq[:], op=ALU.mult), R=[acc, rq], W=[s])
            kb.dma("sp", qT[h, 0:128, ts], s[:], R=[s], W=[qT])
            a1 = P[5]
            a2 = P[6]
            for c in range(8):
                kb.mm(a1, a1[0:64, :], wuq_s, wuq_s[:, c, h * 256 + 128:h * 256 + 192], cqn, cqn[:, c, :], c == 0, c == 7)
            for c in range(8):
                kb.mm(a2, a2[0:64, :], wuq_s, wuq_s[:, c, h * 256 + 192:h * 256 + 256], cqn, cqn[:, c, :], c == 0, c == 7)
            t1 = t1_r.next()
            t2 = t2_r.next()
            kb.op("dve", lambda e: e.tensor_tensor(out=t1[:], in0=a1[0:64, :], in1=cs[:], op=ALU.mult), R=[a1, cs], W=[t1])
            kb.op("dve", lambda e: e.tensor_tensor(out=t2[:], in0=a2[0:64, :], in1=sn[:], op=ALU.mult), R=[a2, sn], W=[t2])
            kb.op("pool", lambda e: e.tensor_tensor(out=t1[:], in0=t1[:], in1=t2[:], op=ALU.add), R=[t1, t2], W=[t1])
            s = stg.next()
            kb.op("dve", lambda e: e.tensor_tensor(out=s[0:64, :], in0=t1[:], in1=rq[0:64, :], op=ALU.mult), R=[t1, rq], W=[s])
            kb.dma("sp", qT[h, 128:192, ts], s[0:64, :], R=[s], W=[qT])
            acc = P[pi % 4]
            pi += 1
            for c in range(4):
                kb.mm(acc, acc[:], wk_s, wk_s[:, c, h * 128:(h + 1) * 128], ckvn, ckvn[:, c, :], c == 0, c == 3)
            s = stg.next()
            kb.op("act", lambda e: e.copy(out=s[:], in_=acc[:]), R=[acc], W=[s])
            kb.dma("sp", knT[h, :, ts], s[:], R=[s], W=[knT])
        for tc in range(4):
            for hf in range(2):
                acc = P[pi % 4]
                pi += 1
                for c in range(4):
                    kb.mm(acc, acc[:], ckvn, ckvn[:, c, tc * 128:(tc + 1) * 128], wv_s, wv_s[:, c, hf * 512:(hf + 1) * 512],
                          c == 0, c == 3)
                s = stg.next()
                kb.op("act", lambda e: e.copy(out=s[:], in_=acc[:]), R=[acc], W=[s])
                t0 = tt * 512 + tc * 128
                kb.dma("sp", vmla[t0:t0 + 128, hf * 512:(hf + 1) * 512], s[:], R=[s], W=[vmla])
    kb.pop()

    kb.push()
    kro = kb.sb([64, S], BF16, "kro")
    kb.dma("sp", kro[:], kroT[:], R=[kroT], W=[kro])
    kn_r = Ring(kb, 2, [128, S], BF16, "kn")
    v_r = Ring(kb, 2, [128, 32, 128], BF16, "vC")
    qn_r = Ring(kb, 2, [128, 512], BF16, "qn")
    qr_r = Ring(kb, 2, [64, 512], BF16, "qr")
    pT_r = Ring(kb, 5, [128, 512], BF16, "pT")
    rr_r = Ring(kb, 2, [128, 512], F32, "rrC")
    ho_r = Ring(kb, 2, [128, 512], BF16, "hoC")
    si = 0
    for h in range(8):
        kn = kn_r.next()
        kb.dma("sp", kn[:], knT[h], R=[knT], W=[kn])
        v = v_r.next()
        kb.dma("sp", v[:], vmla[:, h * 128:(h + 1) * 128].rearrange("(c p) d -> p c d", p=128), R=[vmla], W=[v])
        for j in range(8):
            ts = slice(j * 512, (j + 1) * 512)
            qn = qn_r.next()
            qr = qr_r.next()
            kb.dma("sp", qn[:], qT[h, 0:128, ts], R=[qT], W=[qn])
            kb.dma("sp", qr[:], qT[h, 128:192, ts], R=[qT], W=[qr])
            O = P[4 + (j % 2)]
            L = P[6 + (j % 2)]
            nk = 4 * (j + 1)
            def emit_s(kc):
                nonlocal si
                sps = P[si % 4]
                si += 1
                ks = slice(kc * 128, (kc + 1) * 128)
                kb.mm(sps, sps[:], kn, kn[:, ks], qn, qn[:], True, False)
                kb.mm(sps, sps[:], kro, kro[:, ks], qr, qr[:], False, True)
                return sps

            def emit_rest(kc, sps):
                pT = pT_r.next()
                kb.op("act", lambda e: e.activation(out=pT[:], in_=sps[:], func=AF.Exp, scale=MLA_SCALE), R=[sps], W=[pT])
                m = kc - 4 * j
                if m >= 0:
                    kb.op("pool", lambda e: e.tensor_tensor(out=pT[:], in0=pT[:], in1=masks[:, m, :], op=ALU.mult),
                          R=[pT, masks], W=[pT])
                kb.mm(O, O[:], v, v[:, kc, :], pT, pT[:], kc == 0, kc == nk - 1)
                kb.mm(L, L[:], onesb, onesb[:], pT, pT[:], kc == 0, kc == nk - 1)
            pipelined(nk, emit_s, emit_rest, look=3)
            rr = rr_r.next()
            kb.op("dve", lambda e: e.reciprocal(out=rr[:], in_=L[:]), R=[L], W=[rr])
            ho = ho_r.next()
            kb.op("dve", lambda e: e.tensor_tensor(out=ho[:], in0=O[:], in1=rr[:], op=ALU.mult), R=[O, rr], W=[ho])
            kb.dma("sp", hTo[1024 + h * 128:1024 + (h + 1) * 128, ts], ho[:], R=[ho], W=[hTo])
    kb.pop()

    kb.push()
    ones4k = kb.sb([128, S], F32, "ones4k")
    kb.op("pool", lambda e: e.memset(ones4k[:], 1.0), W=[ones4k])
    gbias = kb.sb([128, 4], F32, "gbias")
    kb.op("dve", lambda e: e.tensor_scalar(out=gbias[:], in0=sp_[:, 0:4], scalar1=1.0 / 15.0, scalar2=None, op0=ALU.mult),
          R=[sp_], W=[gbias])
    ga = kb.sb([128, S], F32, "ga")
    gb = kb.sb([128, S], F32, "gb")
    nA = kb.sb([128, S], F32, "nA")
    e3 = kb.sb([128, S], F32, "e3")
    acol = kb.sb([128, 32], F32, "acol")
    kaS = kb.sb([128, 2, S], BF16, "kaS")
    vaS = kb.sb([128, 32, 512], BF16, "vaS")
    qa_r = Ring(kb, 2, [128, 2, 512], BF16, "qa")
    og_r = Ring(kb, 2, [128, 4, 512], BF16, "og")
    dm_r = Ring(kb, 2, [128, 512], F32, "dm")
    pT_r = Ring(kb, 3, [128, 512], BF16, "pTD")
    rr = kb.sb([128, 512], F32, "rrD")
    h0 = kb.sb([128, 4, 512], F32, "h0")
    sq_r = Ring(kb, 2, [128, 512], F32, "sqD")
    sg_r = Ring(kb, 2, [128, 512], F32, "sgD")
    ho_r = Ring(kb, 2, [128, 512], BF16, "hoD")
    si = 0
    for hd in range(2):
        kb.dma("sp", ga[:], grep[hd], R=[grep], W=[ga])
        kb.dma("sp", gb[:], grep[2 + hd], R=[grep], W=[gb])
        kb.dma("sp", kaS[:], kaT[hd * 256:(hd + 1) * 256, :].rearrange("(c p) t -> p c t", p=128), R=[kaT], W=[kaS])
        kb.dma("sp", vaS[:], va[:, hd * 512:(hd + 1) * 512].rearrange("(c p) d -> p c d", p=128), R=[va], W=[vaS])
        kb.op("act", lambda e: e.activation(out=ga[:], in_=ga[:], func=AF.Tanh, scale=1.0 / 15.0, bias=gbias[:, hd:hd + 1]),
              R=[ga, gbias], W=[ga])
        kb.op("pool", lambda e: e.tensor_scalar(out=ga[:], in0=ga[:], scalar1=15.0, scalar2=None, op0=ALU.mult), R=[ga], W=[ga])
        kb.op("act", lambda e: e.activation(out=gb[:], in_=gb[:], func=AF.Tanh, scale=1.0 / 15.0, bias=gbias[:, 2 + hd:3 + hd]),
              R=[gb, gbias], W=[gb])
        kb.op("act", lambda e: e.activation(out=gb[:], in_=gb[:], func=AF.Exp, scale=-15.0), R=[gb], W=[gb])
        kb.op("act", lambda e: e.activation(out=gb[:], in_=gb[:], func=AF.Ln, bias=1.0), R=[gb], W=[gb])
        kb.op("dve", lambda e: e.tensor_tensor_scan(out=gb[:], data0=ones4k[:], data1=gb[:], initial=0.0,
                                                    op0=ALU.mult, op1=ALU.add), R=[ones4k, gb], W=[gb])
        kb.op("dve", lambda e: e.tensor_tensor(out=ga[:], in0=ga[:], in1=gb[:], op=ALU.add), R=[ga, gb], W=[ga])
        kb.op("dve", lambda e: e.tensor_tensor_scan(out=nA[:], data0=ga[:], data1=ga[:], initial=0.0,
                                                    op0=ALU.max, op1=ALU.max), R=[ga], W=[nA])
        kb.op("pool", lambda e: e.tensor_scalar(out=nA[:], in0=nA[:], scalar1=-1.0, scalar2=None, op0=ALU.mult), R=[nA], W=[nA])
        kb.op("dve", lambda e: e.tensor_tensor(out=e3[:], in0=gb[:], in1=nA[:], op=ALU.add), R=[gb, nA], W=[e3])
        kb.op("act", lambda e: e.activation(out=e3[:], in_=e3[:], func=AF.Exp), R=[e3], W=[e3])
        for c in range(32):
            tp = P[si % 3]
            si += 1
            kb.op("pe", lambda e: e.transpose(out=tp[:, 0:128], in_=ga[:, c * 128:(c + 1) * 128], identity=ident[:]),
                  R=[ga, ident], W=[tp])
            kb.op("dve", lambda e: e.tensor_scalar(out=acol[:, c:c + 1], in0=tp[:, 0:1], scalar1=-math.log(16.0), scalar2=None,
                                                   op0=ALU.add), R=[tp], W=[acol])
        for j in range(8):
            ts = slice(j * 512, (j + 1) * 512)
            qa = qa_r.next()
            kb.dma("sp", qa[:], qaT[hd * 256:(hd + 1) * 256, ts].rearrange("(c p) t -> p c t", p=128), R=[qaT], W=[qa])
            og = og_r.next()
            kb.dma("sp", og[:], ogT[hd * 512:(hd + 1) * 512, ts].rearrange("(c p) t -> p c t", p=128), R=[ogT], W=[og])
            O = [P[3], P[4], P[5], P[6]]
            L = P[7]
            nk = 4 * (j + 1)
            def emit_s(kc):
                nonlocal si
                sps = P[si % 3]
                si += 1
                ks = slice(kc * 128, (kc + 1) * 128)
                kb.mm(sps, sps[:], kaS, kaS[:, 0, ks], qa, qa[:, 0, :], True, False)
                kb.mm(sps, sps[:], kaS, kaS[:, 1, ks], qa, qa[:, 1, :], False, True)
                return sps

            def emit_rest(kc, sps):
                dm = dm_r.next()
                kb.op("act", lambda e: e.activation(out=dm[:], in_=nA[:, ts], func=AF.Exp, bias=acol[:, kc:kc + 1]),
                      R=[nA, acol], W=[dm])
                m = kc - 4 * j
                if m >= 0:
                    kb.op("pool", lambda e: e.tensor_tensor(out=dm[:], in0=dm[:], in1=masks[:, m, :], op=ALU.mult),
                          R=[dm, masks], W=[dm])
                pT = pT_r.next()
                kb.op("dve", lambda e: e.tensor_tensor(out=pT[:], in0=sps[:], in1=dm[:], op=ALU.mult), R=[sps, dm], W=[pT])
                for c in range(4):
                    kb.mm(O[c], O[c][:], vaS, vaS[:, kc, c * 128:(c + 1) * 128], pT, pT[:], kc == 0, kc == nk - 1)
                kb.mm(L, L[:], onesb, onesb[:], pT, pT[:], kc == 0, kc == nk - 1)
            pipelined(nk, emit_s, emit_rest, look=2)
            kb.op("act", lambda e: e.activation(out=rr[:], in_=L[:], func=AF.Abs), R=[L], W=[rr])
            kb.op("dve", lambda e: e.tensor_tensor(out=rr[:], in0=rr[:], in1=e3[:, ts], op=ALU.max), R=[rr, e3], W=[rr])
            kb.op("dve", lambda e: e.reciprocal(out=rr[:], in_=rr[:]), R=[rr], W=[rr])
            ssp = P[si % 3]
            si += 1
            for c in range(4):
                kb.op("dve", lambda e: e.tensor_tensor(out=h0[:, c, :], in0=O[c][:], in1=rr[:], op=ALU.mult),
                      R=[O[c], rr], W=[h0])
                sq = sq_r.next()
                kb.op("act", lambda e: e.activation(out=sq[:], in_=h0[:, c, :], func=AF.Square), R=[h0], W=[sq])
                kb.mm(ssp, ssp[:], onesf, onesf[:], sq, sq[:], c == 0, c == 3)
            kb.op("dve", lambda e: e.tensor_scalar(out=rr[:], in0=ssp[:], scalar1=1.0 / 512.0, scalar2=RMS_EPS,
                                                   op0=ALU.mult, op1=ALU.add), R=[ssp], W=[rr])
            kb.op("act", lambda e: e.activation(out=rr[:], in_=rr[:], func=AF.Ln), R=[rr], W=[rr])
            kb.op("act", lambda e: e.activation(out=rr[:], in_=rr[:], func=AF.Exp, scale=-0.5), R=[rr], W=[rr])
            for c in range(4):
                sg = sg_r.next()
                kb.op("act", lambda e: e.activation(out=sg[:], in_=og[:, c, :], func=AF.Sigmoid), R=[og], W=[sg])
                kb.op("pool", lambda e: e.tensor_tensor(out=sg[:], in0=sg[:], in1=rr[:], op=ALU.mult), R=[sg, rr], W=[sg])
                ho = ho_r.next()
                gi = 4 + hd * 4 + c
                kb.op("dve", lambda e: e.scalar_tensor_tensor(out=ho[:], in0=h0[:, c, :], scalar=sp_[:, gi:gi + 1], in1=sg[:],
                                                              op0=ALU.mult, op1=ALU.mult), R=[h0, sp_, sg], W=[ho])
                r0 = hd * 512 + c * 128
                kb.dma("sp", hTo[r0:r0 + 128, ts], ho[:], R=[ho], W=[hTo])
    kb.pop()
    kb.pop()


AB_OFF = {"q_a": 0, "k_a": 1024, "v_a": 2048, "ig": 4096, "fg": 4100, "og": 4104, "cq": 6152, "ckv": 7176, "kr": 7688}


def prep_M0_consts():
    pos = np.arange(SEQ, dtype=np.float32)
    inv = (10000.0 ** (-np.arange(32, dtype=np.float32) / 32)).astype(np.float32)
    ang = pos[None, :] * inv[:, None]
    cos, sin = np.cos(ang).astype(np.float32), np.sin(ang).astype(np.float32)
    cosT = np.concatenate([cos, cos], 0)
    sinT = np.concatenate([-sin, sin], 0)
    kl = np.arange(128)[:, None]
    ql = np.arange(512)[None, :]
    dmask = np.stack([(128 * m + kl <= ql) for m in range(4)]).astype(np.float32).astype(ml_dtypes.bfloat16)
    return {"cosT": np.ascontiguousarray(cosT), "sinT": np.ascontiguousarray(sinT), "dmask": dmask,
            "ident": np.eye(128, dtype=np.float32)}


def prep_M0_weights(hh, w_in, b_ig, b_fg, mnorm, qnorm, kvnorm, w_uq, w_ukv):
    o = AB_OFF
    cols = []
    cols.append(w_in[:, o["q_a"] + hh * 512:o["q_a"] + hh * 512 + 512])
    cols.append(w_in[:, o["k_a"] + hh * 512:o["k_a"] + hh * 512 + 512])
    cols.append(w_in[:, o["og"] + hh * 1024:o["og"] + hh * 1024 + 1024])
    cols.append(w_in[:, o["cq"]:o["cq"] + 1024])
    cols.append(w_in[:, o["ckv"]:o["ckv"] + 512])
    kr = w_in[:, o["kr"]:o["kr"] + 64]
    cols.append(kr)
    cols.append(np.concatenate([kr[:, 32:64], kr[:, 0:32]], 1))
    cols.append(w_in[:, o["v_a"] + hh * 1024:o["v_a"] + hh * 1024 + 1024])
    for c in (o["ig"] + 2 * hh, o["ig"] + 2 * hh + 1, o["fg"] + 2 * hh, o["fg"] + 2 * hh + 1):
        cols.append(np.repeat(w_in[:, c:c + 1], 128, axis=1))
    wA = tile_kxn(np.concatenate(cols, 1))
    assert wA.shape[0] == M0_NW
    hs = range(hh * 8, hh * 8 + 8)
    uq = []
    for h in hs:
        blk = w_uq[:, h * 192:(h + 1) * 192]
        uq += [blk[:, 0:128], blk[:, 128:192], blk[:, 160:192], blk[:, 128:160]]
    uq = np.concatenate(uq, 1)
    wuq = np.ascontiguousarray(uq.reshape(8, 128, 2048).transpose(1, 0, 2))
    kvk = np.concatenate([w_ukv[:, h * 256:h * 256 + 128] for h in hs], 1)
    kvv = np.concatenate([w_ukv[:, h * 256 + 128:h * 256 + 256] for h in hs], 1)
    wk = np.ascontiguousarray(kvk.reshape(4, 128, 1024).transpose(1, 0, 2))
    wv = np.ascontiguousarray(kvv.reshape(4, 128, 1024).transpose(1, 0, 2))
    sp = np.zeros((128, 24), np.float32)
    sp[:, 0] = b_ig[2 * hh]
    sp[:, 1] = b_ig[2 * hh + 1]
    sp[:, 2] = b_fg[2 * hh]
    sp[:, 3] = b_fg[2 * hh + 1]
    sp[:, 4:12] = mnorm[hh * 1024:(hh + 1) * 1024].reshape(8, 128).T
    sp[:, 12:20] = qnorm.reshape(8, 128).T
    sp[:, 20:24] = kvnorm.reshape(4, 128).T
    return {"wA": wA, "wuq": wuq, "wukv_k": wk, "wukv_v": wv, "smallp": sp}


M1_NW = 29
NSA_SCALE = 128.0 ** -0.5
GELU_C = 1.5957691216057308


def emit_M1(kb, P, xT, hTo):
    S = SEQ
    kb.pfx = "m1_"
    kb.push()
    wA = kb.dram("wA", [M1_NW, 128, 32, 128], F32, "ExternalInput")
    w1k = kb.dram("w1k", [128, 32, 256], F32, "ExternalInput")
    w1v = kb.dram("w1v", [128, 32, 256], F32, "ExternalInput")
    w2k = kb.dram("w2k", [128, 2, 128], F32, "ExternalInput")
    w2v = kb.dram("w2v", [128, 2, 128], F32, "ExternalInput")
    peT = kb.dram("peT", [128, 2, 32], F32, "ExternalInput")
    gbias = kb.dram("gbias", [128, 1], F32, "ExternalInput")
    cmaskT = kb.dram("cmaskT", [256, S], BF16, "ExternalInput")
    ovl = kb.dram("ovl", [256, 64], BF16, "ExternalInput")
    KM = kb.dram("KM", [S, 64], F32, "ExternalInput")
    AM = kb.dram("AM", [S, 64], F32, "ExternalInput")
    Emat = kb.dram("Emat", [64, S], BF16, "ExternalInput")
    dmask = kb.dram("dmask", [4, 128, 512], BF16, "ExternalInput")
    wmask = kb.dram("wmask", [8, 128, 512], BF16, "ExternalInput")
    selm = kb.dram("selm", [48, 48, 128], F32, "ExternalInput")
    ident_d = kb.dram("ident", [128, 128], F32, "ExternalInput")
    qT = kb.dram("qTs", [2048, S], BF16)
    kcT = kb.dram("kcT", [256, S], BF16)
    vcT = kb.dram("vcT", [256, S], BF16)
    ksT = kb.dram("ksT", [256, S], BF16)
    kwT = kb.dram("kwT", [256, S], BF16)
    gsT = kb.dram("gsT", [128, S], F32)
    vsw = kb.dram("vsw", [S, 512], BF16)
    fm_dst = [(qT, i) for i in range(16)] + [(kcT, 0), (kcT, 1), (vcT, 0), (vcT, 1), (ksT, 0), (ksT, 1), (kwT, 0), (kwT, 1)]

    onesb = kb.sb([128, 128], BF16, "onesb")
    kb.op("dve", lambda e: e.memset(onesb[:], 1.0), W=[onesb])
    gb_s = kb.sb([128, 1], F32, "gb_s")
    kb.dma("sp", gb_s[:], gbias[:], W=[gb_s])
    masks = kb.sb([128, 4, 512], BF16, "masks")
    kb.dma("sp", masks[:], dmask[:].rearrange("m p t -> p m t"), W=[masks])
    wmasks = kb.sb([128, 8, 512], BF16, "wmasks")
    kb.dma("sp", wmasks[:], wmask[:].rearrange("m p t -> p m t"), W=[wmasks])
    ident = kb.sb([128, 128], F32, "ident")
    kb.dma("sp", ident[:], ident_d[:], W=[ident])

    kb.push()
    xTb = kb.sb([128, 32, 2048], BF16, "xTb")
    wring = Ring(kb, 6, [128, 32, 128], BF16, "wA")
    stg = Ring(kb, 3, [128, 512], BF16, "stgA")
    stgf = Ring(kb, 2, [128, 512], F32, "stgAf")
    pi = 0
    for st in range(2):
        for q4 in range(4):
            c0 = st * 2048 + q4 * 512
            kb.dma("pool", xTb[:, :, q4 * 512:(q4 + 1) * 512],
                   xT[:, c0:c0 + 512].rearrange("(k p) t -> p k t", p=128), W=[xTb])
        for wi in range(25):
            w = wring.next()
            kb.dma("pool", w[:], wA[wi], W=[w])
            for q4 in range(4):
                acc = P[pi % 4]
                pi += 1
                for kc in range(32):
                    kb.mm(acc, acc[:], w, w[:, kc, :], xTb, xTb[:, kc, q4 * 512:(q4 + 1) * 512], kc == 0, kc == 31)
                c0 = st * 2048 + q4 * 512
                if wi < 24:
                    dst, ci = fm_dst[wi]
                    s = stg.next()
                    kb.op("act", lambda e: e.copy(out=s[:], in_=acc[:]), R=[acc], W=[s])
                    kb.dma("sp", dst[ci * 128:(ci + 1) * 128, c0:c0 + 512], s[:], R=[s], W=[dst])
                else:
                    s = stgf.next()
                    kb.op("act", lambda e: e.activation(out=s[:], in_=acc[:], func=AF.Sigmoid, bias=gb_s[:, 0:1]),
                          R=[acc, gb_s], W=[s])
                    kb.dma("sp", gsT[:, c0:c0 + 512], s[:], R=[s], W=[gsT])
        ws = []
        for j in range(4):
            w = wring.next()
            kb.dma("pool", w[:], wA[25 + j], W=[w])
            ws.append(w)
        for tc in range(16):
            acc = P[pi % 4]
            pi += 1
            for j in range(4):
                for kc in range(32):
                    kb.mm(acc, acc[:, j * 128:(j + 1) * 128], xTb, xTb[:, kc, tc * 128:(tc + 1) * 128],
                          ws[j], ws[j][:, kc, :], kc == 0, kc == 31)
            s = stg.next()
            kb.op("act", lambda e: e.copy(out=s[:], in_=acc[:]), R=[acc], W=[s])
            t0 = st * 2048 + tc * 128
            kb.dma("sp", vsw[t0:t0 + 128, :], s[:], R=[s], W=[vsw])
    kb.pop()

    kcmpT = kb.sb([128, 2, 256], BF16, "kcmpT")
    vcmp = kb.sb([128, 2, 2, 128], BF16, "vcmp")
    kb.push()
    w1s = [kb.sb([128, 32, 256], BF16, "w1k"), kb.sb([128, 32, 256], BF16, "w1v")]
    w2s = [kb.sb([128, 2, 128], BF16, "w2k"), kb.sb([128, 2, 128], BF16, "w2v")]
    pes = kb.sb([128, 2, 32], BF16, "pes")
    kb.dma("pool", w1s[0][:], w1k[:], W=[w1s[0]])
    kb.dma("pool", w1s[1][:], w1v[:], W=[w1s[1]])
    kb.dma("pool", w2s[0][:], w2k[:], W=[w2s[0]])
    kb.dma("pool", w2s[1][:], w2v[:], W=[w2s[1]])
    kb.dma("pool", pes[:], peT[:], W=[pes])
    src_r = Ring(kb, 2, [128, S], BF16, "cmpsrc")
    c0s = kb.sb([128, 4], F32, "c0s")
    u_r = Ring(kb, 2, [128, 256], F32, "u")
    t_r = Ring(kb, 2, [128, 256], F32, "t")
    hid = kb.sb([128, 2, 256], BF16, "hidc")
    pi = 0
    for kv in range(2):
        for hc in range(2):
            acc = P[4]
            for l in range(32):
                kb.mm(acc, acc[:, 0:1], w1s[kv], w1s[kv][:, l, hc * 128:(hc + 1) * 128], pes, pes[:, kv, l:l + 1], l == 0, l == 31)
            kb.op("dve", lambda e: e.tensor_copy(out=c0s[:, kv * 2 + hc:kv * 2 + hc + 1], in_=acc[:, 0:1]), R=[acc], W=[c0s])
        for gl in range(2):
            src = src_r.next()
            sd = kcT if kv == 0 else vcT
            kb.dma("sp", src[:], sd[gl * 128:(gl + 1) * 128, :], R=[sd], W=[src])
            for hc in range(2):
                acc = P[pi % 4]
                pi += 1
                for l in range(32):
                    kb.mm(acc, acc[:, 0:255], w1s[kv], w1s[kv][:, l, hc * 128:(hc + 1) * 128], src, src[:, l:l + 16 * 254 + 1:16],
                          l == 0, l == 31)
                u = u_r.next()
                t = t_r.next()
                ci = kv * 2 + hc
                kb.op("dve", lambda e: e.tensor_scalar(out=u[:, 0:255], in0=acc[:, 0:255], scalar1=c0s[:, ci:ci + 1], scalar2=None,
                                                       op0=ALU.add), R=[acc, c0s], W=[u])
                kb.op("dve", lambda e: e.tensor_tensor(out=t[:, 0:255], in0=u[:, 0:255], in1=u[:, 0:255], op=ALU.mult), R=[u], W=[t])
                kb.op("dve", lambda e: e.tensor_scalar(out=t[:, 0:255], in0=t[:, 0:255], scalar1=0.044715, scalar2=1.0,
                                                       op0=ALU.mult, op1=ALU.add), R=[t], W=[t])
                kb.op("dve", lambda e: e.tensor_tensor(out=t[:, 0:255], in0=t[:, 0:255], in1=u[:, 0:255], op=ALU.mult), R=[t, u], W=[t])
                kb.op("act", lambda e: e.activation(out=t[:, 0:255], in_=t[:, 0:255], func=AF.Sigmoid, scale=GELU_C), R=[t], W=[t])
                kb.op("dve", lambda e: e.tensor_tensor(out=hid[:, hc, 0:255], in0=t[:, 0:255], in1=u[:, 0:255], op=ALU.mult),
                      R=[t, u], W=[hid])
            if kv == 0:
                acc = P[pi % 4]
                pi += 1
                for hc in range(2):
                    kb.mm(acc, acc[:, 0:255], w2s[0], w2s[0][:, hc, :], hid, hid[:, hc, 0:255], hc == 0, hc == 1)
                kb.op("act", lambda e: e.copy(out=kcmpT[:, gl, 0:255], in_=acc[:, 0:255]), R=[acc], W=[kcmpT])
            else:
                for ch in range(2):
                    nn = 128 if ch == 0 else 127
                    acc = P[pi % 4]
                    pi += 1
                    for hc in range(2):
                        kb.mm(acc, acc[0:nn, 0:128], hid, hid[:, hc, ch * 128:ch * 128 + nn], w2s[1], w2s[1][:, hc, :], hc == 0, hc == 1)
                    kb.op("act", lambda e: e.copy(out=vcmp[0:nn, gl, ch, :], in_=acc[0:nn, 0:128]), R=[acc], W=[vcmp])
    kb.pop()

    kb.push()
    zrow = kb.sb([1, 256], BF16, "zrow")
    kb.op("dve", lambda e: e.memset(zrow[:], 0.0), W=[zrow])
    selms = kb.sb([48, 48, 128], F32, "selms")
    kb.dma("sp", selms[:], selm[:], W=[selms])
    ovls = kb.sb([128, 2, 64], BF16, "ovls")
    kb.dma("sp", ovls[:], ovl[:].rearrange("(c p) s -> p c s", p=128), W=[ovls])
    Es = kb.sb([64, S], BF16, "Es")
    kb.dma("sp", Es[:], Emat[:], W=[Es])
    ksS = kb.sb([128, S], BF16, "ksS")
    kwS = kb.sb([128, S], BF16, "kwS")
    vsS = kb.sb([128, 32, 128], BF16, "vsS")
    vwS = kb.sb([128, 32, 128], BF16, "vwS")
    qS = kb.sb([128, 8, 512], BF16, "qS")
    gs = kb.sb([48, 512], F32, "gs")
    cm_r = Ring(kb, 2, [128, 2, 512], BF16, "cm")
    pT_r = Ring(kb, 6, [128, 512], BF16, "pT")
    pn_r = Ring(kb, 2, [128, 512], BF16, "pn")
    rr_r = Ring(kb, 3, [128, 512], F32, "rr")
    grep_r = Ring(kb, 3, [128, 512], F32, "grep")
    outacc = kb.sb([128, 8, 512], F32, "outacc")
    tmpo = Ring(kb, 2, [128, 512], F32, "tmpo")
    ho_r = Ring(kb, 2, [128, 512], BF16, "ho")
    imp = kb.sb([128, 64], F32, "imp")
    km_r = Ring(kb, 2, [128, 64], F32, "km")
    am_r = Ring(kb, 2, [128, 64], F32, "am")
    cmpb = kb.sb([128, 64, 64], BF16, "cmpb")
    rank = kb.sb([128, 64], F32, "rank")
    selT = kb.sb([64, 512], BF16, "selT")
    smask = kb.sb([128, 32, 512], BF16, "smask")
    si = 0
    mi = 0
    s4 = 0

    def gate_rep(br, gl, hg):
        nonlocal si
        r = br * 16 + gl * 8 + hg
        ps = P[si % 3]
        si += 1
        kb.mm(ps, ps[:], selms, selms[:, r, :], gs, gs[:], True, True)
        return ps

    for gl in range(2):
        kb.dma("sp", ksS[:], ksT[gl * 128:(gl + 1) * 128, :], R=[ksT], W=[ksS])
        kb.dma("sp", kwS[:], kwT[gl * 128:(gl + 1) * 128, :], R=[kwT], W=[kwS])
        kb.dma("sp", vsS[:], vsw[:, gl * 128:(gl + 1) * 128].rearrange("(c p) d -> p c d", p=128), R=[vsw], W=[vsS])
        kb.dma("sp", vwS[:], vsw[:, 256 + gl * 128:256 + (gl + 1) * 128].rearrange("(c p) d -> p c d", p=128), R=[vsw], W=[vwS])
        for j in range(8):
            ts = slice(j * 512, (j + 1) * 512)
            kb.dma("sp", qS[:], qT[gl * 1024:(gl + 1) * 1024, ts].rearrange("(c p) t -> p c t", p=128), R=[qT], W=[qS])
            kb.dma("sp", gs[:], gsT[0:48, ts], R=[gsT], W=[gs])
            nb = min(255, 32 * j + 31)
            chunks = [(0, min(nb, 128))] + ([(128, nb - 128)] if nb > 128 else [])
            cm = cm_r.next()
            for ci, (n0, nn) in enumerate(chunks):
                kb.dma("sp", cm[0:nn, ci, :], cmaskT[n0:n0 + nn, ts], W=[cm])
            impP = P[7]
            kb.mm(impP, impP[:, 0:256], zrow, zrow[0:1, 0:128], zrow, zrow[0:1, 0:256], True, False)
            for hg in range(8):
                O = P[3 + (hg % 2)]
                L = P[5 + (hg % 2)]
                pts = []
                for ci, (n0, nn) in enumerate(chunks):
                    sps = P[si % 3]
                    si += 1
                    kb.mm(sps, sps[0:nn, :], kcmpT, kcmpT[:, gl, n0:n0 + nn], qS, qS[:, hg, :], True, True)
                    pT = pT_r.next()
                    kb.op("act", lambda e: e.activation(out=pT[0:nn, :], in_=sps[0:nn, :], func=AF.Exp, scale=NSA_SCALE), R=[sps], W=[pT])
                    kb.op("pool", lambda e: e.tensor_tensor(out=pT[0:nn, :], in0=pT[0:nn, :], in1=cm[0:nn, ci, :], op=ALU.mult),
                          R=[pT, cm], W=[pT])
                    kb.mm(O, O[:], vcmp, vcmp[0:nn, gl, ci, :], pT, pT[0:nn, :], ci == 0, ci == len(chunks) - 1)
                    kb.mm(L, L[:], onesb, onesb[0:nn, :], pT, pT[0:nn, :], ci == 0, ci == len(chunks) - 1)
                    pts.append(pT)
                rr = rr_r.next()
                kb.op("dve", lambda e: e.tensor_scalar(out=rr[:], in0=L[:], scalar1=1e-30, scalar2=None, op0=ALU.max), R=[L], W=[rr])
                kb.op("dve", lambda e: e.reciprocal(out=rr[:], in_=rr[:]), R=[rr], W=[rr])
                for ci, (n0, nn) in enumerate(chunks):
                    pn = pn_r.next()
                    kb.op("dve", lambda e: e.tensor_tensor(out=pn[0:nn, :], in0=pts[ci][0:nn, :], in1=rr[0:nn, :], op=ALU.mult),
                          R=[pts[ci], rr], W=[pn])
                    for qi in range(4):
                        first = (hg == 0 and ci == 0)
                        last = (hg == 7 and ci == len(chunks) - 1)
                        kb.mm(impP, impP[:, qi * 64:(qi + 1) * 64], pn, pn[0:nn, qi * 128:(qi + 1) * 128], ovls, ovls[0:nn, ci, :],
                              False, last)
                gp = gate_rep(0, gl, hg)
                gr = grep_r.next()
                kb.op("dve", lambda e: e.tensor_tensor(out=gr[:], in0=gp[:], in1=rr[:], op=ALU.mult), R=[gp, rr], W=[gr])
                kb.op("dve", lambda e: e.tensor_tensor(out=outacc[:, hg, :], in0=O[:], in1=gr[:], op=ALU.mult), R=[O, gr], W=[outacc])
            for qi in range(4):
                q0 = j * 512 + qi * 128
                km = km_r.next()
                am = am_r.next()
                kb.dma("sp", km[:], KM[q0:q0 + 128, :], W=[km])
                kb.dma("sp", am[:], AM[q0:q0 + 128, :], W=[am])
                kb.op("dve", lambda e: e.tensor_tensor(out=imp[:], in0=impP[:, qi * 64:(qi + 1) * 64], in1=km[:], op=ALU.mult),
                      R=[impP, km], W=[imp])
                kb.op("dve", lambda e: e.tensor_tensor(out=imp[:], in0=imp[:], in1=am[:], op=ALU.add), R=[imp, am], W=[imp])
                kb.op("dve", lambda e: e.tensor_tensor(out=cmpb[:], in0=imp[:, :].unsqueeze(1).to_broadcast([128, 64, 64]),
                                                       in1=imp[:, :].unsqueeze(2).to_broadcast([128, 64, 64]), op=ALU.is_gt),
                      R=[imp], W=[cmpb])
                kb.op("dve", lambda e: e.tensor_reduce(out=rank[:], in_=cmpb[:], axis=AX.X, op=ALU.add), R=[cmpb], W=[rank])
                kb.op("dve", lambda e: e.tensor_scalar(out=rank[:], in0=rank[:], scalar1=15.5, scalar2=None, op0=ALU.is_lt),
                      R=[rank], W=[rank])
                tp = P[si % 3]
                si += 1
                kb.op("pe", lambda e: e.transpose(out=tp[0:64, 0:128], in_=rank[:, :], identity=ident[:]), R=[rank, ident], W=[tp])
                kb.op("act", lambda e: e.copy(out=selT[:, qi * 128:(qi + 1) * 128], in_=tp[0:64, 0:128]), R=[tp], W=[selT])
            nk = 4 * (j + 1)
            for kc in range(nk):
                ps = P[si % 3]
                si += 1
                kb.mm(ps, ps[:], Es, Es[:, kc * 128:(kc + 1) * 128], selT, selT[:], True, True)
                m = kc - 4 * j
                if m >= 0:
                    kb.op("dve", lambda e: e.tensor_tensor(out=smask[:, kc, :], in0=ps[:], in1=masks[:, m, :], op=ALU.mult),
                          R=[ps, masks], W=[smask])
                else:
                    kb.op("act", lambda e: e.copy(out=smask[:, kc, :], in_=ps[:]), R=[ps], W=[smask])
            for hg in range(8):
                O = P[4]
                L = P[5]
                def emit_s(kc):
                    nonlocal s4
                    sps = P[s4 % 4]
                    s4 += 1
                    ks = slice(kc * 128, (kc + 1) * 128)
                    kb.mm(sps, sps[:], ksS, ksS[:, ks], qS, qS[:, hg, :], True, True)
                    return sps

                def emit_rest(kc, sps):
                    nonlocal mi
                    pT = pT_r.next()
                    kb.op("act", lambda e: e.activation(out=pT[:], in_=sps[:], func=AF.Exp, scale=NSA_SCALE), R=[sps], W=[pT])
                    eng = "pool" if (mi % 2 == 0) else "dve"
                    mi += 1
                    kb.op(eng, lambda e: e.tensor_tensor(out=pT[:], in0=pT[:], in1=smask[:, kc, :], op=ALU.mult), R=[pT, smask], W=[pT])
                    kb.mm(O, O[:], vsS, vsS[:, kc, :], pT, pT[:], kc == 0, kc == nk - 1)
                    kb.mm(L, L[:], onesb, onesb[:], pT, pT[:], kc == 0, kc == nk - 1)
                pipelined(nk, emit_s, emit_rest, look=3)
                rr = rr_r.next()
                kb.op("dve", lambda e: e.reciprocal(out=rr[:], in_=L[:]), R=[L], W=[rr])
                gp = gate_rep(1, gl, hg)
                gr = grep_r.next()
                kb.op("dve", lambda e: e.tensor_tensor(out=gr[:], in0=gp[:], in1=rr[:], op=ALU.mult), R=[gp, rr], W=[gr])
                to = tmpo.next()
                kb.op("dve", lambda e: e.tensor_tensor(out=to[:], in0=O[:], in1=gr[:], op=ALU.mult), R=[O, gr], W=[to])
                kb.op("pool", lambda e: e.tensor_tensor(out=outacc[:, hg, :], in0=outacc[:, hg, :], in1=to[:], op=ALU.add),
                      R=[outacc, to], W=[outacc])
                O2 = P[6]
                L2 = P[7]
                wch = [c for c in range(8) if 4 * j - 4 + c >= 0]
                def emit_s2(wi_):
                    nonlocal s4
                    kc = 4 * j - 4 + wch[wi_]
                    sps = P[s4 % 4]
                    s4 += 1
                    ks = slice(kc * 128, (kc + 1) * 128)
                    kb.mm(sps, sps[:], kwS, kwS[:, ks], qS, qS[:, hg, :], True, True)
                    return sps

                def emit_rest2(wi_, sps):
                    nonlocal mi
                    c = wch[wi_]
                    kc = 4 * j - 4 + c
                    pT = pT_r.next()
                    kb.op("act", lambda e: e.activation(out=pT[:], in_=sps[:], func=AF.Exp, scale=NSA_SCALE), R=[sps], W=[pT])
                    eng = "pool" if (mi % 2 == 0) else "dve"
                    mi += 1
                    kb.op(eng, lambda e: e.tensor_tensor(out=pT[:], in0=pT[:], in1=wmasks[:, c, :], op=ALU.mult), R=[pT, wmasks], W=[pT])
                    kb.mm(O2, O2[:], vwS, vwS[:, kc, :], pT, pT[:], wi_ == 0, wi_ == len(wch) - 1)
                    kb.mm(L2, L2[:], onesb, onesb[:], pT, pT[:], wi_ == 0, wi_ == len(wch) - 1)
                pipelined(len(wch), emit_s2, emit_rest2, look=3)
                rr = rr_r.next()
                kb.op("dve", lambda e: e.reciprocal(out=rr[:], in_=L2[:]), R=[L2], W=[rr])
                gp = gate_rep(2, gl, hg)
                gr = grep_r.next()
                kb.op("dve", lambda e: e.tensor_tensor(out=gr[:], in0=gp[:], in1=rr[:], op=ALU.mult), R=[gp, rr], W=[gr])
                to = tmpo.next()
                kb.op("dve", lambda e: e.tensor_tensor(out=to[:], in0=O2[:], in1=gr[:], op=ALU.mult), R=[O2, gr], W=[to])
                ho = ho_r.next()
                kb.op("pool", lambda e: e.tensor_tensor(out=ho[:], in0=outacc[:, hg, :], in1=to[:], op=ALU.add),
                      R=[outacc, to], W=[ho])
                r0 = (gl * 8 + hg) * 128
                kb.dma("sp", hTo[r0:r0 + 128, ts], ho[:], R=[ho], W=[hTo])
    kb.pop()
    kb.pop()


C_OFF = {"q": 0, "kc": 4096, "vc": 4608, "ks": 5120, "vs": 5632, "kw": 6144, "vw": 6656, "gates": 7168}


def prep_M1_consts():
    S = SEQ
    n = np.arange(256)
    q = np.arange(S)
    cmaskT = ((16 * n[:, None] + 31) <= q[None, :]) & (n[:, None] < 255)
    cmp_start = np.arange(255) * 16
    sel_start = np.arange(64) * 64
    ov = np.zeros((256, 64), np.float32)
    ov[:255] = ((cmp_start[:, None] < sel_start[None, :] + 64) & (cmp_start[:, None] + 32 > sel_start[None, :]))
    cur = q // 64
    sb = np.arange(64)
    forced = (sb[None, :] == 0) | (sb[None, :] == cur[:, None]) | (sb[None, :] == cur[:, None] - 1)
    valid = sb[None, :] <= cur[:, None]
    KM = (valid & ~forced).astype(np.float32)
    AM = np.where(valid, np.where(forced, np.float32(1e9), np.float32(0.0)), np.float32(-1e30)).astype(np.float32)
    E = (np.arange(S)[None, :] // 64 == sb[:, None])
    kl = np.arange(128)[:, None]
    ql = np.arange(512)[None, :]
    dmask = np.stack([(128 * m + kl <= ql) for m in range(4)])
    wmask = np.stack([((128 * c + kl - 512 <= ql) & (128 * c + kl > ql)) for c in range(8)])
    selm = np.zeros((48, 48, 128), np.float32)
    for r in range(48):
        selm[r, r, :] = 1.0
    bf = lambda a: a.astype(np.float32).astype(ml_dtypes.bfloat16)
    return {"cmaskT": bf(cmaskT), "ovl": bf(ov), "KM": KM, "AM": AM, "Emat": bf(E), "dmask": bf(dmask), "wmask": bf(wmask),
            "selm": selm, "ident": np.eye(128, dtype=np.float32)}


def prep_M1_weights(gh, w_in, b_gate, pe_k, pe_v, w1k, w2k, w1v, w2v):
    o = C_OFF
    cols = [w_in[:, o["q"] + gh * 2048:o["q"] + (gh + 1) * 2048]]
    for nm in ("kc", "vc", "ks", "kw"):
        cols.append(w_in[:, o[nm] + gh * 256:o[nm] + (gh + 1) * 256])
    gidx = [o["gates"] + br * 32 + (2 * gh + gl) * 8 + hg for br in range(3) for gl in range(2) for hg in range(8)]
    gcols = np.zeros((D, 128), np.float32)
    gcols[:, 0:48] = w_in[:, gidx]
    cols.append(gcols)
    for nm in ("vs", "vw"):
        cols.append(w_in[:, o[nm] + gh * 256:o[nm] + (gh + 1) * 256])
    wA = tile_kxn(np.concatenate(cols, 1))
    assert wA.shape[0] == M1_NW
    gb = np.zeros((128, 1), np.float32)
    gb[0:48, 0] = b_gate[[i - o["gates"] for i in gidx]]
    lay1 = lambda w: np.ascontiguousarray(w.reshape(32, 128, 256).transpose(1, 0, 2))
    lay2 = lambda w: np.ascontiguousarray(w.reshape(2, 128, 128).transpose(1, 0, 2))
    peT = np.ascontiguousarray(np.stack([pe_k.T, pe_v.T], axis=1))
    return {"wA": wA, "w1k": lay1(w1k), "w1v": lay1(w1v), "w2k": lay2(w2k), "w2v": lay2(w2v), "peT": peT, "gbias": gb}


NFH = NFC // 2


def emit_F(kb, P, pfx, hT, xin, xout):
    S = SEQ
    kb.pfx = pfx
    wo_t = kb.dram("wo_t", [32, 128, 16, 128], F32, "ExternalInput")
    wg_t = kb.dram("wg_t", [NFH, 128, 32, 128], F32, "ExternalInput")
    wu_t = kb.dram("wu_t", [NFH, 128, 32, 128], F32, "ExternalInput")
    wd_t = kb.dram("wd_t", [32, 128, NFH, 128], F32, "ExternalInput")
    lnp = kb.dram("lnp", [128, 4, 32], F32, "ExternalInput")
    ypart = kb.dram("ypart", [D, S], F32)
    ysum = kb.dram("ysum", [D, S], F32)
    x1a = kb.dram("x1a", [D, S], F32)
    x1ab = kb.dram("x1ab", [D, S], BF16)
    yp2 = kb.dram("yp2", [8, D, 512], F32)
    ys2 = kb.dram("ys2", [8, D, 512], F32)
    ypart_dc = [Tk(ypart.h[dc * 128:(dc + 1) * 128, :], "ypart%d" % dc) for dc in range(32)]
    ysum_dc = [Tk(ysum.h[dc * 128:(dc + 1) * 128, :], "ysum%d" % dc) for dc in range(32)]
    yp2_t = [Tk(yp2.h[tt], "yp2_%d" % tt) for tt in range(8)]
    ys2_t = [Tk(ys2.h[tt], "ys2_%d" % tt) for tt in range(8)]

    kb.push()
    hTall = kb.sb([128, 16, S], BF16, "hTall")
    for q in range(4):
        kb.dma("sp", hTall[:, :, q * 1024:(q + 1) * 1024],
               hT[:, q * 1024:(q + 1) * 1024].rearrange("(k p) t -> p k t", p=128), R=[hT], W=[hTall])
    wring = Ring(kb, 3, [128, 16, 128], BF16, "wo")
    stg = Ring(kb, 3, [128, 512], F32, "stgo")
    pi = 0
    for dc in range(32):
        w = wring.next()
        kb.dma("pool", w[:], wo_t[dc], W=[w])
        for tt in range(8):
            acc = P[pi % 4]
            pi += 1
            for kc in range(16):
                kb.mm(acc, acc[:], w, w[:, kc, :], hTall, hTall[:, kc, tt * 512:(tt + 1) * 512], kc == 0, kc == 15)
            s = stg.next()
            kb.op("act", lambda e: e.copy(out=s[:], in_=acc[:]), R=[acc], W=[s])
            kb.dma("sp", ypart_dc[dc][:, tt * 512:(tt + 1) * 512], s[:], R=[s], W=[ypart_dc[dc]])
        kb.coll_allreduce(ypart_dc[dc], ypart_dc[dc][:, :], ysum_dc[dc], ysum_dc[dc][:, :])
    kb.pop()

    def ln_stage(get_y, xsrc, gi, bi, sink_factory):
        kb.push()
        ones = kb.sb([128, 128], F32, "ones")
        kb.op("dve", lambda e: e.memset(ones[:], 1.0), W=[ones])
        lneps = kb.sb([128, 1], F32, "lneps")
        kb.op("dve", lambda e: e.memset(lneps[:], LN_EPS), W=[lneps])
        lnsb = kb.sb([128, 4, 32], F32, "lnsb")
        kb.dma("sp", lnsb[:], lnp[:], W=[lnsb])
        xin_r = Ring(kb, 3, [128, 512], F32, "xin")
        y_r = Ring(kb, 3, [128, 512], F32, "yin")
        zT_r = Ring(kb, 2, [128, 32, 512], F32, "zT")
        zviews = {id(b): [Tk(b.h[:, dc, :], "zv%d" % dc) for dc in range(32)] for b in zT_r.bufs}
        xo_r = Ring(kb, 3, [128, 512], F32, "xo")
        sq_r = Ring(kb, 2, [128, 512], F32, "sq")
        mean = kb.sb([128, 512], F32, "mean")
        rstd = kb.sb([128, 512], F32, "rstd")
        nmr = kb.sb([128, 512], F32, "nmr")
        tmpa = kb.sb([128, 512], F32, "tmpa")
        S1, S2 = P[6], P[7]
        sink = sink_factory()
        for tt in range(8):
            ts = slice(tt * 512, (tt + 1) * 512)
            zT = zT_r.next()
            zv = zviews[id(zT)]
            for dc in range(32):
                xi = xin_r.next()
                kb.dma("sp", xi[:], xsrc[dc * 128:(dc + 1) * 128, ts], R=[xsrc], W=[xi])
                yi = y_r.next()
                ytk, yap = get_y(tt, dc)
                kb.dma("sp", yi[:], yap, R=[ytk], W=[yi])
                kb.op("dve", lambda e: e.scalar_tensor_tensor(out=zv[dc][:], in0=xi[:], scalar=ALPHA, in1=yi[:],
                                                              op0=ALU.mult, op1=ALU.add), R=[xi, yi], W=[zv[dc]])
                kb.mm(S1, S1[:], ones, ones[:], zv[dc], zv[dc][:], dc == 0, dc == 31)
                sq = sq_r.next()
                kb.op("act", lambda e: e.activation(out=sq[:], in_=zv[dc][:], func=AF.Square), R=[zv[dc]], W=[sq])
                kb.mm(S2, S2[:], ones, ones[:], sq, sq[:], dc == 0, dc == 31)
            kb.op("act", lambda e: e.mul(out=mean[:], in_=S1[:], mul=1.0 / D), R=[S1], W=[mean])
            kb.op("dve", lambda e: e.tensor_tensor(out=tmpa[:], in0=mean[:], in1=mean[:], op=ALU.mult), R=[mean], W=[tmpa])
            kb.op("dve", lambda e: e.scalar_tensor_tensor(out=tmpa[:], in0=S2[:], scalar=1.0 / D, in1=tmpa[:],
                                                          op0=ALU.mult, op1=ALU.subtract), R=[S2, tmpa], W=[tmpa])
            kb.op("act", lambda e: e.activation(out=rstd[:], in_=tmpa[:], func=AF.Ln, bias=lneps[:, 0:1]), R=[tmpa, lneps], W=[rstd])
            kb.op("act", lambda e: e.activation(out=rstd[:], in_=rstd[:], func=AF.Exp, scale=-0.5), R=[rstd], W=[rstd])
            kb.op("dve", lambda e: e.scalar_tensor_tensor(out=nmr[:], in0=mean[:], scalar=-1.0, in1=rstd[:],
                                                          op0=ALU.mult, op1=ALU.mult), R=[mean, rstd], W=[nmr])
            for dc in range(32):
                t1 = xin_r.next()
                kb.op("dve", lambda e: e.tensor_tensor(out=t1[:], in0=zv[dc][:], in1=rstd[:], op=ALU.mult), R=[zv[dc], rstd], W=[t1])
                kb.op("dve", lambda e: e.tensor_tensor(out=t1[:], in0=t1[:], in1=nmr[:], op=ALU.add), R=[t1, nmr], W=[t1])
                xc = xo_r.next()
                kb.op("act", lambda e: e.activation(out=xc[:], in_=t1[:], func=AF.Identity,
                                                    scale=lnsb[:, gi, dc:dc + 1], bias=lnsb[:, bi, dc:dc + 1]),
                      R=[t1, lnsb], W=[xc])
                sink(tt, dc, xc)
        kb.pop()

    x1a_dc = [Tk(x1a.h[dc * 128:(dc + 1) * 128, :], "x1a%d" % dc) for dc in range(32)]
    x1ab_dc = [Tk(x1ab.h[dc * 128:(dc + 1) * 128, :], "x1ab%d" % dc) for dc in range(32)]
    xout_dc = [Tk(xout.h[dc * 128:(dc + 1) * 128, :], "xout%d" % dc) for dc in range(32)]

    def sink1_factory():
        b_r = Ring(kb, 3, [128, 512], BF16, "xb")

        def sink(tt, dc, xc):
            ts = slice(tt * 512, (tt + 1) * 512)
            kb.dma("sp", x1a_dc[dc][:, ts], xc[:], R=[xc], W=[x1a_dc[dc]])
            xb = b_r.next()
            kb.op("pool", lambda e: e.tensor_copy(out=xb[:], in_=xc[:]), R=[xc], W=[xb])
            kb.dma("sp", x1ab_dc[dc][:, ts], xb[:], R=[xb], W=[x1ab_dc[dc]])
        return sink
    ln_stage(lambda tt, dc: (ysum_dc[dc], ysum_dc[dc][:, tt * 512:(tt + 1) * 512]), xin, 0, 1, sink1_factory)

    kb.push()
    actT = kb.sb([128, 32, 512], BF16, "actT")
    hid = kb.sb([128, NFH, 512], BF16, "hid")
    wring = Ring(kb, 6, [128, NFH, 128], BF16, "wf")
    sg_r = Ring(kb, 2, [128, 512], F32, "sg")
    stg = Ring(kb, 3, [128, 512], F32, "stgf")
    pi = 0
    for tt in range(8):
        ts = slice(tt * 512, (tt + 1) * 512)
        kb.dma("sp", actT[:], x1ab[:, ts].rearrange("(k p) t -> p k t", p=128), R=[x1ab], W=[actT])
        for f in range(NFH):
            wg = wring.next()
            kb.dma("pool", wg[:, 0:32, :], wg_t[f], W=[wg])
            wu = wring.next()
            kb.dma("pool", wu[:, 0:32, :], wu_t[f], W=[wu])
            pg = P[pi % 6]
            pu = P[(pi + 1) % 6]
            pi += 2
            for kc in range(32):
                kb.mm(pg, pg[:], wg, wg[:, kc, :], actT, actT[:, kc, :], kc == 0, kc == 31)
            for kc in range(32):
                kb.mm(pu, pu[:], wu, wu[:, kc, :], actT, actT[:, kc, :], kc == 0, kc == 31)
            sg = sg_r.next()
            kb.op("act", lambda e: e.activation(out=sg[:], in_=pg[:], func=AF.Silu), R=[pg], W=[sg])
            kb.op("dve", lambda e: e.tensor_tensor(out=hid[:, f, :], in0=sg[:], in1=pu[:], op=ALU.mult), R=[sg, pu], W=[hid])
            if f == 20 and tt > 0:
                for i in range(4):
                    kb.coll_allreduce(yp2_t[tt - 1], yp2_t[tt - 1][i * 1024:(i + 1) * 1024, :],
                                      ys2_t[tt - 1], ys2_t[tt - 1][i * 1024:(i + 1) * 1024, :])
        for dc in range(32):
            w = wring.next()
            kb.dma("pool", w[:], wd_t[dc], W=[w])
            acc = P[pi % 6]
            pi += 1
            for fc in range(NFH):
                kb.mm(acc, acc[:], w, w[:, fc, :], hid, hid[:, fc, :], fc == 0, fc == NFH - 1)
            s = stg.next()
            kb.op("act", lambda e: e.copy(out=s[:], in_=acc[:]), R=[acc], W=[s])
            kb.dma("sp", yp2_t[tt][dc * 128:(dc + 1) * 128, :], s[:], R=[s], W=[yp2_t[tt]])
    for i in range(4):
        kb.coll_allreduce(yp2_t[7], yp2_t[7][i * 1024:(i + 1) * 1024, :], ys2_t[7], ys2_t[7][i * 1024:(i + 1) * 1024, :])
    kb.pop()

    def sink2_factory():
        def sink(tt, dc, xc):
            ts = slice(tt * 512, (tt + 1) * 512)
            kb.dma("sp", xout_dc[dc][:, ts], xc[:], R=[xc], W=[xout_dc[dc]])
        return sink
    ln_stage(lambda tt, dc: (ys2_t[tt], ys2_t[tt][dc * 128:(dc + 1) * 128, :]), x1a, 2, 3, sink2_factory)


def build_fused():
    nc = bass.Bass("TRN2", target_bir_lowering=False)
    kb = KB(nc)
    kb.pfx = ""
    xT = kb.dram("xT", [D, SEQ], F32, "ExternalInput")
    xoT = kb.dram("xoT", [D, SEQ], F32, "ExternalOutput")
    h0 = kb.dram("h0T", [2048, SEQ], BF16)
    h1 = kb.dram("h1T", [2048, SEQ], BF16)
    x1T = kb.dram("x1T", [D, SEQ], F32)
    P = [kb.ps((128, 512), F32, "P%d" % i) for i in range(8)]
    emit_M0(kb, P, xT, h0)
    emit_F(kb, P, "f0_", h0, xT, x1T)
    emit_M1(kb, P, x1T, h1)
    emit_F(kb, P, "f1_", h1, x1T, xoT)
    kb.finish()
    return nc


def prep_F_weights(rows, half, w_o, w_gate, w_up, w_down, lnv):
    f0, f1 = half * NFH * 128, (half + 1) * NFH * 128
    return {"wo_t": tile_kxn(w_o[rows]), "wg_t": tile_kxn(w_gate[:, f0:f1]), "wu_t": tile_kxn(w_up[:, f0:f1]),
            "wd_t": tile_kxn(w_down[f0:f1]), "lnp": ln_pack(lnv)}


def kernel(x, ab_w_in, ab_b_igate, ab_b_fgate, ab_mlstm_norm, ab_q_norm, ab_kv_norm, ab_w_uq, ab_w_ukv,
           ab_w_o, c_w_in, c_b_gate, c_pe_k, c_pe_v, c_cmp_w1_k, c_cmp_w2_k, c_cmp_w1_v, c_cmp_w2_v, c_w_o,
           ffn_w_gate, ffn_w_up, ffn_w_down, ln_mix_g, ln_mix_b, ln_ffn_g, ln_ffn_b):
    f = lambda a: np.asarray(a, dtype=np.float32)
    x = f(x)
    B = x.shape[0]
    if "fused" not in _CACHE:
        _CACHE["fused"] = build_fused()
    nc = _CACHE["fused"]
    per_half = []
    c0 = prep_M0_consts()
    c1 = prep_M1_consts()
    for hh in range(2):
        m = {}
        w0 = prep_M0_weights(hh, f(ab_w_in)[0], f(ab_b_igate)[0], f(ab_b_fgate)[0], f(ab_mlstm_norm)[0], f(ab_q_norm)[0],
                             f(ab_kv_norm)[0], f(ab_w_uq)[0], f(ab_w_ukv)[0])
        for k, v in list(w0.items()) + list(c0.items()):
            m["m0_" + k] = v
        w1 = prep_M1_weights(hh, f(c_w_in)[0], f(c_b_gate)[0], f(c_pe_k)[0], f(c_pe_v)[0], f(c_cmp_w1_k)[0], f(c_cmp_w2_k)[0],
                             f(c_cmp_w1_v)[0], f(c_cmp_w2_v)[0])
        for k, v in list(w1.items()) + list(c1.items()):
            m["m1_" + k] = v
        rows0 = np.concatenate([np.arange(hh * 1024, (hh + 1) * 1024), np.arange(2048 + hh * 1024, 2048 + (hh + 1) * 1024)])
        lnv0 = [f(ln_mix_g)[0], f(ln_mix_b)[0], f(ln_ffn_g)[0], f(ln_ffn_b)[0]]
        for k, v in prep_F_weights(rows0, hh, f(ab_w_o)[0], f(ffn_w_gate)[0], f(ffn_w_up)[0], f(ffn_w_down)[0], lnv0).items():
            m["f0_" + k] = v
        rows1 = np.arange(hh * 2048, (hh + 1) * 2048)
        lnv1 = [f(ln_mix_g)[1], f(ln_mix_b)[1], f(ln_ffn_g)[1], f(ln_ffn_b)[1]]
        for k, v in prep_F_weights(rows1, hh, f(c_w_o)[0], f(ffn_w_gate)[1], f(ffn_w_up)[1], f(ffn_w_down)[1], lnv1).items():
            m["f1_" + k] = v
        per_half.append(m)
    xT = [np.ascontiguousarray(x[b].T) for b in range(B)]
    in_maps = []
    for c in range(8):
        m = dict(per_half[c % 2])
        m["xT"] = xT[c // 2]
        in_maps.append(m)
    res = run_bass_kernel_spmd(nc, in_maps, core_ids=list(range(8)))
    out = np.empty((B, SEQ, D), np.float32)
    for b in range(B):
        out[b] = np.asarray(res.results[2 * b]["xoT"]).T
    return out
```
